# Optimizing a Trainium2 kernel written in Bass

```python
import math
import jax, jax.numpy as jnp
from jax import lax
import numpy as np

D_MODEL = 1024
BATCH = 8
SEQ = 4096
DEPTH = 4

ATTN_WIDTH = D_MODEL // 2
SSM_WIDTH = D_MODEL - ATTN_WIDTH
HEAD_DIM = 64
N_Q_HEADS = ATTN_WIDTH // HEAD_DIM
N_KV_HEADS = 2
Q_PER_KV = N_Q_HEADS // N_KV_HEADS
KV_WIDTH = N_KV_HEADS * HEAD_DIM
WINDOW = 128
BLOCK = 128
SSM_GROUP = 16
N_SSM_GROUPS = SSM_WIDTH // SSM_GROUP
STATE = 64
D_FF = 4 * D_MODEL
IN_WIDTH = ATTN_WIDTH + 2 * KV_WIDTH + SSM_WIDTH
N_MOD = 6
EPS = 1e-6
NEG_INF = -1e30
DT_MIN = 1e-3
DT_MAX = 1e-1

kernel_name = "hymba_swa_s5_sqrelu_adaln"


def rmsnorm(x, g):
    xf = x.astype(jnp.float32)
    y = xf * lax.rsqrt(jnp.mean(xf * xf, axis=-1, keepdims=True) + EPS)
    return (y * g.astype(jnp.float32)).astype(x.dtype)


def alibi_slopes():
    s = 2.0 ** (-8.0 * np.arange(1, N_Q_HEADS + 1) / N_Q_HEADS)
    return jnp.asarray(s, dtype=jnp.float32).reshape(N_KV_HEADS, Q_PER_KV)


def sliding_window_attention(q, k, v, sinks):
    b, l = q.shape[0], q.shape[1]
    nb = l // BLOCK
    qb = q.reshape(b, nb, BLOCK, N_KV_HEADS, Q_PER_KV, HEAD_DIM)

    def band(t):
        t = t.reshape(b, l, N_KV_HEADS, HEAD_DIM)
        tp = jnp.pad(t, ((0, 0), (BLOCK, 0), (0, 0), (0, 0)))
        tb = tp.reshape(b, nb + 1, BLOCK, N_KV_HEADS, HEAD_DIM)
        return jnp.concatenate([tb[:, :-1], tb[:, 1:]], axis=2)

    kb, vb = band(k), band(v)
    scores = jnp.einsum('bnqhgd,bnshd->bnhgqs', qb, kb).astype(jnp.float32) * (HEAD_DIM ** -0.5)

    r = jnp.arange(BLOCK)[:, None]
    j = jnp.arange(2 * BLOCK)[None, :]
    diff = BLOCK + r - j
    key_pos = (jnp.arange(nb)[:, None, None] - 1) * BLOCK + j[None]
    valid = ((diff >= 0) & (diff < WINDOW))[None] & (key_pos >= 0)
    bias = -alibi_slopes()[:, :, None, None] * diff.astype(jnp.float32)
    scores = jnp.where(valid[None, :, None, None], scores + bias, NEG_INF)

    sink = jnp.broadcast_to(sinks.astype(jnp.float32).reshape(1, 1, N_KV_HEADS, Q_PER_KV, 1, 1),
                            scores.shape[:-1] + (1,))
    probs = jax.nn.softmax(jnp.concatenate([scores, sink], axis=-1), axis=-1)[..., :-1]
    out = jnp.einsum('bnhgqs,bnshd->bnqhgd', probs.astype(v.dtype), vb)
    return out.reshape(b, l, ATTN_WIDTH)


def s5_mixer(u, lam_re, lam_im, log_dt, b_re, b_im, c_re, c_im, d_skip, w_glu, b_glu):
    bsz, l = u.shape[0], u.shape[1]
    uf = u.astype(jnp.float32)
    ug = uf.reshape(bsz, l, N_SSM_GROUPS, SSM_GROUP)
    dt = jnp.exp(log_dt.astype(jnp.float32))[:, None]
    lr = lam_re.astype(jnp.float32)
    li = lam_im.astype(jnp.float32)
    mag = jnp.exp(lr * dt)
    ang = li * dt
    ab_r = mag * jnp.cos(ang)
    ab_i = mag * jnp.sin(ang)
    nr = ab_r - 1.0
    ni = ab_i
    den = lr * lr + li * li
    f_r = (nr * lr + ni * li) / den
    f_i = (ni * lr - nr * li) / den
    br = b_re.astype(jnp.float32)
    bi = b_im.astype(jnp.float32)
    bb_r = f_r[..., None] * br - f_i[..., None] * bi
    bb_i = f_r[..., None] * bi + f_i[..., None] * br
    bu_r = jnp.einsum('blgc,gpc->blgp', ug, bb_r)
    bu_i = jnp.einsum('blgc,gpc->blgp', ug, bb_i)
    a_r = jnp.broadcast_to(ab_r, bu_r.shape)
    a_i = jnp.broadcast_to(ab_i, bu_i.shape)

    def combine(e1, e2):
        a1r, a1i, b1r, b1i = e1
        a2r, a2i, b2r, b2i = e2
        return (a2r * a1r - a2i * a1i,
                a2r * a1i + a2i * a1r,
                a2r * b1r - a2i * b1i + b2r,
                a2r * b1i + a2i * b1r + b2i)

    _, _, h_r, h_i = lax.associative_scan(combine, (a_r, a_i, bu_r, bu_i), axis=1)
    y = (jnp.einsum('blgp,gcp->blgc', h_r, c_re.astype(jnp.float32))
         - jnp.einsum('blgp,gcp->blgc', h_i, c_im.astype(jnp.float32)))
    y = y.reshape(bsz, l, SSM_WIDTH) + d_skip.astype(jnp.float32) * uf
    z = jax.nn.gelu(y).astype(u.dtype)
    return z * jax.nn.sigmoid(z @ w_glu + b_glu)


def setup_inputs(seed: int = 0) -> dict:
    key = jax.random.key(seed)
    ks = jax.random.split(key, 32)
    f32 = jnp.float32

    def nrm(k, shape, scale):
        return jax.random.normal(k, shape, f32) * scale

    def gain(k, shape):
        return 1.0 + 0.05 * jax.random.normal(k, shape, f32)

    L, G, P, C = DEPTH, N_SSM_GROUPS, STATE, SSM_GROUP
    n_idx = jnp.arange(P, dtype=f32)[None, None, :]
    return {
        "x": nrm(ks[0], (BATCH, SEQ, D_MODEL), 1.0),
        "c": nrm(ks[1], (BATCH, D_MODEL), 1.0),
        "w_ada": nrm(ks[2], (L, D_MODEL, N_MOD * D_MODEL), 0.5 * D_MODEL ** -0.5),
        "b_ada": nrm(ks[3], (L, N_MOD * D_MODEL), 0.02),
        "pre_mix_g": gain(ks[4], (L, D_MODEL)),
        "w_in": nrm(ks[5], (L, D_MODEL, IN_WIDTH), D_MODEL ** -0.5),
        "attn_sinks": nrm(ks[6], (L, N_Q_HEADS), 0.5),
        "lam_re": -0.5 * jnp.exp(0.05 * jax.random.normal(ks[7], (L, G, P), f32)),
        "lam_im": math.pi * n_idx + 0.01 * jax.random.normal(ks[8], (L, G, P), f32),
        "log_dt": jax.random.uniform(ks[9], (L, G), f32, math.log(DT_MIN), math.log(DT_MAX)),
        "b_re": nrm(ks[10], (L, G, P, C), (2.0 * C) ** -0.5),
        "b_im": nrm(ks[11], (L, G, P, C), (2.0 * C) ** -0.5),
        "c_re": nrm(ks[12], (L, G, C, P), (2.0 * P) ** -0.5 * 4.0),
        "c_im": nrm(ks[13], (L, G, C, P), (2.0 * P) ** -0.5 * 4.0),
        "d_skip": nrm(ks[14], (L, SSM_WIDTH), 1.0),
        "w_glu": nrm(ks[15], (L, SSM_WIDTH, SSM_WIDTH), SSM_WIDTH ** -0.5),
        "b_glu": nrm(ks[16], (L, SSM_WIDTH), 0.02),
        "attn_out_g": gain(ks[17], (L, ATTN_WIDTH)),
        "ssm_out_g": gain(ks[18], (L, SSM_WIDTH)),
        "w_out": nrm(ks[19], (L, D_MODEL, D_MODEL), D_MODEL ** -0.5),
        "post_mix_g": gain(ks[20], (L, D_MODEL)),
        "pre_mlp_g": gain(ks[21], (L, D_MODEL)),
        "w_mlp_in": nrm(ks[22], (L, D_MODEL, D_FF), D_MODEL ** -0.5),
        "w_mlp_out": nrm(ks[23], (L, D_FF, D_MODEL), D_FF ** -0.5),
        "post_mlp_g": gain(ks[24], (L, D_MODEL)),
    }


def reference(x, c, w_ada, b_ada, pre_mix_g, w_in, attn_sinks, lam_re, lam_im, log_dt,
              b_re, b_im, c_re, c_im, d_skip, w_glu, b_glu, attn_out_g, ssm_out_g, w_out,
              post_mix_g, pre_mlp_g, w_mlp_in, w_mlp_out, post_mlp_g):
    c_act = jax.nn.silu(c)
    split_pts = [ATTN_WIDTH, ATTN_WIDTH + KV_WIDTH, ATTN_WIDTH + 2 * KV_WIDTH]
    for i in range(DEPTH):
        mod = c_act @ w_ada[i] + b_ada[i]
        sh1, sc1, g1, sh2, sc2, g2 = [m[:, None, :] for m in jnp.split(mod, N_MOD, axis=-1)]

        h = rmsnorm(x, pre_mix_g[i]) * (1.0 + sc1) + sh1
        proj = h @ w_in[i]
        q, k, v, u = jnp.split(proj, split_pts, axis=-1)
        attn = sliding_window_attention(q, k, v, attn_sinks[i])
        ssm = s5_mixer(u, lam_re[i], lam_im[i], log_dt[i], b_re[i], b_im[i], c_re[i], c_im[i],
                       d_skip[i], w_glu[i], b_glu[i])
        heads = jnp.concatenate([rmsnorm(attn, attn_out_g[i]), rmsnorm(ssm, ssm_out_g[i])], axis=-1)
        mixed = heads @ w_out[i]
        x = x + g1 * rmsnorm(mixed, post_mix_g[i])

        h = rmsnorm(x, pre_mlp_g[i]) * (1.0 + sc2) + sh2
        f = jnp.square(jax.nn.relu(h @ w_mlp_in[i])) @ w_mlp_out[i]
        x = x + g2 * rmsnorm(f, post_mlp_g[i])
    return x
```

```python
import contextlib
import numpy as np
import concourse.bass as bass
import concourse.mybir as mybir
from concourse.bass_utils import run_bass_kernel_spmd

F32 = mybir.dt.float32
BF16 = mybir.dt.bfloat16
AF = mybir.ActivationFunctionType
ALU = mybir.AluOpType
ENGS = ("pe", "act", "dve", "pool", "sp")

D = 1024
NH = 8
G = 32
P = 64
C = 16
NP = 16
DFF = 4096
INW = 1408
LC = 16
TT = 256
NV = 52


class Prog:
    def __init__(self, nc, n_epochs=1):
        self.nc = nc
        self.ops = []
        self.epoch = 0
        self.n_epochs = n_epochs
        self.marks = []

    def op(self, eng, fn, reads=(), writes=(), dma=None):
        self.ops.append(dict(eng=eng, fn=fn, reads=tuple(reads), writes=tuple(writes),
                             dma=dma, epoch=self.epoch, waits=[], inc=False))

    def dma(self, q, fn, tag, reads=(), writes=()):
        self.op(q, fn, reads, writes, dma=tag)

    def mark(self, name):
        self.marks.append((name, len(self.ops)))

    def analyze(self):
        ops = self.ops
        last_w, readers, waited, dma_count = {}, {}, {}, {}
        for i, o in enumerate(ops):
            deps = {}
            for r in o["reads"]:
                if r in last_w:
                    deps[last_w[r]] = "raw"
            for w in o["writes"]:
                if w in last_w:
                    deps.setdefault(last_w[w], "waw")
                for rd in readers.get(w, ()):
                    if rd != i:
                        deps.setdefault(rd, "war")
            best = {}
            for d, kind in deps.items():
                od = ops[d]
                if od["dma"] is not None:
                    key = ("dma", od["dma"])
                    best[key] = max(best.get(key, 0), dma_count[od["dma"]])
                else:
                    if od["eng"] == o["eng"] and o["dma"] is None:
                        if od["eng"] == "pe" or kind == "war":
                            continue
                    key = ("eng", od["eng"], od["epoch"])
                    best[key] = max(best.get(key, -1), d)
            for key, val in best.items():
                wk = (o["eng"], key)
                if waited.get(wk, -1) >= val:
                    continue
                waited[wk] = val
                if key[0] == "eng":
                    ops[val]["inc"] = True
                o["waits"].append((key, val))
            if o["dma"] is not None:
                dma_count[o["dma"]] = dma_count.get(o["dma"], 0) + 1
            for r in o["reads"]:
                readers.setdefault(r, []).append(i)
            for w in o["writes"]:
                last_w[w] = i
                readers[w] = []
        cnt = {}
        for o in ops:
            if o["dma"] is None and o["inc"]:
                k = (o["eng"], o["epoch"])
                cnt[k] = cnt.get(k, 0) + 1
                o["cnt"] = cnt[k]
        self.max_counts = cnt
        self.dma_tags = dma_count

    def emit(self, final_dma_tags=()):
        nc = self.nc
        self.analyze()
        ops = self.ops
        with contextlib.ExitStack() as st:
            sems = {}
            for (e, ep) in self.max_counts:
                sems[("eng", e, ep)] = st.enter_context(nc.semaphore(f"s_{e}_{ep}"))
            for t in self.dma_tags:
                sems[("dma", t)] = st.enter_context(nc.semaphore(f"d_{t}"))
            block = st.enter_context(nc.Block())
            engmap = {"pe": "tensor", "act": "scalar", "dve": "vector", "pool": "gpsimd", "sp": "sync"}

            def make(engname):
                def body(eng):
                    for o in ops:
                        if o["eng"] != engname:
                            continue
                        for key, val in o["waits"]:
                            if key[0] == "dma":
                                eng.wait_ge(sems[key], 16 * val)
                            else:
                                eng.wait_ge(sems[key], ops[val]["cnt"])
                        ins = o["fn"](eng)
                        if o["dma"] is not None:
                            ins.then_inc(sems[("dma", o["dma"])], 16)
                        elif o["inc"]:
                            ins.then_inc(sems[("eng", o["eng"], o["epoch"])], 1)
                    if engname == "sp":
                        for t in final_dma_tags:
                            eng.wait_ge(sems[("dma", t)], 16 * self.dma_tags[t])
                return body

            for e in ENGS:
                getattr(block, engmap[e])(make(e))


def build(T, DEPTH):
    nc = bass.Bass("TRN2", target_bir_lowering=False)
    NT = T // TT
    NB = TT // 128
    NCH = TT // LC
    dr = lambda name, shape, dt=F32, kind="ExternalInput": nc.dram_tensor(name, shape, dt, kind=kind).ap()
    x_d = dr("x", [T, D])
    y_d = dr("y", [T, D], kind="ExternalOutput")
    xs_d = dr("xs", [128, 8, T], kind="Internal")
    ccol_d = dr("ccol", [128, 8])
    wada_d = dr("w_ada", [DEPTH, D, 6 * D])
    bada_d = dr("b_ada", [DEPTH, 1, 6 * D])
    vec_d = dr("vecs", [DEPTH, 128, NV])
    win_d = dr("w_in", [DEPTH, D, INW])
    wglu_d = dr("w_glu", [DEPTH, 512, 512])
    wout_d = dr("w_out", [DEPTH, D, D])
    wm1_d = dr("w_mlp_in", [DEPTH, D, DFF])
    wm2_d = dr("w_mlp_out", [DEPTH, DFF, D])
    lamP_d = dr("lamP", [DEPTH, 128, 3, NP])
    lamC_d = dr("lamC", [DEPTH, 128, 3, 4, P])
    bC_d = dr("bC", [DEPTH, 128, 2, 4, P])
    cP_d = dr("cP", [DEPTH, 128, 2, NP, C])
    maskb_d = dr("maskb", [128, NH, 2, 128])
    ident_d = dr("ident", [128, 128])
    mk_d = dr("masks", [128, 12])

    with contextlib.ExitStack() as st:
        sb = lambda name, shape, dt=F32: st.enter_context(nc.sbuf_tensor("s_" + name, shape, dt))
        p = Prog(nc, n_epochs=DEPTH + 1)
        ident = sb("ident", [128, 128])
        maskb = sb("maskb", [128, NH, 2, 128])
        mk = sb("mk", [128, 12])
        ones_b = sb("ones_b", [128, 128], BF16)
        ones_pad = sb("ones_pad", [128, 192], BF16)
        ccol = sb("ccol", [128, 8])
        cact = sb("cact", [128, 8], BF16)
        modcol = sb("modcol", [128, DEPTH, 48])
        vec = sb("vec", [128, DEPTH, NV])
        esink = sb("esink", [128, DEPTH, 4])
        one1 = sb("one1", [1, 2], BF16)
        coefs = sb("coefs", [128, 6, 8])
        dummy = sb("dummy", [128, 2])
        rstd = sb("rstd", [128, TT])
        xT = sb("xT", [128, 8, TT])
        sq = sb("sq", [128, 8, TT], BF16)
        tmpf = sb("tmpf", [128, 8, TT])
        hb = sb("hb", [128, 8, TT], BF16)
        Hc = sb("Hc", [128, DEPTH, 2, NP])
        buW = sb("buW", [128, NP, 2, 128], BF16)
        cW = sb("cW", [128, NP, 2, 128], BF16)
        pa = sb("pa", [128, 12, NP])
        lamP = sb("lamP", [128, 3, NP])
        vpad = sb("vpad", [128, NB + 1, 2, 192], BF16)
        abuf = sb("abuf", [128, 10240], BF16)
        wbuf = sb("wbuf", [128, 65536], BF16)
        pss = [st.enter_context(nc.psum_tensor(f"ps{i}", [128, 512], F32)) for i in range(8)]
        psn = [0]

        def ps():
            psn[0] = (psn[0] + 1) % 8
            return psn[0]

        def carve(buf, off, shape, dt, rows=128):
            n = 1
            for d_ in shape:
                n *= d_
            nel = n * (2 if dt == F32 else 1)
            v = buf[0:rows, off:off + nel]
            if dt == F32:
                v = v.bitcast(F32)
            if len(shape) == 1:
                return v, off + nel
            names = " ".join(f"d{i}" for i in range(len(shape)))
            kw = {f"d{i}": shape[i] for i in range(len(shape) - 1)}
            return v.rearrange(f"p ({names}) -> p {names}", **kw), off + nel

        hid, o_ = carve(abuf, 0, [32, TT], BF16)
        xtm, o_ = carve(abuf, o_, [D], F32)
        assert o_ <= 10240
        modrow_b, _ = carve(abuf, 0, [6 * D], BF16, rows=1)
        W_IN, o_ = carve(wbuf, 0, [8, INW], BF16)
        W_OUT, o_ = carve(wbuf, o_, [8, D], BF16)
        W_GLU, o_ = carve(wbuf, o_, [4, 512], BF16)
        tail0 = o_
        bu, o_ = carve(wbuf, o_, [2, NP, TT], F32)
        hbf, o_ = carve(wbuf, o_, [2, NP // 2, TT], BF16)
        t1, o_ = carve(wbuf, o_, [NP, 64], F32)
        t2, o_ = carve(wbuf, o_, [NP, 64], F32)
        yss, o_ = carve(wbuf, o_, [4, TT], F32)
        ys2, o_ = carve(wbuf, o_, [4, TT], F32)
        attn, o_ = carve(wbuf, o_, [4, TT], F32)
        qT, o_ = carve(wbuf, o_, [4, TT], BF16)
        uT, o_ = carve(wbuf, o_, [4, TT], BF16)
        heads, o_ = carve(wbuf, o_, [4, TT], BF16)
        sheads, o_ = carve(wbuf, o_, [4, TT], BF16)
        zb, o_ = carve(wbuf, o_, [4, TT], BF16)
        sc_f, o_ = carve(wbuf, o_, [512], F32)
        pexp, o_ = carve(wbuf, o_, [2, 512], BF16)
        rden, o_ = carve(wbuf, o_, [128], F32)
        kT, o_ = carve(wbuf, o_, [2, 2, 128 + TT], BF16)
        Abc, o_ = carve(wbuf, o_, [2, NP, NCH], F32)
        Apow, o_ = carve(wbuf, o_, [2, NP, LC], F32)
        cP, o_ = carve(wbuf, o_, [2, NP, C], F32)
        assert o_ <= 65536, o_
        pc, o2 = carve(wbuf, tail0, [12, 4, P], F32)
        lamC, o2 = carve(wbuf, o2, [3, 4, P], F32)
        bC, o2 = carve(wbuf, o2, [2, 4, P], F32)
        assert o2 <= tail0 + 2 * 2 * NP * TT
        W_M1, o3 = carve(wbuf, 0, [8, DFF], BF16)
        W_M2, o3 = carve(wbuf, o3, [32, D], BF16)
        W_ADA, o3 = carve(wbuf, 0, [8, 6 * D], BF16)
        modrow, o3 = carve(wbuf, o3, [6 * D], F32, rows=1)
        assert o3 <= 65536

        TAILK = ["bu0", "bu1", "hbf", "t1a", "t2a", "t1b", "t2b", "c1", "c2", "c3", "c4", "yss", "ys2", "attn", "qT", "uT",
                 "heads", "sheads", "zb", "sc_f", "pexp", "rden", "kT", "Abc", "Apow", "cP", "pc", "lamC", "bC", "modrow", "wbuf"]

        def fence(keys):
            p.op("dve", lambda e: e.memset(dummy[:, 0:1], 0.0), reads=list(keys), writes=list(keys))

        p.dma("sp", lambda e: e.dma_start(out=ident[:], in_=ident_d), "c0", writes=["ident"])
        p.dma("sp", lambda e: e.dma_start(out=maskb[:], in_=maskb_d), "c1", writes=["maskb"])
        p.dma("sp", lambda e: e.dma_start(out=mk[:], in_=mk_d), "c2", writes=["mk"])
        p.dma("sp", lambda e: e.dma_start(out=ccol[:], in_=ccol_d), "c3", writes=["ccol"])
        p.dma("sp", lambda e: e.dma_start(out=vec[:], in_=vec_d.rearrange("l p n -> p l n")), "c4", writes=["vec"])
        p.op("dve", lambda e: e.memset(ones_b[:], 1.0), writes=["ones_b"])
        p.op("dve", lambda e: e.memset(ones_pad[:], 0.0), writes=["ones_pad"])
        p.op("dve", lambda e: e.memset(ones_pad[:, 64:128], 1.0), reads=["ones_pad"], writes=["ones_pad"])
        p.op("dve", lambda e: e.memset(one1[:], 1.0), writes=["one1"])
        p.op("dve", lambda e: e.memset(vpad[:], 0.0), writes=["vpad"])
        p.op("dve", lambda e: e.memset(Hc[:], 0.0), writes=["Hc"])
        p.op("dve", lambda e: e.memset(cW[:], 0.0), writes=["cW"])
        p.op("act", lambda e: e.activation(out=coefs[:, 0, :], in_=ccol[:], func=AF.Sigmoid), reads=["ccol"], writes=["coefs"])
        p.op("dve", lambda e: e.tensor_tensor(out=cact[:], in0=coefs[:, 0, :], in1=ccol[:], op=ALU.mult), reads=["coefs", "ccol"], writes=["cact"])
        for l in range(DEPTH):
            p.op("act", lambda e, l=l: e.activation(out=esink[:, l, :], in_=vec[:, l, 36:40], func=AF.Exp), reads=["vec"], writes=["esink"])
        for l in range(DEPTH):
            p.dma("pool", lambda e, l=l: e.dma_start(out=W_ADA, in_=wada_d[l].rearrange("(k p) n -> p k n", p=128)), "wld", reads=["wbuf"], writes=["wbuf"])
            p.dma("sp", lambda e, l=l: e.dma_start(out=modrow, in_=bada_d[l]), "brow", reads=["modrow"], writes=["modrow"])
            for cc in range(12):
                b = ps()
                for k in range(8):
                    p.op("pe", lambda e, b=b, k=k, cc=cc: e.matmul(pss[b][0:1, :], lhsT=cact[:, k:k + 1], rhs=W_ADA[:, k, cc * 512:(cc + 1) * 512], start=(k == 0), stop=(k == 7)),
                         reads=["cact", "wbuf"], writes=[f"ps{b}"])
                p.op("dve", lambda e, b=b, cc=cc: e.tensor_tensor(out=modrow[:, cc * 512:(cc + 1) * 512], in0=pss[b][0:1, :], in1=modrow[:, cc * 512:(cc + 1) * 512], op=ALU.add),
                     reads=[f"ps{b}", "modrow"], writes=["modrow"])
            b = ps()
            for part in range(2):
                p.op("act", lambda e: e.activation(out=modrow_b, in_=modrow, func=AF.Identity), reads=["modrow", "hid"], writes=["hid"])
                for j in range(48):
                    p.op("pe", lambda e, b=b, j=j, part=part: e.matmul(pss[b][:, j:j + 1], lhsT=modrow_b[0:1, j * 128:(j + 1) * 128], rhs=one1[0:1, 0:1], start=(part == 0 and j == 0), stop=(part == 1 and j == 47)),
                         reads=["hid", "one1"], writes=[f"ps{b}"])
                if part == 0:
                    p.op("dve", lambda e: e.tensor_tensor(out=modrow, in0=modrow, in1=modrow_b, op=ALU.subtract), reads=["modrow", "hid"], writes=["modrow"])
            p.op("dve", lambda e, b=b, l=l: e.tensor_copy(out=modcol[:, l, :], in_=pss[b][:, 0:48]), reads=[f"ps{b}"], writes=["modcol"])

        p.mark('setup_done')
        def stats_rstd(src_key, ktiles, inv_n):
            b = ps()
            n = len(ktiles)
            for i, kv in enumerate(ktiles):
                p.op("pe", lambda e, b=b, kv=kv, i=i, n=n: e.matmul(pss[b][:, 0:TT], lhsT=ones_b[:, :], rhs=kv, start=(i == 0), stop=(i == n - 1)),
                     reads=[src_key, "ones_b"], writes=[f"ps{b}"])
            p.op("act", lambda e, b=b: e.activation(out=rstd[:], in_=pss[b][:, 0:TT], func=AF.Sqrt, bias=mk[:, 3:4], scale=inv_n), reads=[f"ps{b}", "mk"], writes=["rstd"])
            p.op("dve", lambda e: e.reciprocal(out=rstd[:], in_=rstd[:]), reads=["rstd"], writes=["rstd"])

        def cmul(eng, o_r, o_i, a_r, a_i, b_r, b_i, s1, s2, keys):
            rd = list(keys)
            p.op(eng, lambda e: e.tensor_tensor(out=s1, in0=a_r, in1=b_r, op=ALU.mult), reads=rd, writes=rd)
            p.op(eng, lambda e: e.tensor_tensor(out=s2, in0=a_i, in1=b_i, op=ALU.mult), reads=rd, writes=rd)
            p.op(eng, lambda e: e.tensor_tensor(out=o_r, in0=s1, in1=s2, op=ALU.subtract), reads=rd, writes=rd)
            p.op(eng, lambda e: e.tensor_tensor(out=s1, in0=a_r, in1=b_i, op=ALU.mult), reads=rd, writes=rd)
            p.op(eng, lambda e: e.tensor_tensor(out=s2, in0=a_i, in1=b_r, op=ALU.mult), reads=rd, writes=rd)
            p.op(eng, lambda e: e.tensor_tensor(out=o_i, in0=s1, in1=s2, op=ALU.add), reads=rd, writes=rd)

        def ssm_prep(S, lr, li, ldt, keys):
            k = list(keys)
            dt, a, th, s_, c_, x1, x2, x3, x4 = S[0], S[1], S[2], S[3], S[4], S[5], S[6], S[7], S[8]
            p.op("act", lambda e: e.activation(out=dt, in_=ldt, func=AF.Exp), reads=k, writes=k)
            p.op("dve", lambda e: e.tensor_tensor(out=a, in0=lr, in1=dt, op=ALU.mult), reads=k, writes=k)
            p.op("act", lambda e: e.activation(out=a, in_=a, func=AF.Exp), reads=k, writes=k)
            p.op("dve", lambda e: e.tensor_tensor(out=th, in0=li, in1=dt, op=ALU.mult), reads=k, writes=k)
            p.op("act", lambda e: e.activation(out=s_, in_=th, func=AF.Sin, scale=1.0 / 16), reads=k, writes=k)
            p.op("act", lambda e: e.activation(out=c_, in_=th, func=AF.Sin, scale=1.0 / 32), reads=k, writes=k)
            p.op("dve", lambda e: e.tensor_tensor(out=c_, in0=c_, in1=c_, op=ALU.mult), reads=k, writes=k)
            p.op("dve", lambda e: e.tensor_scalar(out=c_, in0=c_, scalar1=-2.0, scalar2=1.0, op0=ALU.mult, op1=ALU.add), reads=k, writes=k)
            for _ in range(4):
                p.op("dve", lambda e: e.tensor_tensor(out=x1, in0=c_, in1=c_, op=ALU.mult), reads=k, writes=k)
                p.op("dve", lambda e: e.tensor_tensor(out=x2, in0=s_, in1=s_, op=ALU.mult), reads=k, writes=k)
                p.op("dve", lambda e: e.tensor_tensor(out=x3, in0=c_, in1=s_, op=ALU.mult), reads=k, writes=k)
                p.op("dve", lambda e: e.tensor_tensor(out=c_, in0=x1, in1=x2, op=ALU.subtract), reads=k, writes=k)
                p.op("dve", lambda e: e.tensor_scalar(out=s_, in0=x3, scalar1=2.0, scalar2=None, op0=ALU.mult), reads=k, writes=k)
            Ar, Ai = S[9], S[10]
            p.op("dve", lambda e: e.tensor_tensor(out=Ar, in0=c_, in1=a, op=ALU.mult), reads=k, writes=k)
            p.op("dve", lambda e: e.tensor_tensor(out=Ai, in0=s_, in1=a, op=ALU.mult), reads=k, writes=k)
            p.op("dve", lambda e: e.tensor_scalar(out=x1, in0=Ar, scalar1=-1.0, scalar2=None, op0=ALU.add), reads=k, writes=k)
            p.op("dve", lambda e: e.tensor_tensor(out=x2, in0=lr, in1=lr, op=ALU.mult), reads=k, writes=k)
            p.op("dve", lambda e: e.tensor_tensor(out=x3, in0=li, in1=li, op=ALU.mult), reads=k, writes=k)
            p.op("dve", lambda e: e.tensor_tensor(out=x2, in0=x2, in1=x3, op=ALU.add), reads=k, writes=k)
            p.op("dve", lambda e: e.reciprocal(out=x2, in_=x2), reads=k, writes=k)
            Fr, Fi = S[11], S[0]
            p.op("dve", lambda e: e.tensor_tensor(out=x3, in0=x1, in1=lr, op=ALU.mult), reads=k, writes=k)
            p.op("dve", lambda e: e.tensor_tensor(out=x4, in0=Ai, in1=li, op=ALU.mult), reads=k, writes=k)
            p.op("dve", lambda e: e.tensor_tensor(out=x3, in0=x3, in1=x4, op=ALU.add), reads=k, writes=k)
            p.op("dve", lambda e: e.tensor_tensor(out=Fr, in0=x3, in1=x2, op=ALU.mult), reads=k, writes=k)
            p.op("dve", lambda e: e.tensor_tensor(out=x3, in0=Ai, in1=lr, op=ALU.mult), reads=k, writes=k)
            p.op("dve", lambda e: e.tensor_tensor(out=x4, in0=x1, in1=li, op=ALU.mult), reads=k, writes=k)
            p.op("dve", lambda e: e.tensor_tensor(out=x3, in0=x3, in1=x4, op=ALU.subtract), reads=k, writes=k)
            p.op("dve", lambda e: e.tensor_tensor(out=Fi, in0=x3, in1=x2, op=ALU.mult), reads=k, writes=k)
            return Ar, Ai, Fr, Fi

        for l in range(DEPTH):
            p.epoch = l + 1
            first, last = (l == 0), (l == DEPTH - 1)
            for (o, gcol, sccol) in ((0, 0, 8), (3, 16, 32)):
                p.op("dve", lambda e, o=o, gcol=gcol, sccol=sccol, l=l: e.scalar_tensor_tensor(out=coefs[:, o, :], in0=modcol[:, l, sccol:sccol + 8], scalar=1.0, in1=vec[:, l, gcol:gcol + 8], op0=ALU.add, op1=ALU.mult),
                     reads=["modcol", "vec", "coefs"], writes=["coefs"])
            for (o, shcol) in ((1, 0), (4, 24)):
                p.op("dve", lambda e, o=o, shcol=shcol, l=l: e.tensor_copy(out=coefs[:, o, :], in_=modcol[:, l, shcol:shcol + 8]), reads=["modcol", "coefs"], writes=["coefs"])
            for (o, gcol, pcol) in ((2, 16, 8), (5, 40, 24)):
                p.op("dve", lambda e, o=o, gcol=gcol, pcol=pcol, l=l: e.tensor_tensor(out=coefs[:, o, :], in0=modcol[:, l, gcol:gcol + 8], in1=vec[:, l, pcol:pcol + 8], op=ALU.mult),
                     reads=["modcol", "vec", "coefs"], writes=["coefs"])

            p.mark('coefs_done')
            fence(TAILK)
            p.dma("pool", lambda e, l=l: e.dma_start(out=W_IN, in_=win_d[l].rearrange("(k p) n -> p k n", p=128)), "wld", reads=["wbuf"], writes=["wbuf"])
            p.dma("pool", lambda e, l=l: e.dma_start(out=W_OUT, in_=wout_d[l].rearrange("(k p) n -> p k n", p=128)), "wld", reads=["wbuf"], writes=["wbuf"])
            p.dma("pool", lambda e, l=l: e.dma_start(out=W_GLU, in_=wglu_d[l].rearrange("(k p) n -> p k n", p=128)), "wld", reads=["wbuf"], writes=["wbuf"])
            p.dma("sp", lambda e, l=l: e.dma_start(out=lamP[:], in_=lamP_d[l]), "s0", reads=["lamP"], writes=["lamP"])
            p.dma("sp", lambda e, l=l: e.dma_start(out=lamC, in_=lamC_d[l]), "s1", reads=["lamC"], writes=["lamC"])
            p.dma("sp", lambda e, l=l: e.dma_start(out=bC, in_=bC_d[l]), "s2", reads=["bC"], writes=["bC"])
            p.dma("sp", lambda e, l=l: e.dma_start(out=cP, in_=cP_d[l]), "s3", reads=["cP"], writes=["cP"])
            p.mark('wload_issued')
            SP_ = [pa[:, i, :] for i in range(12)]
            ArP, AiP, _, _ = ssm_prep(SP_, lamP[:, 0, :], lamP[:, 1, :], lamP[:, 2, :], ["pa", "lamP"])
            p.op("dve", lambda e: e.tensor_copy(out=Apow[:, 0, :, 0], in_=ArP), reads=["pa", "Apow"], writes=["Apow"])
            p.op("dve", lambda e: e.tensor_copy(out=Apow[:, 1, :, 0], in_=AiP), reads=["pa", "Apow"], writes=["Apow"])
            for r in range(1, LC):
                cmul("dve", Apow[:, 0, :, r], Apow[:, 1, :, r], Apow[:, 0, :, r - 1], Apow[:, 1, :, r - 1], ArP, AiP, pa[:, 5, :], pa[:, 6, :], ["pa", "Apow"])
            for ri, Av in ((0, ArP), (1, AiP)):
                p.op("dve", lambda e, ri=ri, Av=Av: e.tensor_copy(out=Abc[:, ri, :, :], in_=Av.unsqueeze(2).to_broadcast([128, NP, NCH])), reads=["pa", "Abc"], writes=["Abc"])
            p.mark('Apow_done')
            for q in range(4):
                for m2 in range(2):
                    c0 = q * 32 + m2 * 16
                    p.op("dve", lambda e, q=q, m2=m2, c0=c0: e.tensor_scalar(out=cW[:, q:NP:4, 0, c0:c0 + 16], in0=cP[:, 0, q:NP:4, :], scalar1=mk[:, m2:m2 + 1], scalar2=None, op0=ALU.mult),
                         reads=["cP", "mk", "cW"], writes=["cW"])
                    p.op("dve", lambda e, q=q, m2=m2, c0=c0: e.tensor_scalar(out=cW[:, q:NP:4, 1, c0:c0 + 16], in0=cP[:, 1, q:NP:4, :], scalar1=mk[:, m2:m2 + 1], scalar2=-1.0, op0=ALU.mult, op1=ALU.mult),
                         reads=["cP", "mk", "cW"], writes=["cW"])
            p.mark('cW_done')
            SC_ = [pc[:, i, :, :] for i in range(12)]
            _, _, FrC, FiC = ssm_prep(SC_, lamC[:, 0, :, :], lamC[:, 1, :, :], lamC[:, 2, :, :], ["pc", "lamC"])
            cmul("dve", pc[:, 1, :, :], pc[:, 2, :, :], FrC, FiC, bC[:, 0, :, :], bC[:, 1, :, :], pc[:, 5, :, :], pc[:, 6, :, :], ["pc", "bC"])
            for q in range(4):
                for m2 in range(2):
                    for ri in range(2):
                        p.op("dve", lambda e, q=q, m2=m2, ri=ri: e.tensor_scalar(out=buW[:, q:NP:4, ri, m2 * 64:(m2 + 1) * 64], in0=pc[:, 1 + ri, :, :], scalar1=mk[:, 4 + q * 2 + m2:5 + q * 2 + m2], scalar2=None, op0=ALU.mult),
                             reads=["pc", "mk", "buW"], writes=["buW"])
            fence(["pc", "lamC", "bC", "bu0", "bu1"])
            p.op("dve", lambda e: e.memset(kT, 0.0), reads=["kT"], writes=["kT"])
            p.mark('tables_done')

            def load_x(ti, from_input):
                t0 = ti * TT
                if from_input:
                    for blk in range(NB):
                        p.dma("sp", lambda e, blk=blk: e.dma_start(out=xtm, in_=x_d[t0 + blk * 128:t0 + (blk + 1) * 128, :]), "xld", reads=["xtm"], writes=["xtm"])
                        for k in range(8):
                            b = ps()
                            p.op("pe", lambda e, b=b, k=k: e.transpose(pss[b][:, 0:128], xtm[:, k * 128:(k + 1) * 128], ident[:]), reads=["xtm", "ident"], writes=[f"ps{b}"])
                            if k % 2:
                                p.op("dve", lambda e, b=b, blk=blk, k=k: e.tensor_copy(out=xT[:, k, blk * 128:(blk + 1) * 128], in_=pss[b][:, 0:128]), reads=[f"ps{b}", "xT"], writes=["xT"])
                            else:
                                p.op("act", lambda e, b=b, blk=blk, k=k: e.activation(out=xT[:, k, blk * 128:(blk + 1) * 128], in_=pss[b][:, 0:128], func=AF.Identity), reads=[f"ps{b}", "xT"], writes=["xT"])
                else:
                    p.dma("sp", lambda e: e.dma_start(out=xT[:], in_=xs_d[:, :, t0:t0 + TT]), "xld", reads=["xT", "xs"], writes=["xT"])

            def store_x(ti, to_output):
                t0 = ti * TT
                if to_output:
                    for blk in range(NB):
                        for k in range(8):
                            b = ps()
                            p.op("pe", lambda e, b=b, blk=blk, k=k: e.transpose(pss[b][:, 0:128], xT[:, k, blk * 128:(blk + 1) * 128], ident[:]), reads=["xT", "ident"], writes=[f"ps{b}"])
                            p.op("dve", lambda e, b=b, k=k: e.tensor_copy(out=xtm[:, k * 128:(k + 1) * 128], in_=pss[b][:, 0:128]), reads=[f"ps{b}", "xtm"], writes=["xtm"])
                        p.dma("sp", lambda e, blk=blk: e.dma_start(out=y_d[t0 + blk * 128:t0 + (blk + 1) * 128, :], in_=xtm), "yst", reads=["xtm"])
                else:
                    p.dma("sp", lambda e: e.dma_start(out=xs_d[:, :, t0:t0 + TT], in_=xT[:]), "xst", reads=["xT"], writes=["xs"])

            def norm_in(o_gs, o_sh):
                for k in range(8):
                    p.op("act", lambda e, k=k: e.activation(out=sq[:, k, :], in_=xT[:, k, :], func=AF.Square), reads=["xT", "sq"], writes=["sq"])
                stats_rstd("sq", [sq[:, k, :] for k in range(8)], 1.0 / D)
                for k in range(8):
                    p.op("dve", lambda e, k=k: e.scalar_tensor_tensor(out=tmpf[:, k, :], in0=xT[:, k, :], scalar=coefs[:, o_gs, k:k + 1], in1=rstd[:], op0=ALU.mult, op1=ALU.mult),
                         reads=["xT", "coefs", "rstd", "tmpf"], writes=["tmpf"])
                    p.op("act", lambda e, k=k: e.activation(out=hb[:, k, :], in_=tmpf[:, k, :], func=AF.Identity, bias=coefs[:, o_sh, k:k + 1]), reads=["tmpf", "coefs", "hb"], writes=["hb"])

            def resid_out(o_coef, src_mm):
                for m in range(8):
                    b = ps()
                    src_mm(m, b)
                    p.op("act", lambda e, b=b, m=m: e.activation(out=tmpf[:, m, :], in_=pss[b][:, 0:TT], func=AF.Identity), reads=[f"ps{b}", "tmpf"], writes=["tmpf"])
                    p.op("act", lambda e, b=b, m=m: e.activation(out=sq[:, m, :], in_=pss[b][:, 0:TT], func=AF.Square), reads=[f"ps{b}", "sq"], writes=["sq"])
                stats_rstd("sq", [sq[:, k, :] for k in range(8)], 1.0 / D)
                for m in range(8):
                    p.op("dve", lambda e, m=m: e.tensor_tensor(out=tmpf[:, m, :], in0=tmpf[:, m, :], in1=rstd[:], op=ALU.mult), reads=["tmpf", "rstd"], writes=["tmpf"])
                    p.op("dve", lambda e, m=m: e.scalar_tensor_tensor(out=xT[:, m, :], in0=tmpf[:, m, :], scalar=coefs[:, o_coef, m:m + 1], in1=xT[:, m, :], op0=ALU.mult, op1=ALU.add),
                         reads=["tmpf", "coefs", "xT"], writes=["xT"])

            for ti in range(NT):
                load_x(ti, first)
                p.mark('x_loaded')
                norm_in(0, 1)
                p.mark('norm_done')
                for m in range(6):
                    b = ps()
                    for k in range(8):
                        p.op("pe", lambda e, b=b, k=k, m=m: e.matmul(pss[b][:, 0:TT], lhsT=W_IN[:, k, m * 128:(m + 1) * 128], rhs=hb[:, k, :], start=(k == 0), stop=(k == 7)), reads=["wbuf", "hb"], writes=[f"ps{b}"])
                    if m < 4:
                        p.op("act", lambda e, b=b, m=m: e.activation(out=qT[:, m, :], in_=pss[b][:, 0:TT], func=AF.Identity, scale=0.125), reads=[f"ps{b}", "qT"], writes=["qT"])
                    else:
                        p.op("dve", lambda e, b=b, m=m: e.tensor_copy(out=kT[0:64, m - 4, 0, 128:128 + TT], in_=pss[b][0:64, 0:TT]), reads=[f"ps{b}", "kT"], writes=["kT"])
                        p.op("dve", lambda e, b=b, m=m: e.tensor_copy(out=kT[64:128, m - 4, 1, 128:128 + TT], in_=pss[b][64:128, 0:TT]), reads=[f"ps{b}", "kT"], writes=["kT"])
                for s in range(4):
                    b = ps()
                    for k in range(8):
                        p.op("pe", lambda e, b=b, k=k, s=s: e.matmul(pss[b][:, 0:TT], lhsT=W_IN[:, k, 896 + s * 128:896 + (s + 1) * 128], rhs=hb[:, k, :], start=(k == 0), stop=(k == 7)), reads=["wbuf", "hb"], writes=[f"ps{b}"])
                    if s % 2:
                        p.op("act", lambda e, b=b, s=s: e.activation(out=uT[:, s, :], in_=pss[b][:, 0:TT], func=AF.Identity), reads=[f"ps{b}", "uT"], writes=["uT"])
                    else:
                        p.op("dve", lambda e, b=b, s=s: e.tensor_copy(out=uT[:, s, :], in_=pss[b][:, 0:TT]), reads=[f"ps{b}", "uT"], writes=["uT"])
                for blk in range(NB):
                    b = ps()
                    for k in range(8):
                        p.op("pe", lambda e, b=b, k=k, blk=blk: e.matmul(pss[b][:, 0:128], lhsT=hb[:, k, blk * 128:(blk + 1) * 128], rhs=W_IN[:, k, 768:896], start=(k == 0), stop=(k == 7)), reads=["wbuf", "hb"], writes=[f"ps{b}"])
                    for g2 in range(2):
                        p.op("dve", lambda e, b=b, blk=blk, g2=g2: e.tensor_copy(out=vpad[:, blk + 1, g2, 64:128], in_=pss[b][:, g2 * 64:(g2 + 1) * 64]), reads=[f"ps{b}", "vpad"], writes=["vpad"])
                p.mark('inproj_done')
                for blk in range(NB):
                    gb = ti * NB + blk
                    kts = (1,) if gb == 0 else (0, 1)
                    for g2 in range(2):
                        for kt in kts:
                            b = ps()
                            for hh in range(4):
                                h = 4 * g2 + hh
                                qt, half = h // 2, h % 2
                                p.op("pe", lambda e, b=b, hh=hh, qt=qt, half=half, kt=kt, blk=blk, g2=g2: e.matmul(
                                    pss[b][:, hh * 128:(hh + 1) * 128], lhsT=kT[:, g2, half, (blk + kt) * 128:(blk + kt + 1) * 128],
                                    rhs=qT[:, qt, blk * 128:(blk + 1) * 128], start=True, stop=True), reads=["kT", "qT"], writes=[f"ps{b}"])
                            p.op("dve", lambda e, b=b, g2=g2, kt=kt: e.tensor_tensor(out=sc_f.rearrange("p (h q) -> p h q", h=4), in0=pss[b][:, :].rearrange("p (h q) -> p h q", h=4),
                                 in1=maskb[:, 4 * g2:4 * g2 + 4, kt, :], op=ALU.add), reads=[f"ps{b}", "maskb", "sc_f"], writes=["sc_f"])
                            p.op("act", lambda e, kt=kt: e.activation(out=pexp[:, kt, :], in_=sc_f, func=AF.Exp), reads=["sc_f", "pexp"], writes=["pexp"])
                        for j in range(2):
                            qt = 2 * g2 + j
                            bo, bd = ps(), ps()
                            n = len(kts) * 2
                            i = 0
                            for kt in kts:
                                for half in range(2):
                                    hh = 2 * j + half
                                    lo = 64 if half == 0 else 0
                                    p.op("pe", lambda e, bo=bo, kt=kt, hh=hh, lo=lo, i=i, n=n, blk=blk, g2=g2: e.matmul(pss[bo][:, 0:128], lhsT=vpad[:, blk + kt, g2, lo:lo + 128], rhs=pexp[:, kt, hh * 128:(hh + 1) * 128], start=(i == 0), stop=(i == n - 1)),
                                         reads=["vpad", "pexp"], writes=[f"ps{bo}"])
                                    p.op("pe", lambda e, bd=bd, kt=kt, hh=hh, lo=lo, i=i, n=n: e.matmul(pss[bd][:, 0:128], lhsT=ones_pad[:, lo:lo + 128], rhs=pexp[:, kt, hh * 128:(hh + 1) * 128], start=(i == 0), stop=(i == n - 1)),
                                         reads=["ones_pad", "pexp"], writes=[f"ps{bd}"])
                                    i += 1
                            p.op("dve", lambda e, bd=bd, qt=qt, l=l: e.tensor_scalar(out=rden, in0=pss[bd][:, 0:128], scalar1=esink[:, l, qt:qt + 1], scalar2=None, op0=ALU.add), reads=[f"ps{bd}", "esink", "rden"], writes=["rden"])
                            p.op("dve", lambda e: e.reciprocal(out=rden, in_=rden), reads=["rden"], writes=["rden"])
                            p.op("dve", lambda e, bo=bo, qt=qt, blk=blk: e.tensor_tensor(out=attn[:, qt, blk * 128:(blk + 1) * 128], in0=pss[bo][:, 0:128], in1=rden, op=ALU.mult), reads=[f"ps{bo}", "rden", "attn"], writes=["attn"])
                p.mark('attn_done')
                p.op("dve", lambda e: e.tensor_copy(out=kT[:, :, :, 0:128], in_=kT[:, :, :, TT:TT + 128]), reads=["kT"], writes=["kT"])
                p.op("dve", lambda e: e.tensor_copy(out=vpad[:, 0, :, 64:128], in_=vpad[:, NB, :, 64:128]), reads=["vpad"], writes=["vpad"])
                for k in range(4):
                    p.op("act", lambda e, k=k: e.activation(out=sq[:, k, :], in_=attn[:, k, :], func=AF.Square), reads=["attn", "sq"], writes=["sq"])
                stats_rstd("sq", [sq[:, k, :] for k in range(4)], 1.0 / 512)
                for k in range(4):
                    p.op("dve", lambda e, k=k, l=l: e.scalar_tensor_tensor(out=heads[:, k, :], in0=attn[:, k, :], scalar=vec[:, l, 32 + k:33 + k], in1=rstd[:], op0=ALU.mult, op1=ALU.mult), reads=["attn", "vec", "rstd", "heads"], writes=["heads"])
                p.mark('heads_done')
                for pr in range(NP):
                    for ri in range(2):
                        b = ps()
                        p.op("pe", lambda e, b=b, pr=pr, ri=ri: e.matmul(pss[b][:, 0:TT], lhsT=buW[:, pr, ri, :], rhs=uT[:, pr // 4, :], start=True, stop=True), reads=["buW", "uT"], writes=[f"ps{b}"])
                        if ri:
                            p.op("act", lambda e, b=b, pr=pr, ri=ri: e.activation(out=bu[:, ri, pr, :], in_=pss[b][:, 0:TT], func=AF.Identity), reads=[f"ps{b}", f"bu{ri}"], writes=[f"bu{ri}"])
                        else:
                            p.op("dve", lambda e, b=b, pr=pr, ri=ri: e.tensor_copy(out=bu[:, ri, pr, :], in_=pss[b][:, 0:TT]), reads=[f"ps{b}", f"bu{ri}"], writes=[f"bu{ri}"])
                p.mark('bu_done')
                buv = lambda ri, r: bu[:, ri, :, :].rearrange("p a (j r) -> p a j r", r=LC)[:, :, :, r]
                t1v = t1[:, :, 0:NCH]; t2v = t2[:, :, 0:NCH]; t3v = t1[:, :, NCH:2 * NCH]; t4v = t2[:, :, NCH:2 * NCH]
                for r in range(1, LC):
                    p.op("dve", lambda e, r=r: e.tensor_tensor(out=t1v, in0=Abc[:, 0], in1=buv(0, r - 1), op=ALU.mult), reads=["Abc", "bu0", "t1a"], writes=["t1a"])
                    p.op("dve", lambda e, r=r: e.tensor_tensor(out=t2v, in0=Abc[:, 1], in1=buv(1, r - 1), op=ALU.mult), reads=["Abc", "bu1", "t2a"], writes=["t2a"])
                    p.op("pool", lambda e, r=r: e.tensor_tensor(out=t3v, in0=Abc[:, 0], in1=buv(1, r - 1), op=ALU.mult), reads=["Abc", "bu1", "t1b"], writes=["t1b"])
                    p.op("pool", lambda e, r=r: e.tensor_tensor(out=t4v, in0=Abc[:, 1], in1=buv(0, r - 1), op=ALU.mult), reads=["Abc", "bu0", "t2b"], writes=["t2b"])
                    p.op("dve", lambda e, r=r: e.tensor_tensor(out=t1v, in0=t1v, in1=t2v, op=ALU.subtract), reads=["t1a", "t2a"], writes=["t1a"])
                    p.op("pool", lambda e, r=r: e.tensor_tensor(out=t3v, in0=t3v, in1=t4v, op=ALU.add), reads=["t1b", "t2b"], writes=["t1b"])
                    p.op("dve", lambda e, r=r: e.tensor_tensor(out=buv(0, r), in0=buv(0, r), in1=t1v, op=ALU.add), reads=["t1a", "bu0"], writes=["bu0"])
                    p.op("pool", lambda e, r=r: e.tensor_tensor(out=buv(1, r), in0=buv(1, r), in1=t3v, op=ALU.add), reads=["t1b", "bu1"], writes=["bu1"])
                c1 = t1[:, :, 2 * NCH:2 * NCH + LC]; c2 = t2[:, :, 2 * NCH:2 * NCH + LC]; c3 = t1[:, :, 3 * NCH:3 * NCH + LC]; c4 = t2[:, :, 3 * NCH:3 * NCH + LC]
                for j in range(NCH):
                    Hr = Hc[:, l, 0, :].unsqueeze(2).to_broadcast([128, NP, LC]); Hi = Hc[:, l, 1, :].unsqueeze(2).to_broadcast([128, NP, LC])
                    sl = lambda ri, j=j: bu[:, ri, :, j * LC:(j + 1) * LC]
                    p.op("dve", lambda e, Hr=Hr: e.tensor_tensor(out=c1, in0=Apow[:, 0], in1=Hr, op=ALU.mult), reads=["Apow", "Hc", "c1"], writes=["c1"])
                    p.op("dve", lambda e, Hi=Hi: e.tensor_tensor(out=c2, in0=Apow[:, 1], in1=Hi, op=ALU.mult), reads=["Apow", "Hc", "c2"], writes=["c2"])
                    p.op("pool", lambda e, Hi=Hi: e.tensor_tensor(out=c3, in0=Apow[:, 0], in1=Hi, op=ALU.mult), reads=["Apow", "Hc", "c3"], writes=["c3"])
                    p.op("pool", lambda e, Hr=Hr: e.tensor_tensor(out=c4, in0=Apow[:, 1], in1=Hr, op=ALU.mult), reads=["Apow", "Hc", "c4"], writes=["c4"])
                    p.op("dve", lambda e: e.tensor_tensor(out=c1, in0=c1, in1=c2, op=ALU.subtract), reads=["c1", "c2"], writes=["c1"])
                    p.op("pool", lambda e: e.tensor_tensor(out=c3, in0=c3, in1=c4, op=ALU.add), reads=["c3", "c4"], writes=["c3"])
                    p.op("dve", lambda e, sl=sl: e.tensor_tensor(out=sl(0), in0=sl(0), in1=c1, op=ALU.add), reads=["c1", "bu0"], writes=["bu0"])
                    p.op("pool", lambda e, sl=sl: e.tensor_tensor(out=sl(1), in0=sl(1), in1=c3, op=ALU.add), reads=["c3", "bu1"], writes=["bu1"])
                    p.op("dve", lambda e, j=j, l=l: e.tensor_copy(out=Hc[:, l, 0, :], in_=bu[:, 0, :, j * LC + LC - 1]), reads=["bu0", "Hc"], writes=["Hc"])
                    p.op("dve", lambda e, j=j, l=l: e.tensor_copy(out=Hc[:, l, 1, :], in_=bu[:, 1, :, j * LC + LC - 1]), reads=["bu1", "Hc"], writes=["Hc"])
                p.mark('scan_done')
                for hf in range(2):
                    for ri in range(2):
                        p.op("act", lambda e, hf=hf, ri=ri: e.activation(out=hbf[:, ri], in_=bu[:, ri, hf * 8:(hf + 1) * 8, :], func=AF.Identity), reads=[f"bu{ri}", "hbf"], writes=["hbf"])
                    for s in (2 * hf, 2 * hf + 1):
                        b = ps()
                        i = 0
                        for q in range(4):
                            pr = 4 * s + q
                            for ri in range(2):
                                p.op("pe", lambda e, b=b, pr=pr, ri=ri, i=i, hf=hf: e.matmul(pss[b][:, 0:TT], lhsT=cW[:, pr, ri, :], rhs=hbf[:, ri, pr - 8 * hf, :], start=(i == 0), stop=(i == 7)), reads=["cW", "hbf"], writes=[f"ps{b}"])
                                i += 1
                        p.op("dve", lambda e, b=b, s=s, l=l: e.scalar_tensor_tensor(out=yss[:, s, :], in0=uT[:, s, :], scalar=vec[:, l, 44 + s:45 + s], in1=pss[b][:, 0:TT], op0=ALU.mult, op1=ALU.add), reads=[f"ps{b}", "uT", "vec", "yss"], writes=["yss"])
                p.mark('y_done')
                p.op("pool", lambda e: e.tensor_tensor(out=ys2, in0=yss, in1=yss, op=ALU.mult), reads=["yss", "ys2"], writes=["ys2"])
                p.op("pool", lambda e: e.tensor_scalar(out=ys2, in0=ys2, scalar1=0.044715, scalar2=1.0, op0=ALU.mult, op1=ALU.add), reads=["ys2"], writes=["ys2"])
                p.op("pool", lambda e: e.tensor_tensor(out=ys2, in0=ys2, in1=yss, op=ALU.mult), reads=["yss", "ys2"], writes=["ys2"])
                p.op("act", lambda e: e.activation(out=ys2, in_=ys2, func=AF.Sigmoid, scale=1.5957691216057308), reads=["ys2"], writes=["ys2"])
                p.op("dve", lambda e: e.tensor_tensor(out=yss, in0=yss, in1=ys2, op=ALU.mult), reads=["yss", "ys2"], writes=["yss"])
                p.op("act", lambda e: e.activation(out=zb, in_=yss, func=AF.Identity), reads=["yss", "zb"], writes=["zb"])
                for mo in range(4):
                    b = ps()
                    for k in range(4):
                        p.op("pe", lambda e, b=b, k=k, mo=mo: e.matmul(pss[b][:, 0:TT], lhsT=W_GLU[:, k, mo * 128:(mo + 1) * 128], rhs=zb[:, k, :], start=(k == 0), stop=(k == 3)), reads=["wbuf", "zb"], writes=[f"ps{b}"])
                    p.op("act", lambda e, b=b, mo=mo, l=l: e.activation(out=ys2[:, mo, :], in_=pss[b][:, 0:TT], func=AF.Sigmoid, bias=vec[:, l, 48 + mo:49 + mo]), reads=[f"ps{b}", "vec", "ys2"], writes=["ys2"])
                p.op("dve", lambda e: e.tensor_tensor(out=ys2, in0=yss, in1=ys2, op=ALU.mult), reads=["yss", "ys2"], writes=["ys2"])
                p.op("act", lambda e: e.activation(out=zb, in_=ys2, func=AF.Square), reads=["ys2", "zb"], writes=["zb"])
                stats_rstd("zb", [zb[:, k, :] for k in range(4)], 1.0 / 512)
                for k in range(4):
                    p.op("dve", lambda e, k=k, l=l: e.scalar_tensor_tensor(out=sheads[:, k, :], in0=ys2[:, k, :], scalar=vec[:, l, 40 + k:41 + k], in1=rstd[:], op0=ALU.mult, op1=ALU.mult), reads=["ys2", "vec", "rstd", "sheads"], writes=["sheads"])

                p.mark('ssm_done')
                def mm_out(m, b):
                    for k in range(8):
                        src, key = (heads, "heads") if k < 4 else (sheads, "sheads")
                        p.op("pe", lambda e, k=k, src=src: e.matmul(pss[b][:, 0:TT], lhsT=W_OUT[:, k, m * 128:(m + 1) * 128], rhs=src[:, k % 4, :], start=(k == 0), stop=(k == 7)), reads=["wbuf", key], writes=[f"ps{b}"])
                resid_out(2, mm_out)
                store_x(ti, False)

            p.mark('mixer_done')
            fence(TAILK)
            p.dma("pool", lambda e, l=l: e.dma_start(out=W_M1, in_=wm1_d[l].rearrange("(k p) n -> p k n", p=128)), "wld", reads=["wbuf"], writes=["wbuf"])
            p.dma("pool", lambda e, l=l: e.dma_start(out=W_M2, in_=wm2_d[l].rearrange("(k p) n -> p k n", p=128)), "wld", reads=["wbuf"], writes=["wbuf"])
            for ti in range(NT):
                load_x(ti, False)
                norm_in(3, 4)
                for f in range(32):
                    b = ps()
                    for k in range(8):
                        p.op("pe", lambda e, b=b, k=k, f=f: e.matmul(pss[b][:, 0:TT], lhsT=W_M1[:, k, f * 128:(f + 1) * 128], rhs=hb[:, k, :], start=(k == 0), stop=(k == 7)), reads=["wbuf", "hb"], writes=[f"ps{b}"])
                    p.op("act", lambda e, b=b, f=f: e.activation(out=hid[:, f, :], in_=pss[b][:, 0:TT], func=AF.Relu), reads=[f"ps{b}", "hid"], writes=["hid"])
                    p.op("pool" if f % 2 else "dve", lambda e, f=f: e.tensor_tensor(out=hid[:, f, :], in0=hid[:, f, :], in1=hid[:, f, :], op=ALU.mult), reads=["hid"], writes=["hid"])

                def mm_mlp(m, b):
                    for f in range(32):
                        p.op("pe", lambda e, f=f: e.matmul(pss[b][:, 0:TT], lhsT=W_M2[:, f, m * 128:(m + 1) * 128], rhs=hid[:, f, :], start=(f == 0), stop=(f == 31)), reads=["wbuf", "hid"], writes=[f"ps{b}"])
                resid_out(5, mm_mlp)
                store_x(ti, last)
        print("n_ops", len(p.ops), p.marks, flush=True)
        import os
        if os.environ.get("KTRUNC"):
            p.ops = p.ops[:int(os.environ["KTRUNC"])]
        if os.environ.get("KDBG"):
            dbg_list = {"modcol": modcol[:], "coefs": coefs[:], "xT": xT[:], "hb": hb[:], "qT": qT, "kT": kT, "uT": uT, "vpad": vpad[:], "attn": attn,
                        "heads": heads, "bu": bu, "yss": yss, "ys2": ys2, "sheads": sheads, "Apow": Apow, "buW": buW[:], "cW": cW[:], "Hc": Hc[:], "rstd": rstd[:], "tmpf": tmpf[:], "pexp": pexp, "hid": hid}
            for nm in os.environ["KDBG"].split(","):
                ap_ = dbg_list[nm]
                shp = list(ap_.shape)
                dd = nc.dram_tensor("dbg_" + nm, shp, F32, kind="ExternalOutput").ap()
                p.dma("pool", lambda e, dd=dd, ap_=ap_: e.dma_start(out=dd, in_=ap_), "dbg_" + nm, reads=TAILK + ["modcol", "coefs", "xT", "hb", "vpad", "buW", "cW", "Hc", "rstd", "tmpf", "hid"])
        fin = ["yst"] if not os.environ.get("KTRUNC") else []
        if os.environ.get("KDBG"):
            fin += ["dbg_" + nm for nm in os.environ["KDBG"].split(",")]
        p.emit(final_dma_tags=fin)
    return nc


def _host_prep(inputs, b, T, DEPTH):
    f = lambda a: np.ascontiguousarray(a, dtype=np.float32)
    L = DEPTH
    col = lambda v, n: v.reshape(n, 128).T
    m = {}
    m["x"] = f(inputs["x"][b, :T])
    m["ccol"] = f(col(inputs["c"][b], 8))
    m["w_ada"] = f(inputs["w_ada"][:L])
    m["b_ada"] = f(inputs["b_ada"][:L].reshape(L, 1, 6 * D))
    vecs = np.zeros((L, 128, NV), np.float32)
    for l in range(L):
        vecs[l, :, 0:8] = col(inputs["pre_mix_g"][l], 8)
        vecs[l, :, 8:16] = col(inputs["post_mix_g"][l], 8)
        vecs[l, :, 16:24] = col(inputs["pre_mlp_g"][l], 8)
        vecs[l, :, 24:32] = col(inputs["post_mlp_g"][l], 8)
        vecs[l, :, 32:36] = col(inputs["attn_out_g"][l], 4)
        vecs[l, :, 36:40] = np.repeat(inputs["attn_sinks"][l].reshape(4, 2), 64, axis=1).T
        vecs[l, :, 40:44] = col(inputs["ssm_out_g"][l], 4)
        vecs[l, :, 44:48] = col(inputs["d_skip"][l], 4)
        vecs[l, :, 48:52] = col(inputs["b_glu"][l], 4)
    m["vecs"] = vecs
    w = inputs["w_in"][:L]
    m["w_in"] = f(np.concatenate([w[:, :, 0:512], w[:, :, 512:576], w[:, :, 512:576], w[:, :, 576:640], w[:, :, 576:640], w[:, :, 640:768], w[:, :, 768:1280]], axis=2))
    for k in ("w_glu", "w_out", "w_mlp_in", "w_mlp_out"):
        m[k] = f(inputs[k][:L])
    lam = np.stack([inputs["lam_re"][:L], inputs["lam_im"][:L], np.broadcast_to(inputs["log_dt"][:L][:, :, None], (L, G, P))], axis=1)
    lam5 = lam.reshape(L, 3, NP, 2, P)
    m["lamP"] = f(lam5.transpose(0, 3, 4, 1, 2).reshape(L, 128, 3, NP))
    lam6 = lam.reshape(L, 3, 4, 4, 2, P)
    lamC = np.broadcast_to(lam6[:, :, :, :, :, None, :], (L, 3, 4, 4, 2, C, P))
    m["lamC"] = f(lamC.transpose(0, 3, 4, 5, 1, 2, 6).reshape(L, 128, 3, 4, P))
    bb = np.stack([inputs["b_re"][:L], inputs["b_im"][:L]], axis=1).reshape(L, 2, 4, 4, 2, P, C)
    m["bC"] = f(bb.transpose(0, 3, 4, 6, 1, 2, 5).reshape(L, 128, 2, 4, P))
    cc = np.stack([inputs["c_re"][:L], inputs["c_im"][:L]], axis=1).reshape(L, 2, NP, 2, C, P)
    m["cP"] = f(cc.transpose(0, 3, 5, 1, 2, 4).reshape(L, 128, 2, NP, C))
    slopes = 2.0 ** (-8.0 * np.arange(1, NH + 1) / NH)
    s_ = np.arange(128)[:, None]; q_ = np.arange(128)[None, :]
    mb = np.full((128, NH, 2, 128), -30000.0, np.float32)
    for h in range(NH):
        d0 = 128 + q_ - s_
        d1 = q_ - s_
        mb[:, h, 0, :] = np.where((d0 >= 0) & (d0 < 128), -slopes[h] * d0, -30000.0)
        mb[:, h, 1, :] = np.where((d1 >= 0) & (d1 < 128), -slopes[h] * d1, -30000.0)
    m["maskb"] = mb
    m["ident"] = np.eye(128, dtype=np.float32)
    mk = np.zeros((128, 12), np.float32); mk[:64, 0] = 1; mk[64:, 1] = 1; mk[:, 3] = 1e-6
    rows = np.arange(128)
    for q in range(4):
        for mm in range(2):
            mk[:, 4 + q * 2 + mm] = ((rows // 32 == q) & ((rows % 32) // 16 == mm)).astype(np.float32)
    m["masks"] = mk
    return m


def kernel(**inputs):
    inputs = {k: np.asarray(v) for k, v in inputs.items()}
    B, T, _ = inputs["x"].shape
    DEPTH = inputs["w_in"].shape[0]
    nc = build(T, DEPTH)
    in_maps = [_host_prep(inputs, b, T, DEPTH) for b in range(B)]
    res = run_bass_kernel_spmd(nc, in_maps, core_ids=list(range(B)))
    return np.stack([r["y"] for r in res.results], axis=0).astype(np.float32)
```

```python
import contextlib
import numpy as np
import concourse.bass as bass
import concourse.mybir as mybir
from concourse.bass_utils import run_bass_kernel_spmd

F32 = mybir.dt.float32
BF16 = mybir.dt.bfloat16
AF = mybir.ActivationFunctionType
ALU = mybir.AluOpType
ENGS = ("pe", "act", "dve", "pool", "sp")
import os as _os
NOSYNC_SAME = bool(_os.environ.get("KNOSYNC"))

D = 1024
NH = 8
G = 32
P = 64
C = 16
NP = 16
DFF = 4096
INW = 1408
LC = 16
TT = 256
NV = 52


class Prog:
    def __init__(self, nc, n_epochs=1):
        self.nc = nc
        self.ops = []
        self.epoch = 0
        self.n_epochs = n_epochs
        self.marks = []

    def op(self, eng, fn, reads=(), writes=(), dma=None):
        self.ops.append(dict(eng=eng, fn=fn, reads=tuple(reads), writes=tuple(writes),
                             dma=dma, epoch=self.epoch, waits=[], inc=False))

    def dma(self, q, fn, tag, reads=(), writes=()):
        self.op(q, fn, reads, writes, dma=tag)

    def mark(self, name):
        self.marks.append((name, len(self.ops)))

    def analyze(self):
        ops = self.ops
        last_w, readers, waited, dma_count = {}, {}, {}, {}
        for i, o in enumerate(ops):
            deps = {}
            for r in o["reads"]:
                if r in last_w:
                    deps[last_w[r]] = "raw"
            for w in o["writes"]:
                if w in last_w:
                    deps.setdefault(last_w[w], "waw")
                for rd in readers.get(w, ()):
                    if rd != i:
                        deps.setdefault(rd, "war")
            best = {}
            for d, kind in deps.items():
                od = ops[d]
                if od["dma"] is not None:
                    key = ("dma", od["dma"])
                    best[key] = max(best.get(key, 0), dma_count[od["dma"]])
                else:
                    if od["eng"] == o["eng"] and o["dma"] is None:
                        if od["eng"] == "pe" or kind in ("war", "waw") or NOSYNC_SAME:
                            continue
                    key = ("eng", od["eng"], od["epoch"])
                    best[key] = max(best.get(key, -1), d)
            for key, val in best.items():
                wk = (o["eng"], key)
                if waited.get(wk, -1) >= val:
                    continue
                waited[wk] = val
                if key[0] == "eng":
                    ops[val]["inc"] = True
                o["waits"].append((key, val))
            if o["dma"] is not None:
                dma_count[o["dma"]] = dma_count.get(o["dma"], 0) + 1
            for r in o["reads"]:
                readers.setdefault(r, []).append(i)
            for w in o["writes"]:
                last_w[w] = i
                readers[w] = []
        cnt = {}
        for o in ops:
            if o["dma"] is None and o["inc"]:
                k = (o["eng"], o["epoch"])
                cnt[k] = cnt.get(k, 0) + 1
                o["cnt"] = cnt[k]
        self.max_counts = cnt
        self.dma_tags = dma_count

    def emit(self, final_dma_tags=()):
        nc = self.nc
        self.analyze()
        ops = self.ops
        with contextlib.ExitStack() as st:
            sems = {}
            for (e, ep) in self.max_counts:
                sems[("eng", e, ep)] = st.enter_context(nc.semaphore(f"s_{e}_{ep}"))
            for t in self.dma_tags:
                sems[("dma", t)] = st.enter_context(nc.semaphore(f"d_{t}"))
            block = st.enter_context(nc.Block())
            engmap = {"pe": "tensor", "act": "scalar", "dve": "vector", "pool": "gpsimd", "sp": "sync"}

            def make(engname):
                def body(eng):
                    for o in ops:
                        if o["eng"] != engname:
                            continue
                        for key, val in o["waits"]:
                            if key[0] == "dma":
                                eng.wait_ge(sems[key], 16 * val)
                            else:
                                eng.wait_ge(sems[key], ops[val]["cnt"])
                        ins = o["fn"](eng)
                        if o["dma"] is not None:
                            ins.then_inc(sems[("dma", o["dma"])], 16)
                        elif o["inc"]:
                            ins.then_inc(sems[("eng", o["eng"], o["epoch"])], 1)
                    if engname == "sp":
                        for t in final_dma_tags:
                            eng.wait_ge(sems[("dma", t)], 16 * self.dma_tags[t])
                return body

            for e in ENGS:
                getattr(block, engmap[e])(make(e))


def _interleave(s1, s2):
    out, i, j = [], 0, 0
    n1, n2 = len(s1), len(s2)
    while i < n1 or j < n2:
        if j < n2 and (i >= n1 or i * n2 > j * n1):
            out.append(s2[j]); j += 1
        else:
            out.append(s1[i]); i += 1
    return out


def build(T, DEPTH):
    nc = bass.Bass("TRN2", target_bir_lowering=False)
    NT = T // TT
    NB = TT // 128
    NCH = TT // LC
    dr = lambda name, shape, dt=F32, kind="ExternalInput": nc.dram_tensor(name, shape, dt, kind=kind).ap()
    x_d = dr("x", [T, D])
    y_d = dr("y", [T, D], kind="ExternalOutput")
    xs_d = dr("xs", [128, 8, T], kind="Internal")
    ccol_d = dr("ccol", [128, 8])
    wada_d = dr("w_ada", [DEPTH, D, 6 * D])
    bada_d = dr("b_ada", [DEPTH, 1, 6 * D])
    vec_d = dr("vecs", [DEPTH, 128, NV])
    win_d = dr("w_in", [DEPTH, D, INW])
    wglu_d = dr("w_glu", [DEPTH, 512, 512])
    wout_d = dr("w_out", [DEPTH, D, D])
    wm1_d = dr("w_mlp_in", [DEPTH, D, DFF])
    wm2_d = dr("w_mlp_out", [DEPTH, DFF, D])
    lamP_d = dr("lamP", [DEPTH, 128, 3, NP])
    lamC_d = dr("lamC", [DEPTH, 128, 3, 4, P])
    bC_d = dr("bC", [DEPTH, 128, 2, 4, P])
    cP_d = dr("cP", [DEPTH, 128, 2, NP, C])
    maskb_d = dr("maskb", [128, NH, 2, 128])
    ident_d = dr("ident", [128, 128])
    mk_d = dr("masks", [128, 12])

    with contextlib.ExitStack() as st:
        sb = lambda name, shape, dt=F32: st.enter_context(nc.sbuf_tensor("s_" + name, shape, dt))
        p = Prog(nc, n_epochs=DEPTH + 1)
        ident = sb("ident", [128, 128])
        maskb = sb("maskb", [128, NH, 2, 128])
        mk = sb("mk", [128, 12])
        ones_b = sb("ones_b", [128, 128], BF16)
        ones_pad = sb("ones_pad", [128, 192], BF16)
        ccol = sb("ccol", [128, 8])
        cact = sb("cact", [128, 8], BF16)
        modcol = sb("modcol", [128, DEPTH, 48])
        vec = sb("vec", [128, DEPTH, NV])
        esink = sb("esink", [128, DEPTH, 4])
        one1 = sb("one1", [1, 2], BF16)
        coefs = sb("coefs", [128, 6, 8])
        dummy = sb("dummy", [128, 2])
        rstd = sb("rstd", [128, TT])
        xT = sb("xT", [128, 8, TT])
        sq = sb("sq", [128, 8, TT], BF16)
        tmpf = sb("tmpf", [128, 8, TT])
        hb = sb("hb", [128, 8, TT], BF16)
        Hc = sb("Hc", [128, DEPTH, 2, NP])
        buW = sb("buW", [128, NP, 2, 128], BF16)
        cW = sb("cW", [128, NP, 2, 128], BF16)
        pa = sb("pa", [128, 12, NP])
        lamP = sb("lamP", [128, 3, NP])
        vpad = sb("vpad", [128, NB + 1, 2, 192], BF16)
        abuf = sb("abuf", [128, 10240], BF16)
        wbuf = sb("wbuf", [128, 65536], BF16)
        pss = [st.enter_context(nc.psum_tensor(f"ps{i}", [128, 512], F32)) for i in range(8)]
        psn = [0]

        def ps():
            psn[0] = (psn[0] + 1) % 8
            return psn[0]

        def carve(buf, off, shape, dt, rows=128):
            n = 1
            for d_ in shape:
                n *= d_
            nel = n * (2 if dt == F32 else 1)
            v = buf[0:rows, off:off + nel]
            if dt == F32:
                v = v.bitcast(F32)
            if len(shape) == 1:
                return v, off + nel
            names = " ".join(f"d{i}" for i in range(len(shape)))
            kw = {f"d{i}": shape[i] for i in range(len(shape) - 1)}
            return v.rearrange(f"p ({names}) -> p {names}", **kw), off + nel

        hid, o_ = carve(abuf, 0, [32, TT], BF16)
        xtm, o_ = carve(abuf, o_, [D], F32)
        assert o_ <= 10240
        modrow_b, _ = carve(abuf, 0, [6 * D], BF16, rows=1)
        W_IN, o_ = carve(wbuf, 0, [8, INW], BF16)
        W_OUT, o_ = carve(wbuf, o_, [8, D], BF16)
        W_GLU, o_ = carve(wbuf, o_, [4, 512], BF16)
        tail0 = o_
        bu, o_ = carve(wbuf, o_, [2, NP, TT], F32)
        hbf, o_ = carve(wbuf, o_, [2, NP // 2, TT], BF16)
        t1, o_ = carve(wbuf, o_, [NP, 64], F32)
        t2, o_ = carve(wbuf, o_, [NP, 64], F32)
        yss, o_ = carve(wbuf, o_, [4, TT], F32)
        ys2, o_ = carve(wbuf, o_, [4, TT], F32)
        attn, o_ = carve(wbuf, o_, [4, TT], F32)
        qT, o_ = carve(wbuf, o_, [4, TT], BF16)
        uT, o_ = carve(wbuf, o_, [4, TT], BF16)
        heads, o_ = carve(wbuf, o_, [4, TT], BF16)
        sheads, o_ = carve(wbuf, o_, [4, TT], BF16)
        zb, o_ = carve(wbuf, o_, [4, TT], BF16)
        sc_f, o_ = carve(wbuf, o_, [512], F32)
        pexp, o_ = carve(wbuf, o_, [2, 512], BF16)
        rden, o_ = carve(wbuf, o_, [128], F32)
        kT, o_ = carve(wbuf, o_, [2, 2, 128 + TT], BF16)
        Abc, o_ = carve(wbuf, o_, [2, NP, NCH], F32)
        Apow, o_ = carve(wbuf, o_, [2, NP, LC], F32)
        cP, o_ = carve(wbuf, o_, [2, NP, C], F32)
        assert o_ <= 65536, o_
        pc, o2 = carve(wbuf, tail0, [12, 4, P], F32)
        lamC, o2 = carve(wbuf, o2, [3, 4, P], F32)
        bC, o2 = carve(wbuf, o2, [2, 4, P], F32)
        assert o2 <= tail0 + 2 * 2 * NP * TT
        W_M1, o3 = carve(wbuf, 0, [8, DFF], BF16)
        W_M2, o3 = carve(wbuf, o3, [32, D], BF16)
        W_ADA, o3 = carve(wbuf, 0, [8, 6 * D], BF16)
        modrow, o3 = carve(wbuf, o3, [6 * D], F32, rows=1)
        assert o3 <= 65536

        XT = [f"xT{k}" for k in range(8)]
        HIDK = [f"hid{f}" for f in range(32)]
        FINEK = [f"qT{m}" for m in range(4)] + [f"uT{m}" for m in range(4)] + [f"attn{m}" for m in range(4)] + [f"heads{m}" for m in range(4)] + [f"sheads{m}" for m in range(4)] + ["pexp0", "pexp1"]
        _ks = int(_os.environ.get("KSPLIT", "8"))
        SCAN_SPLIT = [("dve", 0, _ks)] + ([("pool", _ks, 16)] if _ks < 16 else [])
        BUK = [f"bus{a_}" for (_, a_, _b) in SCAN_SPLIT]
        bukey = lambda pr: [f"bus{a_}" for (_, a_, b_) in SCAN_SPLIT if a_ <= pr < b_][0]
        SCRK = [f"scr{a_}{c_}" for (_, a_, _b) in SCAN_SPLIT for c_ in "abcd"] + [f"Hc{a_}" for (_, a_, _b) in SCAN_SPLIT]
        TAILK = BUK + SCRK + FINEK + ["hbf", "yss", "ys2", "attn", "qT", "uT",
                 "heads", "sheads", "zb", "sc_f", "pexp", "rden", "kT", "Abc", "Apow", "cP", "pc", "lamC", "bC", "modrow", "wbuf"]

        def fence(keys):
            p.op("dve", lambda e: e.memset(dummy[:, 0:1], 0.0), reads=list(keys), writes=list(keys))

        p.dma("sp", lambda e: e.dma_start(out=ident[:], in_=ident_d), "c0", writes=["ident"])
        p.dma("sp", lambda e: e.dma_start(out=maskb[:], in_=maskb_d), "c1", writes=["maskb"])
        p.dma("sp", lambda e: e.dma_start(out=mk[:], in_=mk_d), "c2", writes=["mk"])
        p.dma("sp", lambda e: e.dma_start(out=ccol[:], in_=ccol_d), "c3", writes=["ccol"])
        p.dma("sp", lambda e: e.dma_start(out=vec[:], in_=vec_d.rearrange("l p n -> p l n")), "c4", writes=["vec"])
        p.op("dve", lambda e: e.memset(ones_b[:], 1.0), writes=["ones_b"])
        p.op("dve", lambda e: e.memset(ones_pad[:], 0.0), writes=["ones_pad"])
        p.op("dve", lambda e: e.memset(ones_pad[:, 64:128], 1.0), reads=["ones_pad"], writes=["ones_pad"])
        p.op("dve", lambda e: e.memset(one1[:], 1.0), writes=["one1"])
        p.op("dve", lambda e: e.memset(vpad[:], 0.0), writes=[f"vp{i}" for i in range(NB + 1)])
        p.op("dve", lambda e: e.memset(Hc[:], 0.0), writes=[f"Hc{a_}" for (_, a_, _b) in SCAN_SPLIT])
        p.op("dve", lambda e: e.memset(cW[:], 0.0), writes=["cW"])
        p.op("act", lambda e: e.activation(out=coefs[:, 0, :], in_=ccol[:], func=AF.Sigmoid), reads=["ccol"], writes=["coefs"])
        p.op("dve", lambda e: e.tensor_tensor(out=cact[:], in0=coefs[:, 0, :], in1=ccol[:], op=ALU.mult), reads=["coefs", "ccol"], writes=["cact"])
        for l in range(DEPTH):
            p.op("act", lambda e, l=l: e.activation(out=esink[:, l, :], in_=vec[:, l, 36:40], func=AF.Exp), reads=["vec"], writes=["esink"])
        for l in range(DEPTH):
            p.dma("pool", lambda e, l=l: e.dma_start(out=W_ADA, in_=wada_d[l].rearrange("(k p) n -> p k n", p=128)), "wld", reads=["wbuf"], writes=["wbuf"])
            p.dma("sp", lambda e, l=l: e.dma_start(out=modrow, in_=bada_d[l]), "brow", reads=["modrow"], writes=["modrow"])
            for cc in range(12):
                b = ps()
                for k in range(8):
                    p.op("pe", lambda e, b=b, k=k, cc=cc: e.matmul(pss[b][0:1, :], lhsT=cact[:, k:k + 1], rhs=W_ADA[:, k, cc * 512:(cc + 1) * 512], start=(k == 0), stop=(k == 7)),
                         reads=["cact", "wbuf"], writes=[f"ps{b}"])
                p.op("dve", lambda e, b=b, cc=cc: e.tensor_tensor(out=modrow[:, cc * 512:(cc + 1) * 512], in0=pss[b][0:1, :], in1=modrow[:, cc * 512:(cc + 1) * 512], op=ALU.add),
                     reads=[f"ps{b}", "modrow"], writes=["modrow"])
            b = ps()
            for part in range(2):
                p.op("act", lambda e: e.activation(out=modrow_b, in_=modrow, func=AF.Identity), reads=["modrow"], writes=HIDK)
                for j in range(48):
                    p.op("pe", lambda e, b=b, j=j, part=part: e.matmul(pss[b][:, j:j + 1], lhsT=modrow_b[0:1, j * 128:(j + 1) * 128], rhs=one1[0:1, 0:1], start=(part == 0 and j == 0), stop=(part == 1 and j == 47)),
                         reads=HIDK + ["one1"], writes=[f"ps{b}"])
                if part == 0:
                    p.op("dve", lambda e: e.tensor_tensor(out=modrow, in0=modrow, in1=modrow_b, op=ALU.subtract), reads=["modrow"] + HIDK, writes=["modrow"])
            p.op("dve", lambda e, b=b, l=l: e.tensor_copy(out=modcol[:, l, :], in_=pss[b][:, 0:48]), reads=[f"ps{b}"], writes=["modcol"])

        p.mark('setup_done')
        def stats_rstd(src_keys, ktiles, inv_n):
            b = ps()
            n = len(ktiles)
            for i, kv in enumerate(ktiles):
                p.op("pe", lambda e, b=b, kv=kv, i=i, n=n: e.matmul(pss[b][:, 0:TT], lhsT=ones_b[:, :], rhs=kv, start=(i == 0), stop=(i == n - 1)),
                     reads=[src_keys[i], "ones_b"], writes=[f"ps{b}"])
            p.op("act", lambda e, b=b: e.activation(out=rstd[:], in_=pss[b][:, 0:TT], func=AF.Sqrt, bias=mk[:, 3:4], scale=inv_n), reads=[f"ps{b}", "mk"], writes=["rstd"])
            p.op("dve", lambda e: e.reciprocal(out=rstd[:], in_=rstd[:]), reads=["rstd"], writes=["rstd"])

        def cmul(eng, o_r, o_i, a_r, a_i, b_r, b_i, s1, s2, keys):
            rd = list(keys)
            p.op(eng, lambda e: e.tensor_tensor(out=s1, in0=a_r, in1=b_r, op=ALU.mult), reads=rd, writes=rd)
            p.op(eng, lambda e: e.tensor_tensor(out=s2, in0=a_i, in1=b_i, op=ALU.mult), reads=rd, writes=rd)
            p.op(eng, lambda e: e.tensor_tensor(out=o_r, in0=s1, in1=s2, op=ALU.subtract), reads=rd, writes=rd)
            p.op(eng, lambda e: e.tensor_tensor(out=s1, in0=a_r, in1=b_i, op=ALU.mult), reads=rd, writes=rd)
            p.op(eng, lambda e: e.tensor_tensor(out=s2, in0=a_i, in1=b_r, op=ALU.mult), reads=rd, writes=rd)
            p.op(eng, lambda e: e.tensor_tensor(out=o_i, in0=s1, in1=s2, op=ALU.add), reads=rd, writes=rd)

        def ssm_prep(S, lr, li, ldt, keys):
            k = list(keys)
            dt, a, th, s_, c_, x1, x2, x3, x4 = S[0], S[1], S[2], S[3], S[4], S[5], S[6], S[7], S[8]
            p.op("act", lambda e: e.activation(out=dt, in_=ldt, func=AF.Exp), reads=k, writes=k)
            p.op("dve", lambda e: e.tensor_tensor(out=a, in0=lr, in1=dt, op=ALU.mult), reads=k, writes=k)
            p.op("act", lambda e: e.activation(out=a, in_=a, func=AF.Exp), reads=k, writes=k)
            p.op("dve", lambda e: e.tensor_tensor(out=th, in0=li, in1=dt, op=ALU.mult), reads=k, writes=k)
            p.op("act", lambda e: e.activation(out=s_, in_=th, func=AF.Sin, scale=1.0 / 16), reads=k, writes=k)
            p.op("act", lambda e: e.activation(out=c_, in_=th, func=AF.Sin, scale=1.0 / 32), reads=k, writes=k)
            p.op("dve", lambda e: e.tensor_tensor(out=c_, in0=c_, in1=c_, op=ALU.mult), reads=k, writes=k)
            p.op("dve", lambda e: e.tensor_scalar(out=c_, in0=c_, scalar1=-2.0, scalar2=1.0, op0=ALU.mult, op1=ALU.add), reads=k, writes=k)
            for _ in range(4):
                p.op("dve", lambda e: e.tensor_tensor(out=x1, in0=c_, in1=c_, op=ALU.mult), reads=k, writes=k)
                p.op("dve", lambda e: e.tensor_tensor(out=x2, in0=s_, in1=s_, op=ALU.mult), reads=k, writes=k)
                p.op("dve", lambda e: e.tensor_tensor(out=x3, in0=c_, in1=s_, op=ALU.mult), reads=k, writes=k)
                p.op("dve", lambda e: e.tensor_tensor(out=c_, in0=x1, in1=x2, op=ALU.subtract), reads=k, writes=k)
                p.op("dve", lambda e: e.tensor_scalar(out=s_, in0=x3, scalar1=2.0, scalar2=None, op0=ALU.mult), reads=k, writes=k)
            Ar, Ai = S[9], S[10]
            p.op("dve", lambda e: e.tensor_tensor(out=Ar, in0=c_, in1=a, op=ALU.mult), reads=k, writes=k)
            p.op("dve", lambda e: e.tensor_tensor(out=Ai, in0=s_, in1=a, op=ALU.mult), reads=k, writes=k)
            p.op("dve", lambda e: e.tensor_scalar(out=x1, in0=Ar, scalar1=-1.0, scalar2=None, op0=ALU.add), reads=k, writes=k)
            p.op("dve", lambda e: e.tensor_tensor(out=x2, in0=lr, in1=lr, op=ALU.mult), reads=k, writes=k)
            p.op("dve", lambda e: e.tensor_tensor(out=x3, in0=li, in1=li, op=ALU.mult), reads=k, writes=k)
            p.op("dve", lambda e: e.tensor_tensor(out=x2, in0=x2, in1=x3, op=ALU.add), reads=k, writes=k)
            p.op("dve", lambda e: e.reciprocal(out=x2, in_=x2), reads=k, writes=k)
            Fr, Fi = S[11], S[0]
            p.op("dve", lambda e: e.tensor_tensor(out=x3, in0=x1, in1=lr, op=ALU.mult), reads=k, writes=k)
            p.op("dve", lambda e: e.tensor_tensor(out=x4, in0=Ai, in1=li, op=ALU.mult), reads=k, writes=k)
            p.op("dve", lambda e: e.tensor_tensor(out=x3, in0=x3, in1=x4, op=ALU.add), reads=k, writes=k)
            p.op("dve", lambda e: e.tensor_tensor(out=Fr, in0=x3, in1=x2, op=ALU.mult), reads=k, writes=k)
            p.op("dve", lambda e: e.tensor_tensor(out=x3, in0=Ai, in1=lr, op=ALU.mult), reads=k, writes=k)
            p.op("dve", lambda e: e.tensor_tensor(out=x4, in0=x1, in1=li, op=ALU.mult), reads=k, writes=k)
            p.op("dve", lambda e: e.tensor_tensor(out=x3, in0=x3, in1=x4, op=ALU.subtract), reads=k, writes=k)
            p.op("dve", lambda e: e.tensor_tensor(out=Fi, in0=x3, in1=x2, op=ALU.mult), reads=k, writes=k)
            return Ar, Ai, Fr, Fi

        for l in range(DEPTH):
            p.epoch = l + 1
            first, last = (l == 0), (l == DEPTH - 1)
            for (o, gcol, sccol) in ((0, 0, 8), (3, 16, 32)):
                p.op("dve", lambda e, o=o, gcol=gcol, sccol=sccol, l=l: e.scalar_tensor_tensor(out=coefs[:, o, :], in0=modcol[:, l, sccol:sccol + 8], scalar=1.0, in1=vec[:, l, gcol:gcol + 8], op0=ALU.add, op1=ALU.mult),
                     reads=["modcol", "vec", "coefs"], writes=["coefs"])
            for (o, shcol) in ((1, 0), (4, 24)):
                p.op("dve", lambda e, o=o, shcol=shcol, l=l: e.tensor_copy(out=coefs[:, o, :], in_=modcol[:, l, shcol:shcol + 8]), reads=["modcol", "coefs"], writes=["coefs"])
            for (o, gcol, pcol) in ((2, 16, 8), (5, 40, 24)):
                p.op("dve", lambda e, o=o, gcol=gcol, pcol=pcol, l=l: e.tensor_tensor(out=coefs[:, o, :], in0=modcol[:, l, gcol:gcol + 8], in1=vec[:, l, pcol:pcol + 8], op=ALU.mult),
                     reads=["modcol", "vec", "coefs"], writes=["coefs"])

            p.mark('coefs_done')
            fence(TAILK)
            p.dma("pool", lambda e, l=l: e.dma_start(out=W_IN, in_=win_d[l].rearrange("(k p) n -> p k n", p=128)), "wld", reads=["wbuf"], writes=["wbuf"])
            p.dma("pool", lambda e, l=l: e.dma_start(out=W_OUT, in_=wout_d[l].rearrange("(k p) n -> p k n", p=128)), "wld", reads=["wbuf"], writes=["wbuf"])
            p.dma("pool", lambda e, l=l: e.dma_start(out=W_GLU, in_=wglu_d[l].rearrange("(k p) n -> p k n", p=128)), "wld", reads=["wbuf"], writes=["wbuf"])
            p.dma("sp", lambda e, l=l: e.dma_start(out=lamP[:], in_=lamP_d[l]), "s0", reads=["lamP"], writes=["lamP"])
            p.dma("sp", lambda e, l=l: e.dma_start(out=lamC, in_=lamC_d[l]), "s1", reads=["lamC"], writes=["lamC"])
            p.dma("sp", lambda e, l=l: e.dma_start(out=bC, in_=bC_d[l]), "s2", reads=["bC"], writes=["bC"])
            p.dma("sp", lambda e, l=l: e.dma_start(out=cP, in_=cP_d[l]), "s3", reads=["cP"], writes=["cP"])
            p.mark('wload_issued')
            SP_ = [pa[:, i, :] for i in range(12)]
            ArP, AiP, _, _ = ssm_prep(SP_, lamP[:, 0, :], lamP[:, 1, :], lamP[:, 2, :], ["pa", "lamP"])
            p.op("dve", lambda e: e.tensor_copy(out=Apow[:, 0, :, 0], in_=ArP), reads=["pa", "Apow"], writes=["Apow"])
            p.op("dve", lambda e: e.tensor_copy(out=Apow[:, 1, :, 0], in_=AiP), reads=["pa", "Apow"], writes=["Apow"])
            for r in range(1, LC):
                cmul("dve", Apow[:, 0, :, r], Apow[:, 1, :, r], Apow[:, 0, :, r - 1], Apow[:, 1, :, r - 1], ArP, AiP, pa[:, 5, :], pa[:, 6, :], ["pa", "Apow"])
            for ri, Av in ((0, ArP), (1, AiP)):
                p.op("dve", lambda e, ri=ri, Av=Av: e.tensor_copy(out=Abc[:, ri, :, :], in_=Av.unsqueeze(2).to_broadcast([128, NP, NCH])), reads=["pa", "Abc"], writes=["Abc"])
            p.mark('Apow_done')
            for q in range(4):
                for m2 in range(2):
                    c0 = q * 32 + m2 * 16
                    p.op("dve", lambda e, q=q, m2=m2, c0=c0: e.tensor_scalar(out=cW[:, q:NP:4, 0, c0:c0 + 16], in0=cP[:, 0, q:NP:4, :], scalar1=mk[:, m2:m2 + 1], scalar2=None, op0=ALU.mult),
                         reads=["cP", "mk", "cW"], writes=["cW"])
                    p.op("dve", lambda e, q=q, m2=m2, c0=c0: e.tensor_scalar(out=cW[:, q:NP:4, 1, c0:c0 + 16], in0=cP[:, 1, q:NP:4, :], scalar1=mk[:, m2:m2 + 1], scalar2=-1.0, op0=ALU.mult, op1=ALU.mult),
                         reads=["cP", "mk", "cW"], writes=["cW"])
            p.mark('cW_done')
            SC_ = [pc[:, i, :, :] for i in range(12)]
            _, _, FrC, FiC = ssm_prep(SC_, lamC[:, 0, :, :], lamC[:, 1, :, :], lamC[:, 2, :, :], ["pc", "lamC"])
            cmul("dve", pc[:, 1, :, :], pc[:, 2, :, :], FrC, FiC, bC[:, 0, :, :], bC[:, 1, :, :], pc[:, 5, :, :], pc[:, 6, :, :], ["pc", "bC"])
            for q in range(4):
                for m2 in range(2):
                    for ri in range(2):
                        p.op("dve", lambda e, q=q, m2=m2, ri=ri: e.tensor_scalar(out=buW[:, q:NP:4, ri, m2 * 64:(m2 + 1) * 64], in0=pc[:, 1 + ri, :, :], scalar1=mk[:, 4 + q * 2 + m2:5 + q * 2 + m2], scalar2=None, op0=ALU.mult),
                             reads=["pc", "mk", "buW"], writes=["buW"])
            fence(["pc", "lamC", "bC"] + BUK)
            p.op("dve", lambda e: e.memset(kT, 0.0), reads=["kT"], writes=["kT"])
            p.mark('tables_done')

            def load_x(ti, from_input):
                t0 = ti * TT
                if from_input:
                    for blk in range(NB):
                        p.dma("sp", lambda e, blk=blk: e.dma_start(out=xtm, in_=x_d[t0 + blk * 128:t0 + (blk + 1) * 128, :]), "xld", reads=["xtm"], writes=["xtm"])
                        for k in range(8):
                            b = ps()
                            p.op("pe", lambda e, b=b, k=k: e.transpose(pss[b][:, 0:128], xtm[:, k * 128:(k + 1) * 128], ident[:]), reads=["xtm", "ident"], writes=[f"ps{b}"])
                            if k % 2:
                                p.op("dve", lambda e, b=b, blk=blk, k=k: e.tensor_copy(out=xT[:, k, blk * 128:(blk + 1) * 128], in_=pss[b][:, 0:128]), reads=[f"ps{b}"], writes=[f"xT{k}"])
                            else:
                                p.op("act", lambda e, b=b, blk=blk, k=k: e.activation(out=xT[:, k, blk * 128:(blk + 1) * 128], in_=pss[b][:, 0:128], func=AF.Identity), reads=[f"ps{b}"], writes=[f"xT{k}"])
                else:
                    p.dma("sp", lambda e: e.dma_start(out=xT[:], in_=xs_d[:, :, t0:t0 + TT]), "xld", reads=["xs"], writes=XT)

            def store_x(ti, to_output):
                t0 = ti * TT
                if to_output:
                    for blk in range(NB):
                        for k in range(8):
                            b = ps()
                            p.op("pe", lambda e, b=b, blk=blk, k=k: e.transpose(pss[b][:, 0:128], xT[:, k, blk * 128:(blk + 1) * 128], ident[:]), reads=[f"xT{k}", "ident"], writes=[f"ps{b}"])
                            p.op("dve", lambda e, b=b, k=k: e.tensor_copy(out=xtm[:, k * 128:(k + 1) * 128], in_=pss[b][:, 0:128]), reads=[f"ps{b}", "xtm"], writes=["xtm"])
                        p.dma("sp", lambda e, blk=blk: e.dma_start(out=y_d[t0 + blk * 128:t0 + (blk + 1) * 128, :], in_=xtm), "yst", reads=["xtm"])
                else:
                    p.dma("sp", lambda e: e.dma_start(out=xs_d[:, :, t0:t0 + TT], in_=xT[:]), "xst", reads=XT, writes=["xs"])

            def norm_in(o_gs, o_sh):
                for k in range(8):
                    p.op("act", lambda e, k=k: e.activation(out=sq[:, k, :], in_=xT[:, k, :], func=AF.Square), reads=[f"xT{k}"], writes=[f"sq{k}"])
                stats_rstd([f"sq{k}" for k in range(8)], [sq[:, k, :] for k in range(8)], 1.0 / D)
                for k in range(8):
                    p.op("dve", lambda e, k=k: e.scalar_tensor_tensor(out=tmpf[:, k, :], in0=xT[:, k, :], scalar=coefs[:, o_gs, k:k + 1], in1=rstd[:], op0=ALU.mult, op1=ALU.mult),
                         reads=[f"xT{k}", "coefs", "rstd"], writes=[f"tmpf{k}"])
                    p.op("act", lambda e, k=k: e.activation(out=hb[:, k, :], in_=tmpf[:, k, :], func=AF.Identity, bias=coefs[:, o_sh, k:k + 1]), reads=[f"tmpf{k}", "coefs"], writes=[f"hb{k}"])

            def resid_out(o_coef, src_mm):
                for m in range(8):
                    b = ps()
                    src_mm(m, b)
                    p.op("dve", lambda e, b=b, m=m: e.tensor_copy(out=tmpf[:, m, :], in_=pss[b][:, 0:TT]), reads=[f"ps{b}"], writes=[f"tmpf{m}"])
                    p.op("act", lambda e, m=m: e.activation(out=sq[:, m, :], in_=tmpf[:, m, :], func=AF.Square), reads=[f"tmpf{m}"], writes=[f"sq{m}"])
                stats_rstd([f"sq{k}" for k in range(8)], [sq[:, k, :] for k in range(8)], 1.0 / D)
                for m in range(8):
                    p.op("dve", lambda e, m=m: e.tensor_tensor(out=tmpf[:, m, :], in0=tmpf[:, m, :], in1=rstd[:], op=ALU.mult), reads=[f"tmpf{m}", "rstd"], writes=[f"tmpf{m}"])
                    p.op("dve", lambda e, m=m: e.scalar_tensor_tensor(out=xT[:, m, :], in0=tmpf[:, m, :], scalar=coefs[:, o_coef, m:m + 1], in1=xT[:, m, :], op0=ALU.mult, op1=ALU.add),
                         reads=[f"tmpf{m}", "coefs", f"xT{m}"], writes=[f"xT{m}"])

            for ti in range(NT):
                load_x(ti, first)
                p.mark('x_loaded')
                norm_in(0, 1)
                p.mark('norm_done')
                for m in range(6):
                    b = ps()
                    for k in range(8):
                        p.op("pe", lambda e, b=b, k=k, m=m: e.matmul(pss[b][:, 0:TT], lhsT=W_IN[:, k, m * 128:(m + 1) * 128], rhs=hb[:, k, :], start=(k == 0), stop=(k == 7)), reads=["wbuf", f"hb{k}"], writes=[f"ps{b}"])
                    if m < 4:
                        p.op("act", lambda e, b=b, m=m: e.activation(out=qT[:, m, :], in_=pss[b][:, 0:TT], func=AF.Identity, scale=0.125), reads=[f"ps{b}"], writes=[f"qT{m}"])
                    else:
                        p.op("dve", lambda e, b=b, m=m: e.tensor_copy(out=kT[0:64, m - 4, 0, 128:128 + TT], in_=pss[b][0:64, 0:TT]), reads=[f"ps{b}"], writes=["kT"])
                        p.op("dve", lambda e, b=b, m=m: e.tensor_copy(out=kT[64:128, m - 4, 1, 128:128 + TT], in_=pss[b][64:128, 0:TT]), reads=[f"ps{b}"], writes=["kT"])
                for s in range(4):
                    b = ps()
                    for k in range(8):
                        p.op("pe", lambda e, b=b, k=k, s=s: e.matmul(pss[b][:, 0:TT], lhsT=W_IN[:, k, 896 + s * 128:896 + (s + 1) * 128], rhs=hb[:, k, :], start=(k == 0), stop=(k == 7)), reads=["wbuf", f"hb{k}"], writes=[f"ps{b}"])
                    if s % 2:
                        p.op("act", lambda e, b=b, s=s: e.activation(out=uT[:, s, :], in_=pss[b][:, 0:TT], func=AF.Identity), reads=[f"ps{b}"], writes=[f"uT{s}"])
                    else:
                        p.op("dve", lambda e, b=b, s=s: e.tensor_copy(out=uT[:, s, :], in_=pss[b][:, 0:TT]), reads=[f"ps{b}"], writes=[f"uT{s}"])
                for blk in range(NB):
                    b = ps()
                    for k in range(8):
                        p.op("pe", lambda e, b=b, k=k, blk=blk: e.matmul(pss[b][:, 0:128], lhsT=hb[:, k, blk * 128:(blk + 1) * 128], rhs=W_IN[:, k, 768:896], start=(k == 0), stop=(k == 7)), reads=["wbuf", f"hb{k}"], writes=[f"ps{b}"])
                    for g2 in range(2):
                        p.op("dve", lambda e, b=b, blk=blk, g2=g2: e.tensor_copy(out=vpad[:, blk + 1, g2, 64:128], in_=pss[b][:, g2 * 64:(g2 + 1) * 64]), reads=[f"ps{b}"], writes=[f"vp{blk + 1}"])
                for pr in range(NP):
                    for ri in range(2):
                        b = ps()
                        p.op("pe", lambda e, b=b, pr=pr, ri=ri: e.matmul(pss[b][:, 0:TT], lhsT=buW[:, pr, ri, :], rhs=uT[:, pr // 4, :], start=True, stop=True), reads=["buW", f"uT{pr // 4}"], writes=[f"ps{b}"])
                        if ri:
                            p.op("act", lambda e, b=b, pr=pr, ri=ri: e.activation(out=bu[:, ri, pr, :], in_=pss[b][:, 0:TT], func=AF.Identity), reads=[f"ps{b}"], writes=[bukey(pr)])
                        else:
                            p.op("dve", lambda e, b=b, pr=pr, ri=ri: e.tensor_copy(out=bu[:, ri, pr, :], in_=pss[b][:, 0:TT]), reads=[f"ps{b}"], writes=[bukey(pr)])
                p.mark('bu_done')
                _i0 = len(p.ops)
                p.mark('inproj_done')
                for blk in range(NB):
                    gb = ti * NB + blk
                    kts = (1,) if gb == 0 else (0, 1)
                    for g2 in range(2):
                        for kt in kts:
                            b = ps()
                            for hh in range(4):
                                h = 4 * g2 + hh
                                qt, half = h // 2, h % 2
                                p.op("pe", lambda e, b=b, hh=hh, qt=qt, half=half, kt=kt, blk=blk, g2=g2: e.matmul(
                                    pss[b][:, hh * 128:(hh + 1) * 128], lhsT=kT[:, g2, half, (blk + kt) * 128:(blk + kt + 1) * 128],
                                    rhs=qT[:, qt, blk * 128:(blk + 1) * 128], start=True, stop=True), reads=["kT", f"qT{qt}"], writes=[f"ps{b}"])
                            p.op("dve", lambda e, b=b, g2=g2, kt=kt: e.tensor_tensor(out=sc_f.rearrange("p (h q) -> p h q", h=4), in0=pss[b][:, :].rearrange("p (h q) -> p h q", h=4),
                                 in1=maskb[:, 4 * g2:4 * g2 + 4, kt, :], op=ALU.add), reads=[f"ps{b}", "maskb"], writes=["sc_f"])
                            p.op("act", lambda e, kt=kt: e.activation(out=pexp[:, kt, :], in_=sc_f, func=AF.Exp), reads=["sc_f"], writes=[f"pexp{kt}"])
                        for j in range(2):
                            qt = 2 * g2 + j
                            bo, bd = ps(), ps()
                            n = len(kts) * 2
                            i = 0
                            for kt in kts:
                                for half in range(2):
                                    hh = 2 * j + half
                                    lo = 64 if half == 0 else 0
                                    p.op("pe", lambda e, bo=bo, kt=kt, hh=hh, lo=lo, i=i, n=n, blk=blk, g2=g2: e.matmul(pss[bo][:, 0:128], lhsT=vpad[:, blk + kt, g2, lo:lo + 128], rhs=pexp[:, kt, hh * 128:(hh + 1) * 128], start=(i == 0), stop=(i == n - 1)),
                                         reads=[f"vp{blk + kt}", f"pexp{kt}"], writes=[f"ps{bo}"])
                                    p.op("pe", lambda e, bd=bd, kt=kt, hh=hh, lo=lo, i=i, n=n: e.matmul(pss[bd][:, 0:128], lhsT=ones_pad[:, lo:lo + 128], rhs=pexp[:, kt, hh * 128:(hh + 1) * 128], start=(i == 0), stop=(i == n - 1)),
                                         reads=["ones_pad", f"pexp{kt}"], writes=[f"ps{bd}"])
                                    i += 1
                            p.op("dve", lambda e, bd=bd, qt=qt, l=l: e.tensor_scalar(out=rden, in0=pss[bd][:, 0:128], scalar1=esink[:, l, qt:qt + 1], scalar2=None, op0=ALU.add), reads=[f"ps{bd}", "esink"], writes=["rden"])
                            p.op("dve", lambda e: e.reciprocal(out=rden, in_=rden), reads=["rden"], writes=["rden"])
                            p.op("dve", lambda e, bo=bo, qt=qt, blk=blk: e.tensor_tensor(out=attn[:, qt, blk * 128:(blk + 1) * 128], in0=pss[bo][:, 0:128], in1=rden, op=ALU.mult), reads=[f"ps{bo}", "rden"], writes=[f"attn{qt}"])
                p.mark('attn_done')
                p.op("dve", lambda e: e.tensor_copy(out=kT[:, :, :, 0:128], in_=kT[:, :, :, TT:TT + 128]), reads=["kT"], writes=["kT"])
                p.op("dve", lambda e: e.tensor_copy(out=vpad[:, 0, :, 64:128], in_=vpad[:, NB, :, 64:128]), reads=[f"vp{NB}"], writes=["vp0"])
                for k in range(4):
                    p.op("act", lambda e, k=k: e.activation(out=sq[:, k, :], in_=attn[:, k, :], func=AF.Square), reads=[f"attn{k}"], writes=[f"sq{k}"])
                stats_rstd([f"sq{k}" for k in range(4)], [sq[:, k, :] for k in range(4)], 1.0 / 512)
                for k in range(4):
                    p.op("dve", lambda e, k=k, l=l: e.scalar_tensor_tensor(out=heads[:, k, :], in0=attn[:, k, :], scalar=vec[:, l, 32 + k:33 + k], in1=rstd[:], op0=ALU.mult, op1=ALU.mult), reads=[f"attn{k}", "vec", "rstd"], writes=[f"heads{k}"])
                p.mark('heads_done')
                _s1 = p.ops[_i0:]; del p.ops[_i0:]
                for (eng, pa_, pb_) in SCAN_SPLIT:
                    nb_ = pb_ - pa_
                    kb = f"bus{pa_}"
                    m1 = t1.rearrange("p a (i j) -> p i a j", i=4)[:, 0:2, pa_:pb_, :]
                    m2 = t2.rearrange("p a (i j) -> p i a j", i=4)[:, 0:2, pa_:pb_, :]
                    n1 = t1.rearrange("p a (i j) -> p i a j", i=4)[:, 2:4, pa_:pb_, :]
                    n2 = t2.rearrange("p a (i j) -> p i a j", i=4)[:, 2:4, pa_:pb_, :]
                    km = f"scr{pa_}"
                    zv = lambda r, pa_=pa_, pb_=pb_: bu[:, :, pa_:pb_, :].rearrange("p i a (j r) -> p i a j r", r=LC)[:, :, :, :, r]
                    AR2 = Abc[:, 0, pa_:pb_, :].unsqueeze(1).to_broadcast([128, 2, nb_, NCH])
                    AI2 = Abc[:, 1, pa_:pb_, :].unsqueeze(1).to_broadcast([128, 2, nb_, NCH])
                    for r in range(1, LC):
                        p.op(eng, lambda e, r=r, zv=zv, m1=m1, AR2=AR2: e.tensor_tensor(out=m1, in0=AR2, in1=zv(r - 1), op=ALU.mult), reads=["Abc", kb], writes=[km + "a"])
                        p.op(eng, lambda e, r=r, zv=zv, m2=m2, AI2=AI2: e.tensor_tensor(out=m2, in0=AI2, in1=zv(r - 1), op=ALU.mult), reads=["Abc", kb], writes=[km + "b"])
                        p.op(eng, lambda e, r=r, zv=zv, m1=m1: e.tensor_tensor(out=zv(r), in0=zv(r), in1=m1, op=ALU.add), reads=[km + "a", kb], writes=[kb])
                        p.op(eng, lambda e, r=r, zv=zv, m2=m2: e.tensor_tensor(out=zv(r)[:, 0], in0=zv(r)[:, 0], in1=m2[:, 1], op=ALU.subtract), reads=[km + "b", kb], writes=[kb])
                        p.op(eng, lambda e, r=r, zv=zv, m2=m2: e.tensor_tensor(out=zv(r)[:, 1], in0=zv(r)[:, 1], in1=m2[:, 0], op=ALU.add), reads=[km + "b", kb], writes=[kb])
                    PR2 = Apow[:, 0, pa_:pb_, :].unsqueeze(1).to_broadcast([128, 2, nb_, LC])
                    PI2 = Apow[:, 1, pa_:pb_, :].unsqueeze(1).to_broadcast([128, 2, nb_, LC])
                    Hb = Hc[:, l, :, pa_:pb_].unsqueeze(3).to_broadcast([128, 2, nb_, LC])
                    kh = f"Hc{pa_}"
                    for j in range(NCH):
                        zj = bu[:, :, pa_:pb_, j * LC:(j + 1) * LC]
                        p.op(eng, lambda e, n1=n1, PR2=PR2, Hb=Hb: e.tensor_tensor(out=n1, in0=PR2, in1=Hb, op=ALU.mult), reads=["Apow", kh], writes=[km + "c"])
                        p.op(eng, lambda e, n2=n2, PI2=PI2, Hb=Hb: e.tensor_tensor(out=n2, in0=PI2, in1=Hb, op=ALU.mult), reads=["Apow", kh], writes=[km + "d"])
                        p.op(eng, lambda e, zj=zj, n1=n1: e.tensor_tensor(out=zj, in0=zj, in1=n1, op=ALU.add), reads=[km + "c", kb], writes=[kb])
                        p.op(eng, lambda e, zj=zj, n2=n2: e.tensor_tensor(out=zj[:, 0], in0=zj[:, 0], in1=n2[:, 1], op=ALU.subtract), reads=[km + "d", kb], writes=[kb])
                        p.op(eng, lambda e, zj=zj, n2=n2: e.tensor_tensor(out=zj[:, 1], in0=zj[:, 1], in1=n2[:, 0], op=ALU.add), reads=[km + "d", kb], writes=[kb])
                        p.op(eng, lambda e, j=j, l=l, pa_=pa_, pb_=pb_: e.tensor_copy(out=Hc[:, l, :, pa_:pb_], in_=bu[:, :, pa_:pb_, j * LC + LC - 1]), reads=[kb], writes=[kh])
                _s2 = p.ops[_i0:]; del p.ops[_i0:]
                p.ops.extend(_interleave(_s1, _s2))
                p.mark('scan_done')
                for hf in range(2):
                    for ri in range(2):
                        p.op("act", lambda e, hf=hf, ri=ri: e.activation(out=hbf[:, ri], in_=bu[:, ri, hf * 8:(hf + 1) * 8, :], func=AF.Identity), reads=BUK, writes=["hbf"])
                    for s in (2 * hf, 2 * hf + 1):
                        b = ps()
                        i = 0
                        for q in range(4):
                            pr = 4 * s + q
                            for ri in range(2):
                                p.op("pe", lambda e, b=b, pr=pr, ri=ri, i=i, hf=hf: e.matmul(pss[b][:, 0:TT], lhsT=cW[:, pr, ri, :], rhs=hbf[:, ri, pr - 8 * hf, :], start=(i == 0), stop=(i == 7)), reads=["cW", "hbf"], writes=[f"ps{b}"])
                                i += 1
                        p.op("dve", lambda e, b=b, s=s, l=l: e.scalar_tensor_tensor(out=yss[:, s, :], in0=uT[:, s, :], scalar=vec[:, l, 44 + s:45 + s], in1=pss[b][:, 0:TT], op0=ALU.mult, op1=ALU.add), reads=[f"ps{b}", f"uT{s}", "vec"], writes=["yss"])
                p.mark('y_done')
                p.op("pool", lambda e: e.tensor_tensor(out=ys2, in0=yss, in1=yss, op=ALU.mult), reads=["yss", "ys2"], writes=["ys2"])
                p.op("pool", lambda e: e.tensor_scalar(out=ys2, in0=ys2, scalar1=0.044715, scalar2=1.0, op0=ALU.mult, op1=ALU.add), reads=["ys2"], writes=["ys2"])
                p.op("pool", lambda e: e.tensor_tensor(out=ys2, in0=ys2, in1=yss, op=ALU.mult), reads=["yss", "ys2"], writes=["ys2"])
                p.op("act", lambda e: e.activation(out=ys2, in_=ys2, func=AF.Sigmoid, scale=1.5957691216057308), reads=["ys2"], writes=["ys2"])
                p.op("dve", lambda e: e.tensor_tensor(out=yss, in0=yss, in1=ys2, op=ALU.mult), reads=["yss", "ys2"], writes=["yss"])
                p.op("act", lambda e: e.activation(out=zb, in_=yss, func=AF.Identity), reads=["yss", "zb"], writes=["zb"])
                for mo in range(4):
                    b = ps()
                    for k in range(4):
                        p.op("pe", lambda e, b=b, k=k, mo=mo: e.matmul(pss[b][:, 0:TT], lhsT=W_GLU[:, k, mo * 128:(mo + 1) * 128], rhs=zb[:, k, :], start=(k == 0), stop=(k == 3)), reads=["wbuf", "zb"], writes=[f"ps{b}"])
                    p.op("act", lambda e, b=b, mo=mo, l=l: e.activation(out=ys2[:, mo, :], in_=pss[b][:, 0:TT], func=AF.Sigmoid, bias=vec[:, l, 48 + mo:49 + mo]), reads=[f"ps{b}", "vec"], writes=["ys2"])
                p.op("dve", lambda e: e.tensor_tensor(out=ys2, in0=yss, in1=ys2, op=ALU.mult), reads=["yss", "ys2"], writes=["ys2"])
                p.op("act", lambda e: e.activation(out=zb, in_=ys2, func=AF.Square), reads=["ys2", "zb"], writes=["zb"])
                stats_rstd(["zb"] * 4, [zb[:, k, :] for k in range(4)], 1.0 / 512)
                for k in range(4):
                    p.op("dve", lambda e, k=k, l=l: e.scalar_tensor_tensor(out=sheads[:, k, :], in0=ys2[:, k, :], scalar=vec[:, l, 40 + k:41 + k], in1=rstd[:], op0=ALU.mult, op1=ALU.mult), reads=["ys2", "vec", "rstd"], writes=[f"sheads{k}"])

                p.mark('ssm_done')
                def mm_out(m, b):
                    for k in range(8):
                        src, key = (heads, f"heads{k}") if k < 4 else (sheads, f"sheads{k - 4}")
                        p.op("pe", lambda e, k=k, src=src: e.matmul(pss[b][:, 0:TT], lhsT=W_OUT[:, k, m * 128:(m + 1) * 128], rhs=src[:, k % 4, :], start=(k == 0), stop=(k == 7)), reads=["wbuf", key], writes=[f"ps{b}"])
                resid_out(2, mm_out)
                store_x(ti, False)

            p.mark('mixer_done')
            fence(TAILK)
            p.dma("pool", lambda e, l=l: e.dma_start(out=W_M1, in_=wm1_d[l].rearrange("(k p) n -> p k n", p=128)), "wld", reads=["wbuf"], writes=["wbuf"])
            p.dma("pool", lambda e, l=l: e.dma_start(out=W_M2, in_=wm2_d[l].rearrange("(k p) n -> p k n", p=128)), "wld", reads=["wbuf"], writes=["wbuf"])
            for ti in range(NT):
                load_x(ti, False)
                norm_in(3, 4)
                for f in range(32):
                    b = ps()
                    for k in range(8):
                        p.op("pe", lambda e, b=b, k=k, f=f: e.matmul(pss[b][:, 0:TT], lhsT=W_M1[:, k, f * 128:(f + 1) * 128], rhs=hb[:, k, :], start=(k == 0), stop=(k == 7)), reads=["wbuf", f"hb{k}"], writes=[f"ps{b}"])
                    p.op("act", lambda e, b=b, f=f: e.activation(out=hid[:, f, :], in_=pss[b][:, 0:TT], func=AF.Relu), reads=[f"ps{b}"], writes=[f"hid{f}"])
                    p.op("pool" if f % 2 else "dve", lambda e, f=f: e.tensor_tensor(out=hid[:, f, :], in0=hid[:, f, :], in1=hid[:, f, :], op=ALU.mult), reads=[f"hid{f}"], writes=[f"hid{f}"])

                def mm_mlp(m, b):
                    for f in range(32):
                        p.op("pe", lambda e, f=f: e.matmul(pss[b][:, 0:TT], lhsT=W_M2[:, f, m * 128:(m + 1) * 128], rhs=hid[:, f, :], start=(f == 0), stop=(f == 31)), reads=["wbuf", f"hid{f}"], writes=[f"ps{b}"])
                resid_out(5, mm_mlp)
                store_x(ti, last)
        print("n_ops", len(p.ops), p.marks, flush=True)
        import os
        if os.environ.get("KTRUNC"):
            p.ops = p.ops[:int(os.environ["KTRUNC"])]
        if os.environ.get("KDBG"):
            dbg_list = {"modcol": modcol[:], "coefs": coefs[:], "xT": xT[:], "hb": hb[:], "qT": qT, "kT": kT, "uT": uT, "vpad": vpad[:], "attn": attn,
                        "heads": heads, "bu": bu, "yss": yss, "ys2": ys2, "sheads": sheads, "Apow": Apow, "buW": buW[:], "cW": cW[:], "Hc": Hc[:], "rstd": rstd[:], "tmpf": tmpf[:], "pexp": pexp, "hid": hid}
            for nm in os.environ["KDBG"].split(","):
                ap_ = dbg_list[nm]
                shp = list(ap_.shape)
                dd = nc.dram_tensor("dbg_" + nm, shp, F32, kind="ExternalOutput").ap()
                p.dma("pool", lambda e, dd=dd, ap_=ap_: e.dma_start(out=dd, in_=ap_), "dbg_" + nm, reads=TAILK + XT + HIDK + ["modcol", "coefs", "buW", "cW", "rstd"] + [f"hb{k}" for k in range(8)] + [f"tmpf{k}" for k in range(8)] + [f"vp{i}" for i in range(NB + 1)])
        fin = ["yst"] if not os.environ.get("KTRUNC") else []
        if os.environ.get("KDBG"):
            fin += ["dbg_" + nm for nm in os.environ["KDBG"].split(",")]
        p.emit(final_dma_tags=fin)
    return nc


def _host_prep(inputs, b, T, DEPTH):
    f = lambda a: np.ascontiguousarray(a, dtype=np.float32)
    L = DEPTH
    col = lambda v, n: v.reshape(n, 128).T
    m = {}
    m["x"] = f(inputs["x"][b, :T])
    m["ccol"] = f(col(inputs["c"][b], 8))
    m["w_ada"] = f(inputs["w_ada"][:L])
    m["b_ada"] = f(inputs["b_ada"][:L].reshape(L, 1, 6 * D))
    vecs = np.zeros((L, 128, NV), np.float32)
    for l in range(L):
        vecs[l, :, 0:8] = col(inputs["pre_mix_g"][l], 8)
        vecs[l, :, 8:16] = col(inputs["post_mix_g"][l], 8)
        vecs[l, :, 16:24] = col(inputs["pre_mlp_g"][l], 8)
        vecs[l, :, 24:32] = col(inputs["post_mlp_g"][l], 8)
        vecs[l, :, 32:36] = col(inputs["attn_out_g"][l], 4)
        vecs[l, :, 36:40] = np.repeat(inputs["attn_sinks"][l].reshape(4, 2), 64, axis=1).T
        vecs[l, :, 40:44] = col(inputs["ssm_out_g"][l], 4)
        vecs[l, :, 44:48] = col(inputs["d_skip"][l], 4)
        vecs[l, :, 48:52] = col(inputs["b_glu"][l], 4)
    m["vecs"] = vecs
    w = inputs["w_in"][:L]
    m["w_in"] = f(np.concatenate([w[:, :, 0:512], w[:, :, 512:576], w[:, :, 512:576], w[:, :, 576:640], w[:, :, 576:640], w[:, :, 640:768], w[:, :, 768:1280]], axis=2))
    for k in ("w_glu", "w_out", "w_mlp_in", "w_mlp_out"):
        m[k] = f(inputs[k][:L])
    lam = np.stack([inputs["lam_re"][:L], inputs["lam_im"][:L], np.broadcast_to(inputs["log_dt"][:L][:, :, None], (L, G, P))], axis=1)
    lam5 = lam.reshape(L, 3, NP, 2, P)
    m["lamP"] = f(lam5.transpose(0, 3, 4, 1, 2).reshape(L, 128, 3, NP))
    lam6 = lam.reshape(L, 3, 4, 4, 2, P)
    lamC = np.broadcast_to(lam6[:, :, :, :, :, None, :], (L, 3, 4, 4, 2, C, P))
    m["lamC"] = f(lamC.transpose(0, 3, 4, 5, 1, 2, 6).reshape(L, 128, 3, 4, P))
    bb = np.stack([inputs["b_re"][:L], inputs["b_im"][:L]], axis=1).reshape(L, 2, 4, 4, 2, P, C)
    m["bC"] = f(bb.transpose(0, 3, 4, 6, 1, 2, 5).reshape(L, 128, 2, 4, P))
    cc = np.stack([inputs["c_re"][:L], inputs["c_im"][:L]], axis=1).reshape(L, 2, NP, 2, C, P)
    m["cP"] = f(cc.transpose(0, 3, 5, 1, 2, 4).reshape(L, 128, 2, NP, C))
    slopes = 2.0 ** (-8.0 * np.arange(1, NH + 1) / NH)
    s_ = np.arange(128)[:, None]; q_ = np.arange(128)[None, :]
    mb = np.full((128, NH, 2, 128), -30000.0, np.float32)
    for h in range(NH):
        d0 = 128 + q_ - s_
        d1 = q_ - s_
        mb[:, h, 0, :] = np.where((d0 >= 0) & (d0 < 128), -slopes[h] * d0, -30000.0)
        mb[:, h, 1, :] = np.where((d1 >= 0) & (d1 < 128), -slopes[h] * d1, -30000.0)
    m["maskb"] = mb
    m["ident"] = np.eye(128, dtype=np.float32)
    mk = np.zeros((128, 12), np.float32); mk[:64, 0] = 1; mk[64:, 1] = 1; mk[:, 3] = 1e-6
    rows = np.arange(128)
    for q in range(4):
        for mm in range(2):
            mk[:, 4 + q * 2 + mm] = ((rows // 32 == q) & ((rows % 32) // 16 == mm)).astype(np.float32)
    m["masks"] = mk
    return m


def kernel(**inputs):
    inputs = {k: np.asarray(v) for k, v in inputs.items()}
    B, T, _ = inputs["x"].shape
    DEPTH = inputs["w_in"].shape[0]
    nc = build(T, DEPTH)
    in_maps = [_host_prep(inputs, b, T, DEPTH) for b in range(B)]
    res = run_bass_kernel_spmd(nc, in_maps, core_ids=list(range(B)))
    return np.stack([r["y"] for r in res.results], axis=0).astype(np.float32)
```

```python
import contextlib
import numpy as np
import concourse.bass as bass
import concourse.mybir as mybir
from concourse.bass_utils import run_bass_kernel_spmd

F32 = mybir.dt.float32
BF16 = mybir.dt.bfloat16
AF = mybir.ActivationFunctionType
ALU = mybir.AluOpType
ENGS = ("pe", "act", "dve", "pool", "sp")
import os as _os
NOSYNC_SAME = bool(_os.environ.get("KNOSYNC"))

D = 1024
NH = 8
G = 32
P = 64
C = 16
NP = 16
DFF = 4096
INW = 1408
LC = 16
TT = 256
NV = 52


class Prog:
    def __init__(self, nc, n_epochs=1):
        self.nc = nc
        self.ops = []
        self.epoch = 0
        self.n_epochs = n_epochs
        self.marks = []

    def op(self, eng, fn, reads=(), writes=(), dma=None):
        self.ops.append(dict(eng=eng, fn=fn, reads=tuple(reads), writes=tuple(writes),
                             dma=dma, epoch=self.epoch, waits=[], inc=False))

    def dma(self, q, fn, tag, reads=(), writes=()):
        self.op(q, fn, reads, writes, dma=tag)

    def mark(self, name):
        self.marks.append((name, len(self.ops)))

    def analyze(self):
        ops = self.ops
        last_w, readers, waited, dma_count = {}, {}, {}, {}
        for i, o in enumerate(ops):
            deps = {}
            for r in o["reads"]:
                if r in last_w:
                    deps[last_w[r]] = "raw"
            for w in o["writes"]:
                if w in last_w:
                    deps.setdefault(last_w[w], "waw")
                for rd in readers.get(w, ()):
                    if rd != i:
                        deps.setdefault(rd, "war")
            best = {}
            for d, kind in deps.items():
                od = ops[d]
                if od["dma"] is not None:
                    key = ("dma", od["dma"])
                    best[key] = max(best.get(key, 0), dma_count[od["dma"]])
                else:
                    if od["eng"] == o["eng"] and o["dma"] is None:
                        if od["eng"] == "pe" or kind in ("war", "waw") or NOSYNC_SAME:
                            continue
                    key = ("eng", od["eng"], od["epoch"])
                    best[key] = max(best.get(key, -1), d)
            for key, val in best.items():
                wk = (o["eng"], key)
                if waited.get(wk, -1) >= val:
                    continue
                waited[wk] = val
                if key[0] == "eng":
                    ops[val]["inc"] = True
                o["waits"].append((key, val))
            if o["dma"] is not None:
                dma_count[o["dma"]] = dma_count.get(o["dma"], 0) + 1
            for r in o["reads"]:
                readers.setdefault(r, []).append(i)
            for w in o["writes"]:
                last_w[w] = i
                readers[w] = []
        cnt = {}
        for o in ops:
            if o["dma"] is None and o["inc"]:
                k = (o["eng"], o["epoch"])
                cnt[k] = cnt.get(k, 0) + 1
                o["cnt"] = cnt[k]
        self.max_counts = cnt
        self.dma_tags = dma_count

    def emit(self, final_dma_tags=()):
        nc = self.nc
        self.analyze()
        ops = self.ops
        with contextlib.ExitStack() as st:
            sems = {}
            for (e, ep) in self.max_counts:
                sems[("eng", e, ep)] = st.enter_context(nc.semaphore(f"s_{e}_{ep}"))
            for t in self.dma_tags:
                sems[("dma", t)] = st.enter_context(nc.semaphore(f"d_{t}"))
            block = st.enter_context(nc.Block())
            engmap = {"pe": "tensor", "act": "scalar", "dve": "vector", "pool": "gpsimd", "sp": "sync"}

            def make(engname):
                def body(eng):
                    for o in ops:
                        if o["eng"] != engname:
                            continue
                        for key, val in o["waits"]:
                            if key[0] == "dma":
                                eng.wait_ge(sems[key], 16 * val)
                            else:
                                eng.wait_ge(sems[key], ops[val]["cnt"])
                        ins = o["fn"](eng)
                        if o["dma"] is not None:
                            ins.then_inc(sems[("dma", o["dma"])], 16)
                        elif o["inc"]:
                            ins.then_inc(sems[("eng", o["eng"], o["epoch"])], 1)
                    if engname == "sp":
                        for t in final_dma_tags:
                            eng.wait_ge(sems[("dma", t)], 16 * self.dma_tags[t])
                return body

            for e in ENGS:
                getattr(block, engmap[e])(make(e))


def _interleave(s1, s2):
    out, i, j = [], 0, 0
    n1, n2 = len(s1), len(s2)
    while i < n1 or j < n2:
        if j < n2 and (i >= n1 or i * n2 > j * n1):
            out.append(s2[j]); j += 1
        else:
            out.append(s1[i]); i += 1
    return out


def build(T, DEPTH):
    nc = bass.Bass("TRN2", target_bir_lowering=False)
    NT = T // TT
    NB = TT // 128
    NCH = TT // LC
    dr = lambda name, shape, dt=F32, kind="ExternalInput": nc.dram_tensor(name, shape, dt, kind=kind).ap()
    x_d = dr("x", [T, D])
    y_d = dr("y", [T, D], kind="ExternalOutput")
    xs_d = dr("xs", [128, 8, T], kind="Internal")
    ccol_d = dr("ccol", [128, 8])
    wada_d = dr("w_ada", [DEPTH, D, 6 * D])
    bada_d = dr("b_ada", [DEPTH, 1, 6 * D])
    vec_d = dr("vecs", [DEPTH, 128, NV])
    win_d = dr("w_in", [DEPTH, D, INW])
    wglu_d = dr("w_glu", [DEPTH, 512, 512])
    wout_d = dr("w_out", [DEPTH, D, D])
    wm1_d = dr("w_mlp_in", [DEPTH, D, DFF])
    wm2_d = dr("w_mlp_out", [DEPTH, DFF, D])
    lamP_d = dr("lamP", [DEPTH, 128, 3, NP])
    lamC_d = dr("lamC", [DEPTH, 128, 3, 4, P])
    bC_d = dr("bC", [DEPTH, 128, 2, 4, P])
    cP_d = dr("cP", [DEPTH, 128, 2, NP, C])
    maskb_d = dr("maskb", [128, NH, 2, 128])
    ident_d = dr("ident", [128, 128])
    mk_d = dr("masks", [128, 12])

    with contextlib.ExitStack() as st:
        sb = lambda name, shape, dt=F32: st.enter_context(nc.sbuf_tensor("s_" + name, shape, dt))
        p = Prog(nc, n_epochs=DEPTH + 1)
        ident = sb("ident", [128, 128])
        mk = sb("mk", [128, 12])
        ones_b = sb("ones_b", [128, 128], BF16)
        ones_pad = sb("ones_pad", [128, 192], BF16)
        ccol = sb("ccol", [128, 8])
        cact = sb("cact", [128, 8], BF16)
        modcol = sb("modcol", [128, DEPTH, 48])
        vec = sb("vec", [128, DEPTH, NV])
        esink = sb("esink", [128, DEPTH, 4])
        one1 = sb("one1", [1, 2], BF16)
        coefs = sb("coefs", [128, 6, 8])
        dummy = sb("dummy", [128, 2])
        rstd = sb("rstd", [128, TT])
        xT = sb("xT", [128, 8, TT])
        sq = sb("sq", [128, 8, TT], BF16)
        tmpf = sb("tmpf", [128, 8, TT])
        hb = sb("hb", [128, 8, TT], BF16)
        mbuf = sb("mbuf", [128, 14208], BF16)
        abuf = sb("abuf", [128, 10240], BF16)
        wbuf = sb("wbuf", [128, 65536], BF16)
        pss = [st.enter_context(nc.psum_tensor(f"ps{i}", [128, 512], F32)) for i in range(8)]
        psn = [0]

        def ps():
            psn[0] = (psn[0] + 1) % 8
            return psn[0]

        def carve(buf, off, shape, dt, rows=128):
            n = 1
            for d_ in shape:
                n *= d_
            nel = n * (2 if dt == F32 else 1)
            v = buf[0:rows, off:off + nel]
            if dt == F32:
                v = v.bitcast(F32)
            if len(shape) == 1:
                return v, off + nel
            names = " ".join(f"d{i}" for i in range(len(shape)))
            kw = {f"d{i}": shape[i] for i in range(len(shape) - 1)}
            return v.rearrange(f"p ({names}) -> p {names}", **kw), off + nel

        maskb, om = carve(mbuf, 0, [NH, 2, 128], F32)
        buW, om = carve(mbuf, om, [NP, 2, 128], BF16)
        cW, om = carve(mbuf, om, [NP, 2, 128], BF16)
        pa, om = carve(mbuf, om, [12, NP], F32)
        lamP, om = carve(mbuf, om, [3, NP], F32)
        vpad, om = carve(mbuf, om, [NB + 1, 2, 192], BF16)
        Hc, om = carve(mbuf, om, [2, NP], F32)
        assert om <= 14208, om
        xT_B, ob = carve(mbuf, 0, [8, TT], F32)
        sq_B, ob = carve(mbuf, ob, [8, TT], BF16)
        tmpf_B, ob = carve(mbuf, ob, [8, TT], F32)
        hb_B, ob = carve(mbuf, ob, [8, TT], BF16)
        rstd_B, ob = carve(mbuf, ob, [TT], F32)
        assert ob <= 14208, ob
        hid, o_ = carve(abuf, 0, [32, TT], BF16)
        xtm, o_ = carve(abuf, o_, [D], F32)
        assert o_ <= 10240
        modrow_b, _ = carve(abuf, 0, [6 * D], BF16, rows=1)
        W_IN, o_ = carve(wbuf, 0, [8, INW], BF16)
        W_OUT, o_ = carve(wbuf, o_, [8, D], BF16)
        W_GLU, o_ = carve(wbuf, o_, [4, 512], BF16)
        tail0 = o_
        bu, o_ = carve(wbuf, o_, [2, NP, TT], F32)
        hbf, o_ = carve(wbuf, o_, [2, NP // 2, TT], BF16)
        t1, o_ = carve(wbuf, o_, [NP, 64], F32)
        t2, o_ = carve(wbuf, o_, [NP, 64], F32)
        yss, o_ = carve(wbuf, o_, [4, TT], F32)
        ys2, o_ = carve(wbuf, o_, [4, TT], F32)
        attn, o_ = carve(wbuf, o_, [4, TT], F32)
        qT, o_ = carve(wbuf, o_, [4, TT], BF16)
        uT, o_ = carve(wbuf, o_, [4, TT], BF16)
        heads, o_ = carve(wbuf, o_, [4, TT], BF16)
        sheads, o_ = carve(wbuf, o_, [4, TT], BF16)
        zb, o_ = carve(wbuf, o_, [4, TT], BF16)
        sc_f, o_ = carve(wbuf, o_, [512], F32)
        pexp, o_ = carve(wbuf, o_, [2, 512], BF16)
        rden, o_ = carve(wbuf, o_, [128], F32)
        kT, o_ = carve(wbuf, o_, [2, 2, 128 + TT], BF16)
        Abc, o_ = carve(wbuf, o_, [2, NP, NCH], F32)
        Apow, o_ = carve(wbuf, o_, [2, NP, LC], F32)
        cP, o_ = carve(wbuf, o_, [2, NP, C], F32)
        assert o_ <= 65536, o_
        pc, o2 = carve(wbuf, tail0, [12, 4, P], F32)
        lamC, o2 = carve(wbuf, o2, [3, 4, P], F32)
        bC, o2 = carve(wbuf, o2, [2, 4, P], F32)
        assert o2 <= tail0 + 2 * 2 * NP * TT
        W_M1, o3 = carve(wbuf, 0, [8, DFF], BF16)
        W_M2, o3 = carve(wbuf, o3, [32, D], BF16)
        W_ADA, o3 = carve(wbuf, 0, [8, 6 * D], BF16)
        modrow, o3 = carve(wbuf, o3, [6 * D], F32, rows=1)
        assert o3 <= 65536

        XT = [f"xT{k}" for k in range(8)]
        HIDK = [f"hid{f}" for f in range(32)]
        FINEK = [f"qT{m}" for m in range(4)] + [f"uT{m}" for m in range(4)] + [f"attn{m}" for m in range(4)] + [f"heads{m}" for m in range(4)] + [f"sheads{m}" for m in range(4)] + ["pexp0", "pexp1"]
        _ks = int(_os.environ.get("KSPLIT", "8"))
        SCAN_SPLIT = [("dve", 0, _ks)] + ([("pool", _ks, 16)] if _ks < 16 else [])
        BUK = [f"bus{a_}" for (_, a_, _b) in SCAN_SPLIT]
        bukey = lambda pr: [f"bus{a_}" for (_, a_, b_) in SCAN_SPLIT if a_ <= pr < b_][0]
        SCRK = [f"scr{a_}{c_}" for (_, a_, _b) in SCAN_SPLIT for c_ in "abcd"] + [f"Hc{a_}" for (_, a_, _b) in SCAN_SPLIT]
        MK = ["maskb", "buW", "cW", "pa", "lamP"] + [f"vp{i}" for i in range(NB + 1)]
        BK = [f"B{n_}{k}" for n_ in ("xT", "sq", "tmpf", "hb") for k in range(8)] + ["Brstd"]
        TAILK = MK + BK + BUK + SCRK + FINEK + ["hbf", "yss", "ys2", "attn", "qT", "uT",
                 "heads", "sheads", "zb", "sc_f", "pexp", "rden", "kT", "Abc", "Apow", "cP", "pc", "lamC", "bC", "modrow", "wbuf"]

        def fence(keys):
            p.op("dve", lambda e: e.memset(dummy[:, 0:1], 0.0), reads=list(keys), writes=list(keys))

        p.dma("sp", lambda e: e.dma_start(out=ident[:], in_=ident_d), "c0", writes=["ident"])
        p.dma("sp", lambda e: e.dma_start(out=mk[:], in_=mk_d), "c2", writes=["mk"])
        p.dma("sp", lambda e: e.dma_start(out=ccol[:], in_=ccol_d), "c3", writes=["ccol"])
        p.dma("sp", lambda e: e.dma_start(out=vec[:], in_=vec_d.rearrange("l p n -> p l n")), "c4", writes=["vec"])
        p.op("dve", lambda e: e.memset(ones_b[:], 1.0), writes=["ones_b"])
        p.op("dve", lambda e: e.memset(ones_pad[:], 0.0), writes=["ones_pad"])
        p.op("dve", lambda e: e.memset(ones_pad[:, 64:128], 1.0), reads=["ones_pad"], writes=["ones_pad"])
        p.op("dve", lambda e: e.memset(one1[:], 1.0), writes=["one1"])
        p.op("act", lambda e: e.activation(out=coefs[:, 0, :], in_=ccol[:], func=AF.Sigmoid), reads=["ccol"], writes=["coefs"])
        p.op("dve", lambda e: e.tensor_tensor(out=cact[:], in0=coefs[:, 0, :], in1=ccol[:], op=ALU.mult), reads=["coefs", "ccol"], writes=["cact"])
        for l in range(DEPTH):
            p.op("act", lambda e, l=l: e.activation(out=esink[:, l, :], in_=vec[:, l, 36:40], func=AF.Exp), reads=["vec"], writes=["esink"])
        for l in range(DEPTH):
            p.dma("pool", lambda e, l=l: e.dma_start(out=W_ADA, in_=wada_d[l].rearrange("(k p) n -> p k n", p=128)), "wld", reads=["wbuf"], writes=["wbuf"])
            p.dma("sp", lambda e, l=l: e.dma_start(out=modrow, in_=bada_d[l]), "brow", reads=["modrow"], writes=["modrow"])
            for cc in range(12):
                b = ps()
                for k in range(8):
                    p.op("pe", lambda e, b=b, k=k, cc=cc: e.matmul(pss[b][0:1, :], lhsT=cact[:, k:k + 1], rhs=W_ADA[:, k, cc * 512:(cc + 1) * 512], start=(k == 0), stop=(k == 7)),
                         reads=["cact", "wbuf"], writes=[f"ps{b}"])
                p.op("dve", lambda e, b=b, cc=cc: e.tensor_tensor(out=modrow[:, cc * 512:(cc + 1) * 512], in0=pss[b][0:1, :], in1=modrow[:, cc * 512:(cc + 1) * 512], op=ALU.add),
                     reads=[f"ps{b}", "modrow"], writes=["modrow"])
            b = ps()
            for part in range(2):
                p.op("act", lambda e: e.activation(out=modrow_b, in_=modrow, func=AF.Identity), reads=["modrow"], writes=HIDK)
                for j in range(48):
                    p.op("pe", lambda e, b=b, j=j, part=part: e.matmul(pss[b][:, j:j + 1], lhsT=modrow_b[0:1, j * 128:(j + 1) * 128], rhs=one1[0:1, 0:1], start=(part == 0 and j == 0), stop=(part == 1 and j == 47)),
                         reads=HIDK + ["one1"], writes=[f"ps{b}"])
                if part == 0:
                    p.op("dve", lambda e: e.tensor_tensor(out=modrow, in0=modrow, in1=modrow_b, op=ALU.subtract), reads=["modrow"] + HIDK, writes=["modrow"])
            p.op("dve", lambda e, b=b, l=l: e.tensor_copy(out=modcol[:, l, :], in_=pss[b][:, 0:48]), reads=[f"ps{b}"], writes=["modcol"])

        p.mark('setup_done')
        class BSet:
            def __init__(self, xT_, sq_, tmpf_, hb_, rstd_, pre):
                self.xT, self.sq, self.tmpf, self.hb, self.rstd, self.pre = xT_, sq_, tmpf_, hb_, rstd_, pre
        SA = BSet(xT[:], sq[:], tmpf[:], hb[:], rstd[:], "")
        SB = BSet(xT_B, sq_B, tmpf_B, hb_B, rstd_B, "B")

        def stats_rstd(src_keys, ktiles, inv_n, rs=None, rkey="rstd"):
            rstd = SA.rstd if rs is None else rs
            b = ps()
            n = len(ktiles)
            for i, kv in enumerate(ktiles):
                p.op("pe", lambda e, b=b, kv=kv, i=i, n=n: e.matmul(pss[b][:, 0:TT], lhsT=ones_b[:, :], rhs=kv, start=(i == 0), stop=(i == n - 1)),
                     reads=[src_keys[i], "ones_b"], writes=[f"ps{b}"])
            p.op("act", lambda e, b=b: e.activation(out=rstd[:], in_=pss[b][:, 0:TT], func=AF.Sqrt, bias=mk[:, 3:4], scale=inv_n), reads=[f"ps{b}", "mk"], writes=[rkey])
            p.op("dve", lambda e: e.reciprocal(out=rstd[:], in_=rstd[:]), reads=[rkey], writes=[rkey])

        def cmul(eng, o_r, o_i, a_r, a_i, b_r, b_i, s1, s2, keys):
            rd = list(keys)
            p.op(eng, lambda e: e.tensor_tensor(out=s1, in0=a_r, in1=b_r, op=ALU.mult), reads=rd, writes=rd)
            p.op(eng, lambda e: e.tensor_tensor(out=s2, in0=a_i, in1=b_i, op=ALU.mult), reads=rd, writes=rd)
            p.op(eng, lambda e: e.tensor_tensor(out=o_r, in0=s1, in1=s2, op=ALU.subtract), reads=rd, writes=rd)
            p.op(eng, lambda e: e.tensor_tensor(out=s1, in0=a_r, in1=b_i, op=ALU.mult), reads=rd, writes=rd)
            p.op(eng, lambda e: e.tensor_tensor(out=s2, in0=a_i, in1=b_r, op=ALU.mult), reads=rd, writes=rd)
            p.op(eng, lambda e: e.tensor_tensor(out=o_i, in0=s1, in1=s2, op=ALU.add), reads=rd, writes=rd)

        def ssm_prep(S, lr, li, ldt, keys):
            k = list(keys)
            dt, a, th, s_, c_, x1, x2, x3, x4 = S[0], S[1], S[2], S[3], S[4], S[5], S[6], S[7], S[8]
            p.op("act", lambda e: e.activation(out=dt, in_=ldt, func=AF.Exp), reads=k, writes=k)
            p.op("dve", lambda e: e.tensor_tensor(out=a, in0=lr, in1=dt, op=ALU.mult), reads=k, writes=k)
            p.op("act", lambda e: e.activation(out=a, in_=a, func=AF.Exp), reads=k, writes=k)
            p.op("dve", lambda e: e.tensor_tensor(out=th, in0=li, in1=dt, op=ALU.mult), reads=k, writes=k)
            p.op("act", lambda e: e.activation(out=s_, in_=th, func=AF.Sin, scale=1.0 / 16), reads=k, writes=k)
            p.op("act", lambda e: e.activation(out=c_, in_=th, func=AF.Sin, scale=1.0 / 32), reads=k, writes=k)
            p.op("dve", lambda e: e.tensor_tensor(out=c_, in0=c_, in1=c_, op=ALU.mult), reads=k, writes=k)
            p.op("dve", lambda e: e.tensor_scalar(out=c_, in0=c_, scalar1=-2.0, scalar2=1.0, op0=ALU.mult, op1=ALU.add), reads=k, writes=k)
            for _ in range(4):
                p.op("dve", lambda e: e.tensor_tensor(out=x1, in0=c_, in1=c_, op=ALU.mult), reads=k, writes=k)
                p.op("dve", lambda e: e.tensor_tensor(out=x2, in0=s_, in1=s_, op=ALU.mult), reads=k, writes=k)
                p.op("dve", lambda e: e.tensor_tensor(out=x3, in0=c_, in1=s_, op=ALU.mult), reads=k, writes=k)
                p.op("dve", lambda e: e.tensor_tensor(out=c_, in0=x1, in1=x2, op=ALU.subtract), reads=k, writes=k)
                p.op("dve", lambda e: e.tensor_scalar(out=s_, in0=x3, scalar1=2.0, scalar2=None, op0=ALU.mult), reads=k, writes=k)
            Ar, Ai = S[9], S[10]
            p.op("dve", lambda e: e.tensor_tensor(out=Ar, in0=c_, in1=a, op=ALU.mult), reads=k, writes=k)
            p.op("dve", lambda e: e.tensor_tensor(out=Ai, in0=s_, in1=a, op=ALU.mult), reads=k, writes=k)
            p.op("dve", lambda e: e.tensor_scalar(out=x1, in0=Ar, scalar1=-1.0, scalar2=None, op0=ALU.add), reads=k, writes=k)
            p.op("dve", lambda e: e.tensor_tensor(out=x2, in0=lr, in1=lr, op=ALU.mult), reads=k, writes=k)
            p.op("dve", lambda e: e.tensor_tensor(out=x3, in0=li, in1=li, op=ALU.mult), reads=k, writes=k)
            p.op("dve", lambda e: e.tensor_tensor(out=x2, in0=x2, in1=x3, op=ALU.add), reads=k, writes=k)
            p.op("dve", lambda e: e.reciprocal(out=x2, in_=x2), reads=k, writes=k)
            Fr, Fi = S[11], S[0]
            p.op("dve", lambda e: e.tensor_tensor(out=x3, in0=x1, in1=lr, op=ALU.mult), reads=k, writes=k)
            p.op("dve", lambda e: e.tensor_tensor(out=x4, in0=Ai, in1=li, op=ALU.mult), reads=k, writes=k)
            p.op("dve", lambda e: e.tensor_tensor(out=x3, in0=x3, in1=x4, op=ALU.add), reads=k, writes=k)
            p.op("dve", lambda e: e.tensor_tensor(out=Fr, in0=x3, in1=x2, op=ALU.mult), reads=k, writes=k)
            p.op("dve", lambda e: e.tensor_tensor(out=x3, in0=Ai, in1=lr, op=ALU.mult), reads=k, writes=k)
            p.op("dve", lambda e: e.tensor_tensor(out=x4, in0=x1, in1=li, op=ALU.mult), reads=k, writes=k)
            p.op("dve", lambda e: e.tensor_tensor(out=x3, in0=x3, in1=x4, op=ALU.subtract), reads=k, writes=k)
            p.op("dve", lambda e: e.tensor_tensor(out=Fi, in0=x3, in1=x2, op=ALU.mult), reads=k, writes=k)
            return Ar, Ai, Fr, Fi

        for l in range(DEPTH):
            p.epoch = l + 1
            first, last = (l == 0), (l == DEPTH - 1)
            for (o, gcol, sccol) in ((0, 0, 8), (3, 16, 32)):
                p.op("dve", lambda e, o=o, gcol=gcol, sccol=sccol, l=l: e.scalar_tensor_tensor(out=coefs[:, o, :], in0=modcol[:, l, sccol:sccol + 8], scalar=1.0, in1=vec[:, l, gcol:gcol + 8], op0=ALU.add, op1=ALU.mult),
                     reads=["modcol", "vec", "coefs"], writes=["coefs"])
            for (o, shcol) in ((1, 0), (4, 24)):
                p.op("dve", lambda e, o=o, shcol=shcol, l=l: e.tensor_copy(out=coefs[:, o, :], in_=modcol[:, l, shcol:shcol + 8]), reads=["modcol", "coefs"], writes=["coefs"])
            for (o, gcol, pcol) in ((2, 16, 8), (5, 40, 24)):
                p.op("dve", lambda e, o=o, gcol=gcol, pcol=pcol, l=l: e.tensor_tensor(out=coefs[:, o, :], in0=modcol[:, l, gcol:gcol + 8], in1=vec[:, l, pcol:pcol + 8], op=ALU.mult),
                     reads=["modcol", "vec", "coefs"], writes=["coefs"])

            p.mark('coefs_done')
            fence(TAILK)
            p.dma("sp", lambda e: e.dma_start(out=maskb, in_=maskb_d), "c1", reads=["maskb"], writes=["maskb"])
            p.op("dve", lambda e: e.memset(vpad, 0.0), writes=[f"vp{i}" for i in range(NB + 1)])
            p.op("dve", lambda e: e.memset(Hc, 0.0), writes=[f"Hc{a_}" for (_, a_, _b) in SCAN_SPLIT])
            p.op("dve", lambda e: e.memset(cW, 0.0), writes=["cW"])
            p.dma("pool", lambda e, l=l: e.dma_start(out=W_IN, in_=win_d[l].rearrange("(k p) n -> p k n", p=128)), "wld", reads=["wbuf"], writes=["wbuf"])
            p.dma("pool", lambda e, l=l: e.dma_start(out=W_OUT, in_=wout_d[l].rearrange("(k p) n -> p k n", p=128)), "wld", reads=["wbuf"], writes=["wbuf"])
            p.dma("pool", lambda e, l=l: e.dma_start(out=W_GLU, in_=wglu_d[l].rearrange("(k p) n -> p k n", p=128)), "wld", reads=["wbuf"], writes=["wbuf"])
            p.dma("sp", lambda e, l=l: e.dma_start(out=lamP, in_=lamP_d[l]), "s0", reads=["lamP"], writes=["lamP"])
            p.dma("sp", lambda e, l=l: e.dma_start(out=lamC, in_=lamC_d[l]), "s1", reads=["lamC"], writes=["lamC"])
            p.dma("sp", lambda e, l=l: e.dma_start(out=bC, in_=bC_d[l]), "s2", reads=["bC"], writes=["bC"])
            p.dma("sp", lambda e, l=l: e.dma_start(out=cP, in_=cP_d[l]), "s3", reads=["cP"], writes=["cP"])
            p.mark('wload_issued')
            SP_ = [pa[:, i, :] for i in range(12)]
            ArP, AiP, _, _ = ssm_prep(SP_, lamP[:, 0, :], lamP[:, 1, :], lamP[:, 2, :], ["pa", "lamP"])
            p.op("dve", lambda e: e.tensor_copy(out=Apow[:, 0, :, 0], in_=ArP), reads=["pa", "Apow"], writes=["Apow"])
            p.op("dve", lambda e: e.tensor_copy(out=Apow[:, 1, :, 0], in_=AiP), reads=["pa", "Apow"], writes=["Apow"])
            for r in range(1, LC):
                cmul("dve", Apow[:, 0, :, r], Apow[:, 1, :, r], Apow[:, 0, :, r - 1], Apow[:, 1, :, r - 1], ArP, AiP, pa[:, 5, :], pa[:, 6, :], ["pa", "Apow"])
            for ri, Av in ((0, ArP), (1, AiP)):
                p.op("dve", lambda e, ri=ri, Av=Av: e.tensor_copy(out=Abc[:, ri, :, :], in_=Av.unsqueeze(2).to_broadcast([128, NP, NCH])), reads=["pa", "Abc"], writes=["Abc"])
            p.mark('Apow_done')
            for q in range(4):
                for m2 in range(2):
                    c0 = q * 32 + m2 * 16
                    p.op("dve", lambda e, q=q, m2=m2, c0=c0: e.tensor_scalar(out=cW[:, q:NP:4, 0, c0:c0 + 16], in0=cP[:, 0, q:NP:4, :], scalar1=mk[:, m2:m2 + 1], scalar2=None, op0=ALU.mult),
                         reads=["cP", "mk", "cW"], writes=["cW"])
                    p.op("dve", lambda e, q=q, m2=m2, c0=c0: e.tensor_scalar(out=cW[:, q:NP:4, 1, c0:c0 + 16], in0=cP[:, 1, q:NP:4, :], scalar1=mk[:, m2:m2 + 1], scalar2=-1.0, op0=ALU.mult, op1=ALU.mult),
                         reads=["cP", "mk", "cW"], writes=["cW"])
            p.mark('cW_done')
            SC_ = [pc[:, i, :, :] for i in range(12)]
            _, _, FrC, FiC = ssm_prep(SC_, lamC[:, 0, :, :], lamC[:, 1, :, :], lamC[:, 2, :, :], ["pc", "lamC"])
            cmul("dve", pc[:, 1, :, :], pc[:, 2, :, :], FrC, FiC, bC[:, 0, :, :], bC[:, 1, :, :], pc[:, 5, :, :], pc[:, 6, :, :], ["pc", "bC"])
            for q in range(4):
                for m2 in range(2):
                    for ri in range(2):
                        p.op("dve", lambda e, q=q, m2=m2, ri=ri: e.tensor_scalar(out=buW[:, q:NP:4, ri, m2 * 64:(m2 + 1) * 64], in0=pc[:, 1 + ri, :, :], scalar1=mk[:, 4 + q * 2 + m2:5 + q * 2 + m2], scalar2=None, op0=ALU.mult),
                             reads=["pc", "mk", "buW"], writes=["buW"])
            fence(["pc", "lamC", "bC"] + BUK)
            p.op("dve", lambda e: e.memset(kT, 0.0), reads=["kT"], writes=["kT"])
            p.mark('tables_done')

            def load_x(ti, from_input, bs=SA):
                t0 = ti * TT
                xT, P_ = bs.xT, bs.pre
                if from_input:
                    for blk in range(NB):
                        p.dma("sp", lambda e, blk=blk: e.dma_start(out=xtm, in_=x_d[t0 + blk * 128:t0 + (blk + 1) * 128, :]), "xtmld", reads=["xtm"], writes=["xtm"])
                        for k in range(8):
                            b = ps()
                            p.op("pe", lambda e, b=b, k=k: e.transpose(pss[b][:, 0:128], xtm[:, k * 128:(k + 1) * 128], ident[:]), reads=["xtm", "ident"], writes=[f"ps{b}"])
                            if k % 2:
                                p.op("dve", lambda e, b=b, blk=blk, k=k: e.tensor_copy(out=xT[:, k, blk * 128:(blk + 1) * 128], in_=pss[b][:, 0:128]), reads=[f"ps{b}"], writes=[f"{P_}xT{k}"])
                            else:
                                p.op("act", lambda e, b=b, blk=blk, k=k: e.activation(out=xT[:, k, blk * 128:(blk + 1) * 128], in_=pss[b][:, 0:128], func=AF.Identity), reads=[f"ps{b}"], writes=[f"{P_}xT{k}"])
                else:
                    p.dma("sp", lambda e: e.dma_start(out=xT[:], in_=xs_d[:, :, t0:t0 + TT]), "xld" + P_, reads=[f"xs{ti}"], writes=[P_ + k_ for k_ in XT])

            def store_x(ti, to_output, bs=SA):
                t0 = ti * TT
                xT, P_ = bs.xT, bs.pre
                if to_output:
                    for blk in range(NB):
                        for k in range(8):
                            b = ps()
                            p.op("pe", lambda e, b=b, blk=blk, k=k: e.transpose(pss[b][:, 0:128], xT[:, k, blk * 128:(blk + 1) * 128], ident[:]), reads=[f"{P_}xT{k}", "ident"], writes=[f"ps{b}"])
                            p.op("dve", lambda e, b=b, k=k: e.tensor_copy(out=xtm[:, k * 128:(k + 1) * 128], in_=pss[b][:, 0:128]), reads=[f"ps{b}", "xtm"], writes=["xtm"])
                        p.dma("sp", lambda e, blk=blk: e.dma_start(out=y_d[t0 + blk * 128:t0 + (blk + 1) * 128, :], in_=xtm), "yst", reads=["xtm"])
                else:
                    p.dma("sp", lambda e: e.dma_start(out=xs_d[:, :, t0:t0 + TT], in_=xT[:]), "xst" + P_, reads=[P_ + k_ for k_ in XT], writes=[f"xs{ti}"])

            def norm_in(o_gs, o_sh, bs=SA):
                xT, sq, tmpf, hb, rstd, P_ = bs.xT, bs.sq, bs.tmpf, bs.hb, bs.rstd, bs.pre
                for k in range(8):
                    p.op("act", lambda e, k=k: e.activation(out=sq[:, k, :], in_=xT[:, k, :], func=AF.Square), reads=[f"{P_}xT{k}"], writes=[f"{P_}sq{k}"])
                stats_rstd([f"{P_}sq{k}" for k in range(8)], [sq[:, k, :] for k in range(8)], 1.0 / D, rstd, P_ + "rstd")
                for k in range(8):
                    p.op("dve", lambda e, k=k: e.scalar_tensor_tensor(out=tmpf[:, k, :], in0=xT[:, k, :], scalar=coefs[:, o_gs, k:k + 1], in1=rstd, op0=ALU.mult, op1=ALU.mult),
                         reads=[f"{P_}xT{k}", "coefs", P_ + "rstd"], writes=[f"{P_}tmpf{k}"])
                    p.op("act", lambda e, k=k: e.activation(out=hb[:, k, :], in_=tmpf[:, k, :], func=AF.Identity, bias=coefs[:, o_sh, k:k + 1]), reads=[f"{P_}tmpf{k}", "coefs"], writes=[f"{P_}hb{k}"])

            def resid_out(o_coef, src_mm, bs=SA):
                xT, sq, tmpf, hb, rstd, P_ = bs.xT, bs.sq, bs.tmpf, bs.hb, bs.rstd, bs.pre
                for m in range(8):
                    b = ps()
                    src_mm(m, b)
                    p.op("dve", lambda e, b=b, m=m: e.tensor_copy(out=tmpf[:, m, :], in_=pss[b][:, 0:TT]), reads=[f"ps{b}"], writes=[f"{P_}tmpf{m}"])
                    p.op("act", lambda e, m=m: e.activation(out=sq[:, m, :], in_=tmpf[:, m, :], func=AF.Square), reads=[f"{P_}tmpf{m}"], writes=[f"{P_}sq{m}"])
                stats_rstd([f"{P_}sq{k}" for k in range(8)], [sq[:, k, :] for k in range(8)], 1.0 / D, rstd, P_ + "rstd")
                for m in range(8):
                    p.op("dve", lambda e, m=m: e.tensor_tensor(out=tmpf[:, m, :], in0=tmpf[:, m, :], in1=rstd, op=ALU.mult), reads=[f"{P_}tmpf{m}", P_ + "rstd"], writes=[f"{P_}tmpf{m}"])
                    p.op("dve", lambda e, m=m: e.scalar_tensor_tensor(out=xT[:, m, :], in0=tmpf[:, m, :], scalar=coefs[:, o_coef, m:m + 1], in1=xT[:, m, :], op0=ALU.mult, op1=ALU.add),
                         reads=[f"{P_}tmpf{m}", "coefs", f"{P_}xT{m}"], writes=[f"{P_}xT{m}"])

            for ti in range(NT):
                load_x(ti, first)
                p.mark('x_loaded')
                norm_in(0, 1)
                p.mark('norm_done')
                for m in range(6):
                    b = ps()
                    for k in range(8):
                        p.op("pe", lambda e, b=b, k=k, m=m: e.matmul(pss[b][:, 0:TT], lhsT=W_IN[:, k, m * 128:(m + 1) * 128], rhs=hb[:, k, :], start=(k == 0), stop=(k == 7)), reads=["wbuf", f"hb{k}"], writes=[f"ps{b}"])
                    if m < 4:
                        p.op("act", lambda e, b=b, m=m: e.activation(out=qT[:, m, :], in_=pss[b][:, 0:TT], func=AF.Identity, scale=0.125), reads=[f"ps{b}"], writes=[f"qT{m}"])
                    else:
                        p.op("dve", lambda e, b=b, m=m: e.tensor_copy(out=kT[0:64, m - 4, 0, 128:128 + TT], in_=pss[b][0:64, 0:TT]), reads=[f"ps{b}"], writes=["kT"])
                        p.op("dve", lambda e, b=b, m=m: e.tensor_copy(out=kT[64:128, m - 4, 1, 128:128 + TT], in_=pss[b][64:128, 0:TT]), reads=[f"ps{b}"], writes=["kT"])
                for s in range(4):
                    b = ps()
                    for k in range(8):
                        p.op("pe", lambda e, b=b, k=k, s=s: e.matmul(pss[b][:, 0:TT], lhsT=W_IN[:, k, 896 + s * 128:896 + (s + 1) * 128], rhs=hb[:, k, :], start=(k == 0), stop=(k == 7)), reads=["wbuf", f"hb{k}"], writes=[f"ps{b}"])
                    if s % 2:
                        p.op("act", lambda e, b=b, s=s: e.activation(out=uT[:, s, :], in_=pss[b][:, 0:TT], func=AF.Identity), reads=[f"ps{b}"], writes=[f"uT{s}"])
                    else:
                        p.op("dve", lambda e, b=b, s=s: e.tensor_copy(out=uT[:, s, :], in_=pss[b][:, 0:TT]), reads=[f"ps{b}"], writes=[f"uT{s}"])
                for blk in range(NB):
                    b = ps()
                    for k in range(8):
                        p.op("pe", lambda e, b=b, k=k, blk=blk: e.matmul(pss[b][:, 0:128], lhsT=hb[:, k, blk * 128:(blk + 1) * 128], rhs=W_IN[:, k, 768:896], start=(k == 0), stop=(k == 7)), reads=["wbuf", f"hb{k}"], writes=[f"ps{b}"])
                    for g2 in range(2):
                        p.op("dve", lambda e, b=b, blk=blk, g2=g2: e.tensor_copy(out=vpad[:, blk + 1, g2, 64:128], in_=pss[b][:, g2 * 64:(g2 + 1) * 64]), reads=[f"ps{b}"], writes=[f"vp{blk + 1}"])
                for pr in range(NP):
                    for ri in range(2):
                        b = ps()
                        p.op("pe", lambda e, b=b, pr=pr, ri=ri: e.matmul(pss[b][:, 0:TT], lhsT=buW[:, pr, ri, :], rhs=uT[:, pr // 4, :], start=True, stop=True), reads=["buW", f"uT{pr // 4}"], writes=[f"ps{b}"])
                        if ri:
                            p.op("act", lambda e, b=b, pr=pr, ri=ri: e.activation(out=bu[:, ri, pr, :], in_=pss[b][:, 0:TT], func=AF.Identity), reads=[f"ps{b}"], writes=[bukey(pr)])
                        else:
                            p.op("dve", lambda e, b=b, pr=pr, ri=ri: e.tensor_copy(out=bu[:, ri, pr, :], in_=pss[b][:, 0:TT]), reads=[f"ps{b}"], writes=[bukey(pr)])
                p.mark('bu_done')
                _i0 = len(p.ops)
                p.mark('inproj_done')
                for blk in range(NB):
                    gb = ti * NB + blk
                    kts = (1,) if gb == 0 else (0, 1)
                    for g2 in range(2):
                        for kt in kts:
                            b = ps()
                            for hh in range(4):
                                h = 4 * g2 + hh
                                qt, half = h // 2, h % 2
                                p.op("pe", lambda e, b=b, hh=hh, qt=qt, half=half, kt=kt, blk=blk, g2=g2: e.matmul(
                                    pss[b][:, hh * 128:(hh + 1) * 128], lhsT=kT[:, g2, half, (blk + kt) * 128:(blk + kt + 1) * 128],
                                    rhs=qT[:, qt, blk * 128:(blk + 1) * 128], start=True, stop=True), reads=["kT", f"qT{qt}"], writes=[f"ps{b}"])
                            p.op("dve", lambda e, b=b, g2=g2, kt=kt: e.tensor_tensor(out=sc_f.rearrange("p (h q) -> p h q", h=4), in0=pss[b][:, :].rearrange("p (h q) -> p h q", h=4),
                                 in1=maskb[:, 4 * g2:4 * g2 + 4, kt, :], op=ALU.add), reads=[f"ps{b}", "maskb"], writes=["sc_f"])
                            p.op("act", lambda e, kt=kt: e.activation(out=pexp[:, kt, :], in_=sc_f, func=AF.Exp), reads=["sc_f"], writes=[f"pexp{kt}"])
                        for j in range(2):
                            qt = 2 * g2 + j
                            bo, bd = ps(), ps()
                            n = len(kts) * 2
                            i = 0
                            for kt in kts:
                                for half in range(2):
                                    hh = 2 * j + half
                                    lo = 64 if half == 0 else 0
                                    p.op("pe", lambda e, bo=bo, kt=kt, hh=hh, lo=lo, i=i, n=n, blk=blk, g2=g2: e.matmul(pss[bo][:, 0:128], lhsT=vpad[:, blk + kt, g2, lo:lo + 128], rhs=pexp[:, kt, hh * 128:(hh + 1) * 128], start=(i == 0), stop=(i == n - 1)),
                                         reads=[f"vp{blk + kt}", f"pexp{kt}"], writes=[f"ps{bo}"])
                                    p.op("pe", lambda e, bd=bd, kt=kt, hh=hh, lo=lo, i=i, n=n: e.matmul(pss[bd][:, 0:128], lhsT=ones_pad[:, lo:lo + 128], rhs=pexp[:, kt, hh * 128:(hh + 1) * 128], start=(i == 0), stop=(i == n - 1)),
                                         reads=["ones_pad", f"pexp{kt}"], writes=[f"ps{bd}"])
                                    i += 1
                            p.op("dve", lambda e, bd=bd, qt=qt, l=l: e.tensor_scalar(out=rden, in0=pss[bd][:, 0:128], scalar1=esink[:, l, qt:qt + 1], scalar2=None, op0=ALU.add), reads=[f"ps{bd}", "esink"], writes=["rden"])
                            p.op("dve", lambda e: e.reciprocal(out=rden, in_=rden), reads=["rden"], writes=["rden"])
                            p.op("dve", lambda e, bo=bo, qt=qt, blk=blk: e.tensor_tensor(out=attn[:, qt, blk * 128:(blk + 1) * 128], in0=pss[bo][:, 0:128], in1=rden, op=ALU.mult), reads=[f"ps{bo}", "rden"], writes=[f"attn{qt}"])
                p.mark('attn_done')
                p.op("dve", lambda e: e.tensor_copy(out=kT[:, :, :, 0:128], in_=kT[:, :, :, TT:TT + 128]), reads=["kT"], writes=["kT"])
                p.op("dve", lambda e: e.tensor_copy(out=vpad[:, 0, :, 64:128], in_=vpad[:, NB, :, 64:128]), reads=[f"vp{NB}"], writes=["vp0"])
                for k in range(4):
                    p.op("act", lambda e, k=k: e.activation(out=sq[:, k, :], in_=attn[:, k, :], func=AF.Square), reads=[f"attn{k}"], writes=[f"sq{k}"])
                stats_rstd([f"sq{k}" for k in range(4)], [sq[:, k, :] for k in range(4)], 1.0 / 512)
                for k in range(4):
                    p.op("dve", lambda e, k=k, l=l: e.scalar_tensor_tensor(out=heads[:, k, :], in0=attn[:, k, :], scalar=vec[:, l, 32 + k:33 + k], in1=rstd[:], op0=ALU.mult, op1=ALU.mult), reads=[f"attn{k}", "vec", "rstd"], writes=[f"heads{k}"])
                p.mark('heads_done')
                _s1 = p.ops[_i0:]; del p.ops[_i0:]
                for (eng, pa_, pb_) in SCAN_SPLIT:
                    nb_ = pb_ - pa_
                    kb = f"bus{pa_}"
                    m1 = t1.rearrange("p a (i j) -> p i a j", i=4)[:, 0:2, pa_:pb_, :]
                    m2 = t2.rearrange("p a (i j) -> p i a j", i=4)[:, 0:2, pa_:pb_, :]
                    n1 = t1.rearrange("p a (i j) -> p i a j", i=4)[:, 2:4, pa_:pb_, :]
                    n2 = t2.rearrange("p a (i j) -> p i a j", i=4)[:, 2:4, pa_:pb_, :]
                    km = f"scr{pa_}"
                    zv = lambda r, pa_=pa_, pb_=pb_: bu[:, :, pa_:pb_, :].rearrange("p i a (j r) -> p i a j r", r=LC)[:, :, :, :, r]
                    AR2 = Abc[:, 0, pa_:pb_, :].unsqueeze(1).to_broadcast([128, 2, nb_, NCH])
                    AI2 = Abc[:, 1, pa_:pb_, :].unsqueeze(1).to_broadcast([128, 2, nb_, NCH])
                    for r in range(1, LC):
                        p.op(eng, lambda e, r=r, zv=zv, m1=m1, AR2=AR2: e.tensor_tensor(out=m1, in0=AR2, in1=zv(r - 1), op=ALU.mult), reads=["Abc", kb], writes=[km + "a"])
                        p.op(eng, lambda e, r=r, zv=zv, m2=m2, AI2=AI2: e.tensor_tensor(out=m2, in0=AI2, in1=zv(r - 1), op=ALU.mult), reads=["Abc", kb], writes=[km + "b"])
                        p.op(eng, lambda e, r=r, zv=zv, m1=m1: e.tensor_tensor(out=zv(r), in0=zv(r), in1=m1, op=ALU.add), reads=[km + "a", kb], writes=[kb])
                        p.op(eng, lambda e, r=r, zv=zv, m2=m2: e.tensor_tensor(out=zv(r)[:, 0], in0=zv(r)[:, 0], in1=m2[:, 1], op=ALU.subtract), reads=[km + "b", kb], writes=[kb])
                        p.op(eng, lambda e, r=r, zv=zv, m2=m2: e.tensor_tensor(out=zv(r)[:, 1], in0=zv(r)[:, 1], in1=m2[:, 0], op=ALU.add), reads=[km + "b", kb], writes=[kb])
                    PR2 = Apow[:, 0, pa_:pb_, :].unsqueeze(1).to_broadcast([128, 2, nb_, LC])
                    PI2 = Apow[:, 1, pa_:pb_, :].unsqueeze(1).to_broadcast([128, 2, nb_, LC])
                    Hb = Hc[:, :, pa_:pb_].unsqueeze(3).to_broadcast([128, 2, nb_, LC])
                    kh = f"Hc{pa_}"
                    for j in range(NCH):
                        zj = bu[:, :, pa_:pb_, j * LC:(j + 1) * LC]
                        p.op(eng, lambda e, n1=n1, PR2=PR2, Hb=Hb: e.tensor_tensor(out=n1, in0=PR2, in1=Hb, op=ALU.mult), reads=["Apow", kh], writes=[km + "c"])
                        p.op(eng, lambda e, n2=n2, PI2=PI2, Hb=Hb: e.tensor_tensor(out=n2, in0=PI2, in1=Hb, op=ALU.mult), reads=["Apow", kh], writes=[km + "d"])
                        p.op(eng, lambda e, zj=zj, n1=n1: e.tensor_tensor(out=zj, in0=zj, in1=n1, op=ALU.add), reads=[km + "c", kb], writes=[kb])
                        p.op(eng, lambda e, zj=zj, n2=n2: e.tensor_tensor(out=zj[:, 0], in0=zj[:, 0], in1=n2[:, 1], op=ALU.subtract), reads=[km + "d", kb], writes=[kb])
                        p.op(eng, lambda e, zj=zj, n2=n2: e.tensor_tensor(out=zj[:, 1], in0=zj[:, 1], in1=n2[:, 0], op=ALU.add), reads=[km + "d", kb], writes=[kb])
                        p.op(eng, lambda e, j=j, l=l, pa_=pa_, pb_=pb_: e.tensor_copy(out=Hc[:, :, pa_:pb_], in_=bu[:, :, pa_:pb_, j * LC + LC - 1]), reads=[kb], writes=[kh])
                _s2 = p.ops[_i0:]; del p.ops[_i0:]
                p.ops.extend(_interleave(_s1, _s2))
                p.mark('scan_done')
                for hf in range(2):
                    for ri in range(2):
                        p.op("act", lambda e, hf=hf, ri=ri: e.activation(out=hbf[:, ri], in_=bu[:, ri, hf * 8:(hf + 1) * 8, :], func=AF.Identity), reads=BUK, writes=["hbf"])
                    for s in (2 * hf, 2 * hf + 1):
                        b = ps()
                        i = 0
                        for q in range(4):
                            pr = 4 * s + q
                            for ri in range(2):
                                p.op("pe", lambda e, b=b, pr=pr, ri=ri, i=i, hf=hf: e.matmul(pss[b][:, 0:TT], lhsT=cW[:, pr, ri, :], rhs=hbf[:, ri, pr - 8 * hf, :], start=(i == 0), stop=(i == 7)), reads=["cW", "hbf"], writes=[f"ps{b}"])
                                i += 1
                        p.op("dve", lambda e, b=b, s=s, l=l: e.scalar_tensor_tensor(out=yss[:, s, :], in0=uT[:, s, :], scalar=vec[:, l, 44 + s:45 + s], in1=pss[b][:, 0:TT], op0=ALU.mult, op1=ALU.add), reads=[f"ps{b}", f"uT{s}", "vec"], writes=["yss"])
                p.mark('y_done')
                p.op("pool", lambda e: e.tensor_tensor(out=ys2, in0=yss, in1=yss, op=ALU.mult), reads=["yss", "ys2"], writes=["ys2"])
                p.op("pool", lambda e: e.tensor_scalar(out=ys2, in0=ys2, scalar1=0.044715, scalar2=1.0, op0=ALU.mult, op1=ALU.add), reads=["ys2"], writes=["ys2"])
                p.op("pool", lambda e: e.tensor_tensor(out=ys2, in0=ys2, in1=yss, op=ALU.mult), reads=["yss", "ys2"], writes=["ys2"])
                p.op("act", lambda e: e.activation(out=ys2, in_=ys2, func=AF.Sigmoid, scale=1.5957691216057308), reads=["ys2"], writes=["ys2"])
                p.op("dve", lambda e: e.tensor_tensor(out=yss, in0=yss, in1=ys2, op=ALU.mult), reads=["yss", "ys2"], writes=["yss"])
                p.op("act", lambda e: e.activation(out=zb, in_=yss, func=AF.Identity), reads=["yss", "zb"], writes=["zb"])
                for mo in range(4):
                    b = ps()
                    for k in range(4):
                        p.op("pe", lambda e, b=b, k=k, mo=mo: e.matmul(pss[b][:, 0:TT], lhsT=W_GLU[:, k, mo * 128:(mo + 1) * 128], rhs=zb[:, k, :], start=(k == 0), stop=(k == 3)), reads=["wbuf", "zb"], writes=[f"ps{b}"])
                    p.op("act", lambda e, b=b, mo=mo, l=l: e.activation(out=ys2[:, mo, :], in_=pss[b][:, 0:TT], func=AF.Sigmoid, bias=vec[:, l, 48 + mo:49 + mo]), reads=[f"ps{b}", "vec"], writes=["ys2"])
                p.op("dve", lambda e: e.tensor_tensor(out=ys2, in0=yss, in1=ys2, op=ALU.mult), reads=["yss", "ys2"], writes=["ys2"])
                p.op("act", lambda e: e.activation(out=zb, in_=ys2, func=AF.Square), reads=["ys2", "zb"], writes=["zb"])
                stats_rstd(["zb"] * 4, [zb[:, k, :] for k in range(4)], 1.0 / 512)
                for k in range(4):
                    p.op("dve", lambda e, k=k, l=l: e.scalar_tensor_tensor(out=sheads[:, k, :], in0=ys2[:, k, :], scalar=vec[:, l, 40 + k:41 + k], in1=rstd[:], op0=ALU.mult, op1=ALU.mult), reads=["ys2", "vec", "rstd"], writes=[f"sheads{k}"])

                p.mark('ssm_done')
                def mm_out(m, b):
                    for k in range(8):
                        src, key = (heads, f"heads{k}") if k < 4 else (sheads, f"sheads{k - 4}")
                        p.op("pe", lambda e, k=k, src=src: e.matmul(pss[b][:, 0:TT], lhsT=W_OUT[:, k, m * 128:(m + 1) * 128], rhs=src[:, k % 4, :], start=(k == 0), stop=(k == 7)), reads=["wbuf", key], writes=[f"ps{b}"])
                resid_out(2, mm_out)
                store_x(ti, False)

            p.mark('mixer_done')
            fence(TAILK)
            p.dma("pool", lambda e, l=l: e.dma_start(out=W_M1, in_=wm1_d[l].rearrange("(k p) n -> p k n", p=128)), "wld", reads=["wbuf"], writes=["wbuf"])
            p.dma("pool", lambda e, l=l: e.dma_start(out=W_M2, in_=wm2_d[l].rearrange("(k p) n -> p k n", p=128)), "wld", reads=["wbuf"], writes=["wbuf"])
            sets = [SA, SB]
            load_x(0, False, sets[0])
            norm_in(3, 4, sets[0])
            for ti in range(NT):
                bs = sets[ti % 2]
                for f in range(32):
                    b = ps()
                    for k in range(8):
                        p.op("pe", lambda e, b=b, k=k, f=f, bs=bs: e.matmul(pss[b][:, 0:TT], lhsT=W_M1[:, k, f * 128:(f + 1) * 128], rhs=bs.hb[:, k, :], start=(k == 0), stop=(k == 7)), reads=["wbuf", f"{bs.pre}hb{k}"], writes=[f"ps{b}"])
                    p.op("act", lambda e, b=b, f=f: e.activation(out=hid[:, f, :], in_=pss[b][:, 0:TT], func=AF.Relu), reads=[f"ps{b}"], writes=[f"hid{f}"])
                    p.op("pool" if f % 2 else "dve", lambda e, f=f: e.tensor_tensor(out=hid[:, f, :], in0=hid[:, f, :], in1=hid[:, f, :], op=ALU.mult), reads=[f"hid{f}"], writes=[f"hid{f}"])
                if ti + 1 < NT:
                    load_x(ti + 1, False, sets[(ti + 1) % 2])
                    norm_in(3, 4, sets[(ti + 1) % 2])

                def mm_mlp(m, b):
                    for f in range(32):
                        p.op("pe", lambda e, f=f: e.matmul(pss[b][:, 0:TT], lhsT=W_M2[:, f, m * 128:(m + 1) * 128], rhs=hid[:, f, :], start=(f == 0), stop=(f == 31)), reads=["wbuf", f"hid{f}"], writes=[f"ps{b}"])
                resid_out(5, mm_mlp, bs)
                store_x(ti, last, bs)
        print("n_ops", len(p.ops), p.marks, flush=True)
        import os
        if os.environ.get("KTRUNC"):
            p.ops = p.ops[:int(os.environ["KTRUNC"])]
        if os.environ.get("KDBG"):
            dbg_list = {"modcol": modcol[:], "coefs": coefs[:], "xT": xT[:], "hb": hb[:], "qT": qT, "kT": kT, "uT": uT, "vpad": vpad, "attn": attn,
                        "heads": heads, "bu": bu, "yss": yss, "ys2": ys2, "sheads": sheads, "Apow": Apow, "buW": buW, "cW": cW, "Hc": Hc, "rstd": rstd[:], "tmpf": tmpf[:], "pexp": pexp, "hid": hid}
            for nm in os.environ["KDBG"].split(","):
                ap_ = dbg_list[nm]
                shp = list(ap_.shape)
                dd = nc.dram_tensor("dbg_" + nm, shp, F32, kind="ExternalOutput").ap()
                p.dma("pool", lambda e, dd=dd, ap_=ap_: e.dma_start(out=dd, in_=ap_), "dbg_" + nm, reads=TAILK + XT + HIDK + ["modcol", "coefs", "buW", "cW", "rstd"] + [f"hb{k}" for k in range(8)] + [f"tmpf{k}" for k in range(8)] + [f"vp{i}" for i in range(NB + 1)])
        fin = ["yst"] if not os.environ.get("KTRUNC") else []
        if os.environ.get("KDBG"):
            fin += ["dbg_" + nm for nm in os.environ["KDBG"].split(",")]
        p.emit(final_dma_tags=fin)
    return nc


def _host_prep(inputs, b, T, DEPTH):
    f = lambda a: np.ascontiguousarray(a, dtype=np.float32)
    L = DEPTH
    col = lambda v, n: v.reshape(n, 128).T
    m = {}
    m["x"] = f(inputs["x"][b, :T])
    m["ccol"] = f(col(inputs["c"][b], 8))
    m["w_ada"] = f(inputs["w_ada"][:L])
    m["b_ada"] = f(inputs["b_ada"][:L].reshape(L, 1, 6 * D))
    vecs = np.zeros((L, 128, NV), np.float32)
    for l in range(L):
        vecs[l, :, 0:8] = col(inputs["pre_mix_g"][l], 8)
        vecs[l, :, 8:16] = col(inputs["post_mix_g"][l], 8)
        vecs[l, :, 16:24] = col(inputs["pre_mlp_g"][l], 8)
        vecs[l, :, 24:32] = col(inputs["post_mlp_g"][l], 8)
        vecs[l, :, 32:36] = col(inputs["attn_out_g"][l], 4)
        vecs[l, :, 36:40] = np.repeat(inputs["attn_sinks"][l].reshape(4, 2), 64, axis=1).T
        vecs[l, :, 40:44] = col(inputs["ssm_out_g"][l], 4)
        vecs[l, :, 44:48] = col(inputs["d_skip"][l], 4)
        vecs[l, :, 48:52] = col(inputs["b_glu"][l], 4)
    m["vecs"] = vecs
    w = inputs["w_in"][:L]
    m["w_in"] = f(np.concatenate([w[:, :, 0:512], w[:, :, 512:576], w[:, :, 512:576], w[:, :, 576:640], w[:, :, 576:640], w[:, :, 640:768], w[:, :, 768:1280]], axis=2))
    for k in ("w_glu", "w_out", "w_mlp_in", "w_mlp_out"):
        m[k] = f(inputs[k][:L])
    lam = np.stack([inputs["lam_re"][:L], inputs["lam_im"][:L], np.broadcast_to(inputs["log_dt"][:L][:, :, None], (L, G, P))], axis=1)
    lam5 = lam.reshape(L, 3, NP, 2, P)
    m["lamP"] = f(lam5.transpose(0, 3, 4, 1, 2).reshape(L, 128, 3, NP))
    lam6 = lam.reshape(L, 3, 4, 4, 2, P)
    lamC = np.broadcast_to(lam6[:, :, :, :, :, None, :], (L, 3, 4, 4, 2, C, P))
    m["lamC"] = f(lamC.transpose(0, 3, 4, 5, 1, 2, 6).reshape(L, 128, 3, 4, P))
    bb = np.stack([inputs["b_re"][:L], inputs["b_im"][:L]], axis=1).reshape(L, 2, 4, 4, 2, P, C)
    m["bC"] = f(bb.transpose(0, 3, 4, 6, 1, 2, 5).reshape(L, 128, 2, 4, P))
    cc = np.stack([inputs["c_re"][:L], inputs["c_im"][:L]], axis=1).reshape(L, 2, NP, 2, C, P)
    m["cP"] = f(cc.transpose(0, 3, 5, 1, 2, 4).reshape(L, 128, 2, NP, C))
    slopes = 2.0 ** (-8.0 * np.arange(1, NH + 1) / NH)
    s_ = np.arange(128)[:, None]; q_ = np.arange(128)[None, :]
    mb = np.full((128, NH, 2, 128), -30000.0, np.float32)
    for h in range(NH):
        d0 = 128 + q_ - s_
        d1 = q_ - s_
        mb[:, h, 0, :] = np.where((d0 >= 0) & (d0 < 128), -slopes[h] * d0, -30000.0)
        mb[:, h, 1, :] = np.where((d1 >= 0) & (d1 < 128), -slopes[h] * d1, -30000.0)
    m["maskb"] = mb
    m["ident"] = np.eye(128, dtype=np.float32)
    mk = np.zeros((128, 12), np.float32); mk[:64, 0] = 1; mk[64:, 1] = 1; mk[:, 3] = 1e-6
    rows = np.arange(128)
    for q in range(4):
        for mm in range(2):
            mk[:, 4 + q * 2 + mm] = ((rows // 32 == q) & ((rows % 32) // 16 == mm)).astype(np.float32)
    m["masks"] = mk
    return m


def kernel(**inputs):
    inputs = {k: np.asarray(v) for k, v in inputs.items()}
    B, T, _ = inputs["x"].shape
    DEPTH = inputs["w_in"].shape[0]
    nc = build(T, DEPTH)
    in_maps = [_host_prep(inputs, b, T, DEPTH) for b in range(B)]
    res = run_bass_kernel_spmd(nc, in_maps, core_ids=list(range(B)))
    return np.stack([r["y"] for r in res.results], axis=0).astype(np.float32)
```

```python
import contextlib
import numpy as np
import concourse.bass as bass
import concourse.mybir as mybir
from concourse.bass_utils import run_bass_kernel_spmd

F32 = mybir.dt.float32
BF16 = mybir.dt.bfloat16
AF = mybir.ActivationFunctionType
ALU = mybir.AluOpType
ENGS = ("pe", "act", "dve", "pool", "sp")
import os as _os
NOSYNC_SAME = bool(_os.environ.get("KNOSYNC"))

D = 1024
NH = 8
G = 32
P = 64
C = 16
NP = 16
DFF = 4096
INW = 1408
LC = 16
TT = 256
NV = 52


class Prog:
    def __init__(self, nc, n_epochs=1):
        self.nc = nc
        self.ops = []
        self.epoch = 0
        self.n_epochs = n_epochs
        self.marks = []

    def op(self, eng, fn, reads=(), writes=(), dma=None):
        self.ops.append(dict(eng=eng, fn=fn, reads=tuple(reads), writes=tuple(writes),
                             dma=dma, epoch=self.epoch, waits=[], inc=False))

    def dma(self, q, fn, tag, reads=(), writes=()):
        self.op(q, fn, reads, writes, dma=tag)

    def mark(self, name):
        self.marks.append((name, len(self.ops)))

    def analyze(self):
        ops = self.ops
        last_w, readers, waited, dma_count = {}, {}, {}, {}
        for i, o in enumerate(ops):
            deps = {}
            for r in o["reads"]:
                if r in last_w:
                    deps[last_w[r]] = "raw"
            for w in o["writes"]:
                if w in last_w:
                    deps.setdefault(last_w[w], "waw")
                for rd in readers.get(w, ()):
                    if rd != i:
                        deps.setdefault(rd, "war")
            best = {}
            for d, kind in deps.items():
                od = ops[d]
                if od["dma"] is not None:
                    key = ("dma", od["dma"])
                    best[key] = max(best.get(key, 0), dma_count[od["dma"]])
                else:
                    if od["eng"] == o["eng"] and o["dma"] is None:
                        if od["eng"] == "pe" or kind in ("war", "waw") or NOSYNC_SAME:
                            continue
                    key = ("eng", od["eng"], od["epoch"])
                    best[key] = max(best.get(key, -1), d)
            for key, val in best.items():
                wk = (o["eng"], key)
                if waited.get(wk, -1) >= val:
                    continue
                waited[wk] = val
                if key[0] == "eng":
                    ops[val]["inc"] = True
                o["waits"].append((key, val))
            if o["dma"] is not None:
                dma_count[o["dma"]] = dma_count.get(o["dma"], 0) + 1
            for r in o["reads"]:
                readers.setdefault(r, []).append(i)
            for w in o["writes"]:
                last_w[w] = i
                readers[w] = []
        cnt = {}
        for o in ops:
            if o["dma"] is None and o["inc"]:
                k = (o["eng"], o["epoch"])
                cnt[k] = cnt.get(k, 0) + 1
                o["cnt"] = cnt[k]
        self.max_counts = cnt
        self.dma_tags = dma_count

    def emit(self, final_dma_tags=()):
        nc = self.nc
        self.analyze()
        ops = self.ops
        with contextlib.ExitStack() as st:
            sems = {}
            for (e, ep) in self.max_counts:
                sems[("eng", e, ep)] = st.enter_context(nc.semaphore(f"s_{e}_{ep}"))
            for t in self.dma_tags:
                sems[("dma", t)] = st.enter_context(nc.semaphore(f"d_{t}"))
            block = st.enter_context(nc.Block())
            engmap = {"pe": "tensor", "act": "scalar", "dve": "vector", "pool": "gpsimd", "sp": "sync"}

            def make(engname):
                def body(eng):
                    for o in ops:
                        if o["eng"] != engname:
                            continue
                        for key, val in o["waits"]:
                            if key[0] == "dma":
                                eng.wait_ge(sems[key], 16 * val)
                            else:
                                eng.wait_ge(sems[key], ops[val]["cnt"])
                        ins = o["fn"](eng)
                        if o["dma"] is not None:
                            ins.then_inc(sems[("dma", o["dma"])], 16)
                        elif o["inc"]:
                            ins.then_inc(sems[("eng", o["eng"], o["epoch"])], 1)
                    if engname == "sp":
                        for t in final_dma_tags:
                            eng.wait_ge(sems[("dma", t)], 16 * self.dma_tags[t])
                return body

            for e in ENGS:
                getattr(block, engmap[e])(make(e))


def _interleave(s1, s2):
    out, i, j = [], 0, 0
    n1, n2 = len(s1), len(s2)
    while i < n1 or j < n2:
        if j < n2 and (i >= n1 or i * n2 > j * n1):
            out.append(s2[j]); j += 1
        else:
            out.append(s1[i]); i += 1
    return out


def build(T, DEPTH):
    nc = bass.Bass("TRN2", target_bir_lowering=False)
    NT = T // TT
    NB = TT // 128
    NCH = TT // LC
    dr = lambda name, shape, dt=F32, kind="ExternalInput": nc.dram_tensor(name, shape, dt, kind=kind).ap()
    x_d = dr("x", [T, D])
    y_d = dr("y", [T, D], kind="ExternalOutput")
    xs_d = dr("xs", [128, 8, T], kind="Internal")
    ccol_d = dr("ccol", [128, 8])
    wada_d = dr("w_ada", [DEPTH, D, 6 * D])
    bada_d = dr("b_ada", [DEPTH, 1, 6 * D])
    vec_d = dr("vecs", [DEPTH, 128, NV])
    win_d = dr("w_in", [DEPTH, D, INW])
    wglu_d = dr("w_glu", [DEPTH, 512, 512])
    wout_d = dr("w_out", [DEPTH, D, D])
    wm1_d = dr("w_mlp_in", [DEPTH, D, DFF])
    wm2_d = dr("w_mlp_out", [DEPTH, DFF, D])
    lamP_d = dr("lamP", [DEPTH, 128, 3, NP])
    lamC_d = dr("lamC", [DEPTH, 128, 3, 4, P])
    bC_d = dr("bC", [DEPTH, 128, 2, 4, P])
    cP_d = dr("cP", [DEPTH, 128, 2, NP, C])
    maskb_d = dr("maskb", [128, NH, 2, 128])
    ident_d = dr("ident", [128, 128])
    mk_d = dr("masks", [128, 12])

    with contextlib.ExitStack() as st:
        sb = lambda name, shape, dt=F32: st.enter_context(nc.sbuf_tensor("s_" + name, shape, dt))
        p = Prog(nc, n_epochs=DEPTH + 1)
        ident = sb("ident", [128, 128])
        mk = sb("mk", [128, 12])
        ones_b = sb("ones_b", [128, 128], BF16)
        ones_pad = sb("ones_pad", [128, 192], BF16)
        ccol = sb("ccol", [128, 8])
        cact = sb("cact", [128, 8], BF16)
        modcol = sb("modcol", [128, DEPTH, 48])
        vec = sb("vec", [128, DEPTH, NV])
        esink = sb("esink", [128, DEPTH, 4])
        one1 = sb("one1", [1, 2], BF16)
        coefs = sb("coefs", [128, 6, 8])
        dummy = sb("dummy", [128, 2])
        rstd = sb("rstd", [128, TT])
        xT = sb("xT", [128, 8, TT])
        sq = sb("sq", [128, 8, TT], BF16)
        tmpf = sb("tmpf", [128, 8, TT])
        hb = sb("hb", [128, 8, TT], BF16)
        mbuf = sb("mbuf", [128, 14208], BF16)
        abuf = sb("abuf", [128, 10240], BF16)
        wbuf = sb("wbuf", [128, 65536], BF16)
        pss = [st.enter_context(nc.psum_tensor(f"ps{i}", [128, 512], F32)) for i in range(8)]
        psn = [0]

        def ps():
            psn[0] = (psn[0] + 1) % 8
            return psn[0]

        def carve(buf, off, shape, dt, rows=128):
            n = 1
            for d_ in shape:
                n *= d_
            nel = n * (2 if dt == F32 else 1)
            v = buf[0:rows, off:off + nel]
            if dt == F32:
                v = v.bitcast(F32)
            if len(shape) == 1:
                return v, off + nel
            names = " ".join(f"d{i}" for i in range(len(shape)))
            kw = {f"d{i}": shape[i] for i in range(len(shape) - 1)}
            return v.rearrange(f"p ({names}) -> p {names}", **kw), off + nel

        maskb, om = carve(mbuf, 0, [NH, 2, 128], F32)
        buW, om = carve(mbuf, om, [NP, 2, 128], BF16)
        cW, om = carve(mbuf, om, [NP, 2, 128], BF16)
        pa, om = carve(mbuf, om, [12, NP], F32)
        lamP, om = carve(mbuf, om, [3, NP], F32)
        vpad, om = carve(mbuf, om, [NB + 1, 2, 192], BF16)
        Hc, om = carve(mbuf, om, [2, NP], F32)
        assert om <= 14208, om
        xT_B, ob = carve(mbuf, 0, [8, TT], F32)
        sq_B, ob = carve(mbuf, ob, [8, TT], BF16)
        tmpf_B, ob = carve(mbuf, ob, [8, TT], F32)
        hb_B, ob = carve(mbuf, ob, [8, TT], BF16)
        rstd_B, ob = carve(mbuf, ob, [TT], F32)
        assert ob <= 14208, ob
        hid, o_ = carve(abuf, 0, [32, TT], BF16)
        xtm, o_ = carve(abuf, o_, [D], F32)
        assert o_ <= 10240
        modrow_b, _ = carve(abuf, 0, [6 * D], BF16, rows=1)
        W_IN, o_ = carve(wbuf, 0, [8, INW], BF16)
        W_OUT, o_ = carve(wbuf, o_, [8, D], BF16)
        W_GLU, o_ = carve(wbuf, o_, [4, 512], BF16)
        tail0 = o_
        bu, o_ = carve(wbuf, o_, [2, NP, TT], F32)
        hbf, o_ = carve(wbuf, o_, [2, NP // 2, TT], BF16)
        t1, o_ = carve(wbuf, o_, [NP, 64], F32)
        t2, o_ = carve(wbuf, o_, [NP, 64], F32)
        yss, o_ = carve(wbuf, o_, [4, TT], F32)
        ys2, o_ = carve(wbuf, o_, [4, TT], F32)
        attn, o_ = carve(wbuf, o_, [4, TT], F32)
        qT, o_ = carve(wbuf, o_, [4, TT], BF16)
        uT, o_ = carve(wbuf, o_, [4, TT], BF16)
        heads, o_ = carve(wbuf, o_, [4, TT], BF16)
        sheads, o_ = carve(wbuf, o_, [4, TT], BF16)
        zb, o_ = carve(wbuf, o_, [4, TT], BF16)
        sc_f, o_ = carve(wbuf, o_, [512], F32)
        pexp, o_ = carve(wbuf, o_, [2, 512], BF16)
        rden, o_ = carve(wbuf, o_, [128], F32)
        kT, o_ = carve(wbuf, o_, [2, 2, 128 + TT], BF16)
        Abc, o_ = carve(wbuf, o_, [2, NP, NCH], F32)
        Apow, o_ = carve(wbuf, o_, [2, NP, LC], F32)
        cP, o_ = carve(wbuf, o_, [2, NP, C], F32)
        assert o_ <= 65536, o_
        pc, o2 = carve(wbuf, tail0, [12, 4, P], F32)
        lamC, o2 = carve(wbuf, o2, [3, 4, P], F32)
        bC, o2 = carve(wbuf, o2, [2, 4, P], F32)
        assert o2 <= tail0 + 2 * 2 * NP * TT
        W_M1, o3 = carve(wbuf, 0, [8, DFF], BF16)
        W_M2, o3 = carve(wbuf, o3, [32, D], BF16)
        W_ADA, o3 = carve(wbuf, 0, [8, 6 * D], BF16)
        modrow, o3 = carve(wbuf, o3, [6 * D], F32, rows=1)
        assert o3 <= 65536

        XT = [f"xT{k}" for k in range(8)]
        HIDK = [f"hid{f}" for f in range(32)]
        FINEK = [f"qT{m}" for m in range(4)] + [f"uT{m}" for m in range(4)] + [f"attn{m}" for m in range(4)] + [f"heads{m}" for m in range(4)] + [f"sheads{m}" for m in range(4)] + ["pexp0", "pexp1"]
        _ks = int(_os.environ.get("KSPLIT", "12"))
        SCAN_SPLIT = [("dve", 0, _ks)] + ([("pool", _ks, 16)] if _ks < 16 else [])
        BUK = [f"bus{a_}{c_}" for (_, a_, _b) in SCAN_SPLIT for c_ in "ri"]
        bukey = lambda pr, ri: [f"bus{a_}" + "ri"[ri] for (_, a_, b_) in SCAN_SPLIT if a_ <= pr < b_][0]
        SCRK = [f"scr{a_}{c_}" for (_, a_, _b) in SCAN_SPLIT for c_ in "abcd"] + [f"Hc{a_}" for (_, a_, _b) in SCAN_SPLIT]
        MK = ["maskb", "buW", "cW", "pa", "lamP"] + [f"vp{i}" for i in range(NB + 1)]
        BK = [f"B{n_}{k}" for n_ in ("xT", "sq", "tmpf", "hb") for k in range(8)] + ["Brstd"]
        TAILK = MK + BK + BUK + SCRK + FINEK + ["hbf", "yss", "ys2", "attn", "qT", "uT",
                 "heads", "sheads", "zb", "sc_f", "pexp", "rden", "kT", "Abc", "Apow", "cP", "pc", "lamC", "bC", "modrow", "wbuf"]

        def fence(keys):
            p.op("dve", lambda e: e.memset(dummy[:, 0:1], 0.0), reads=list(keys), writes=list(keys))

        p.dma("sp", lambda e: e.dma_start(out=ident[:], in_=ident_d), "c0", writes=["ident"])
        p.dma("sp", lambda e: e.dma_start(out=mk[:], in_=mk_d), "c2", writes=["mk"])
        p.dma("sp", lambda e: e.dma_start(out=ccol[:], in_=ccol_d), "c3", writes=["ccol"])
        p.dma("sp", lambda e: e.dma_start(out=vec[:], in_=vec_d.rearrange("l p n -> p l n")), "c4", writes=["vec"])
        p.op("dve", lambda e: e.memset(ones_b[:], 1.0), writes=["ones_b"])
        p.op("dve", lambda e: e.memset(ones_pad[:], 0.0), writes=["ones_pad"])
        p.op("dve", lambda e: e.memset(ones_pad[:, 64:128], 1.0), reads=["ones_pad"], writes=["ones_pad"])
        p.op("dve", lambda e: e.memset(one1[:], 1.0), writes=["one1"])
        p.op("act", lambda e: e.activation(out=coefs[:, 0, :], in_=ccol[:], func=AF.Sigmoid), reads=["ccol"], writes=["coefs"])
        p.op("dve", lambda e: e.tensor_tensor(out=cact[:], in0=coefs[:, 0, :], in1=ccol[:], op=ALU.mult), reads=["coefs", "ccol"], writes=["cact"])
        for l in range(DEPTH):
            p.op("act", lambda e, l=l: e.activation(out=esink[:, l, :], in_=vec[:, l, 36:40], func=AF.Exp), reads=["vec"], writes=["esink"])
        for l in range(DEPTH):
            p.dma("pool", lambda e, l=l: e.dma_start(out=W_ADA, in_=wada_d[l].rearrange("(k p) n -> p k n", p=128)), "wld", reads=["wbuf"], writes=["wbuf"])
            p.dma("sp", lambda e, l=l: e.dma_start(out=modrow, in_=bada_d[l]), "brow", reads=["modrow"], writes=["modrow"])
            for cc in range(12):
                b = ps()
                for k in range(8):
                    p.op("pe", lambda e, b=b, k=k, cc=cc: e.matmul(pss[b][0:1, :], lhsT=cact[:, k:k + 1], rhs=W_ADA[:, k, cc * 512:(cc + 1) * 512], start=(k == 0), stop=(k == 7)),
                         reads=["cact", "wbuf"], writes=[f"ps{b}"])
                p.op("dve", lambda e, b=b, cc=cc: e.tensor_tensor(out=modrow[:, cc * 512:(cc + 1) * 512], in0=pss[b][0:1, :], in1=modrow[:, cc * 512:(cc + 1) * 512], op=ALU.add),
                     reads=[f"ps{b}", "modrow"], writes=["modrow"])
            b = ps()
            for part in range(2):
                p.op("act", lambda e: e.activation(out=modrow_b, in_=modrow, func=AF.Identity), reads=["modrow"], writes=HIDK)
                for j in range(48):
                    p.op("pe", lambda e, b=b, j=j, part=part: e.matmul(pss[b][:, j:j + 1], lhsT=modrow_b[0:1, j * 128:(j + 1) * 128], rhs=one1[0:1, 0:1], start=(part == 0 and j == 0), stop=(part == 1 and j == 47)),
                         reads=HIDK + ["one1"], writes=[f"ps{b}"])
                if part == 0:
                    p.op("dve", lambda e: e.tensor_tensor(out=modrow, in0=modrow, in1=modrow_b, op=ALU.subtract), reads=["modrow"] + HIDK, writes=["modrow"])
            p.op("dve", lambda e, b=b, l=l: e.tensor_copy(out=modcol[:, l, :], in_=pss[b][:, 0:48]), reads=[f"ps{b}"], writes=["modcol"])

        p.mark('setup_done')
        class BSet:
            def __init__(self, xT_, sq_, tmpf_, hb_, rstd_, pre):
                self.xT, self.sq, self.tmpf, self.hb, self.rstd, self.pre = xT_, sq_, tmpf_, hb_, rstd_, pre
        SA = BSet(xT[:], sq[:], tmpf[:], hb[:], rstd[:], "")
        SB = BSet(xT_B, sq_B, tmpf_B, hb_B, rstd_B, "B")

        def stats_rstd(src_keys, ktiles, inv_n, rs=None, rkey="rstd"):
            rstd = SA.rstd if rs is None else rs
            b = ps()
            n = len(ktiles)
            for i, kv in enumerate(ktiles):
                p.op("pe", lambda e, b=b, kv=kv, i=i, n=n: e.matmul(pss[b][:, 0:TT], lhsT=ones_b[:, :], rhs=kv, start=(i == 0), stop=(i == n - 1)),
                     reads=[src_keys[i], "ones_b"], writes=[f"ps{b}"])
            p.op("act", lambda e, b=b: e.activation(out=rstd[:], in_=pss[b][:, 0:TT], func=AF.Sqrt, bias=mk[:, 3:4], scale=inv_n), reads=[f"ps{b}", "mk"], writes=[rkey])
            p.op("dve", lambda e: e.reciprocal(out=rstd[:], in_=rstd[:]), reads=[rkey], writes=[rkey])

        def cmul(eng, o_r, o_i, a_r, a_i, b_r, b_i, s1, s2, keys):
            rd = list(keys)
            p.op(eng, lambda e: e.tensor_tensor(out=s1, in0=a_r, in1=b_r, op=ALU.mult), reads=rd, writes=rd)
            p.op(eng, lambda e: e.tensor_tensor(out=s2, in0=a_i, in1=b_i, op=ALU.mult), reads=rd, writes=rd)
            p.op(eng, lambda e: e.tensor_tensor(out=o_r, in0=s1, in1=s2, op=ALU.subtract), reads=rd, writes=rd)
            p.op(eng, lambda e: e.tensor_tensor(out=s1, in0=a_r, in1=b_i, op=ALU.mult), reads=rd, writes=rd)
            p.op(eng, lambda e: e.tensor_tensor(out=s2, in0=a_i, in1=b_r, op=ALU.mult), reads=rd, writes=rd)
            p.op(eng, lambda e: e.tensor_tensor(out=o_i, in0=s1, in1=s2, op=ALU.add), reads=rd, writes=rd)

        def ssm_prep(S, lr, li, ldt, keys):
            k = list(keys)
            dt, a, th, s_, c_, x1, x2, x3, x4 = S[0], S[1], S[2], S[3], S[4], S[5], S[6], S[7], S[8]
            p.op("act", lambda e: e.activation(out=dt, in_=ldt, func=AF.Exp), reads=k, writes=k)
            p.op("dve", lambda e: e.tensor_tensor(out=a, in0=lr, in1=dt, op=ALU.mult), reads=k, writes=k)
            p.op("act", lambda e: e.activation(out=a, in_=a, func=AF.Exp), reads=k, writes=k)
            p.op("dve", lambda e: e.tensor_tensor(out=th, in0=li, in1=dt, op=ALU.mult), reads=k, writes=k)
            p.op("act", lambda e: e.activation(out=s_, in_=th, func=AF.Sin, scale=1.0 / 16), reads=k, writes=k)
            p.op("act", lambda e: e.activation(out=c_, in_=th, func=AF.Sin, scale=1.0 / 32), reads=k, writes=k)
            p.op("dve", lambda e: e.tensor_tensor(out=c_, in0=c_, in1=c_, op=ALU.mult), reads=k, writes=k)
            p.op("dve", lambda e: e.tensor_scalar(out=c_, in0=c_, scalar1=-2.0, scalar2=1.0, op0=ALU.mult, op1=ALU.add), reads=k, writes=k)
            for _ in range(4):
                p.op("dve", lambda e: e.tensor_tensor(out=x1, in0=c_, in1=c_, op=ALU.mult), reads=k, writes=k)
                p.op("dve", lambda e: e.tensor_tensor(out=x2, in0=s_, in1=s_, op=ALU.mult), reads=k, writes=k)
                p.op("dve", lambda e: e.tensor_tensor(out=x3, in0=c_, in1=s_, op=ALU.mult), reads=k, writes=k)
                p.op("dve", lambda e: e.tensor_tensor(out=c_, in0=x1, in1=x2, op=ALU.subtract), reads=k, writes=k)
                p.op("dve", lambda e: e.tensor_scalar(out=s_, in0=x3, scalar1=2.0, scalar2=None, op0=ALU.mult), reads=k, writes=k)
            Ar, Ai = S[9], S[10]
            p.op("dve", lambda e: e.tensor_tensor(out=Ar, in0=c_, in1=a, op=ALU.mult), reads=k, writes=k)
            p.op("dve", lambda e: e.tensor_tensor(out=Ai, in0=s_, in1=a, op=ALU.mult), reads=k, writes=k)
            p.op("dve", lambda e: e.tensor_scalar(out=x1, in0=Ar, scalar1=-1.0, scalar2=None, op0=ALU.add), reads=k, writes=k)
            p.op("dve", lambda e: e.tensor_tensor(out=x2, in0=lr, in1=lr, op=ALU.mult), reads=k, writes=k)
            p.op("dve", lambda e: e.tensor_tensor(out=x3, in0=li, in1=li, op=ALU.mult), reads=k, writes=k)
            p.op("dve", lambda e: e.tensor_tensor(out=x2, in0=x2, in1=x3, op=ALU.add), reads=k, writes=k)
            p.op("dve", lambda e: e.reciprocal(out=x2, in_=x2), reads=k, writes=k)
            Fr, Fi = S[11], S[0]
            p.op("dve", lambda e: e.tensor_tensor(out=x3, in0=x1, in1=lr, op=ALU.mult), reads=k, writes=k)
            p.op("dve", lambda e: e.tensor_tensor(out=x4, in0=Ai, in1=li, op=ALU.mult), reads=k, writes=k)
            p.op("dve", lambda e: e.tensor_tensor(out=x3, in0=x3, in1=x4, op=ALU.add), reads=k, writes=k)
            p.op("dve", lambda e: e.tensor_tensor(out=Fr, in0=x3, in1=x2, op=ALU.mult), reads=k, writes=k)
            p.op("dve", lambda e: e.tensor_tensor(out=x3, in0=Ai, in1=lr, op=ALU.mult), reads=k, writes=k)
            p.op("dve", lambda e: e.tensor_tensor(out=x4, in0=x1, in1=li, op=ALU.mult), reads=k, writes=k)
            p.op("dve", lambda e: e.tensor_tensor(out=x3, in0=x3, in1=x4, op=ALU.subtract), reads=k, writes=k)
            p.op("dve", lambda e: e.tensor_tensor(out=Fi, in0=x3, in1=x2, op=ALU.mult), reads=k, writes=k)
            return Ar, Ai, Fr, Fi

        for l in range(DEPTH):
            p.epoch = l + 1
            first, last = (l == 0), (l == DEPTH - 1)
            for (o, gcol, sccol) in ((0, 0, 8), (3, 16, 32)):
                p.op("dve", lambda e, o=o, gcol=gcol, sccol=sccol, l=l: e.scalar_tensor_tensor(out=coefs[:, o, :], in0=modcol[:, l, sccol:sccol + 8], scalar=1.0, in1=vec[:, l, gcol:gcol + 8], op0=ALU.add, op1=ALU.mult),
                     reads=["modcol", "vec", "coefs"], writes=["coefs"])
            for (o, shcol) in ((1, 0), (4, 24)):
                p.op("dve", lambda e, o=o, shcol=shcol, l=l: e.tensor_copy(out=coefs[:, o, :], in_=modcol[:, l, shcol:shcol + 8]), reads=["modcol", "coefs"], writes=["coefs"])
            for (o, gcol, pcol) in ((2, 16, 8), (5, 40, 24)):
                p.op("dve", lambda e, o=o, gcol=gcol, pcol=pcol, l=l: e.tensor_tensor(out=coefs[:, o, :], in0=modcol[:, l, gcol:gcol + 8], in1=vec[:, l, pcol:pcol + 8], op=ALU.mult),
                     reads=["modcol", "vec", "coefs"], writes=["coefs"])

            p.mark('coefs_done')
            fence(TAILK)
            p.dma("sp", lambda e: e.dma_start(out=maskb, in_=maskb_d), "c1", reads=["maskb"], writes=["maskb"])
            p.op("dve", lambda e: e.memset(vpad, 0.0), writes=[f"vp{i}" for i in range(NB + 1)])
            p.op("dve", lambda e: e.memset(Hc, 0.0), writes=[f"Hc{a_}" for (_, a_, _b) in SCAN_SPLIT])
            p.op("dve", lambda e: e.memset(cW, 0.0), writes=["cW"])
            p.dma("pool", lambda e, l=l: e.dma_start(out=W_IN, in_=win_d[l].rearrange("(k p) n -> p k n", p=128)), "wld", reads=["wbuf"], writes=["wbuf"])
            p.dma("pool", lambda e, l=l: e.dma_start(out=W_OUT, in_=wout_d[l].rearrange("(k p) n -> p k n", p=128)), "wld", reads=["wbuf"], writes=["wbuf"])
            p.dma("pool", lambda e, l=l: e.dma_start(out=W_GLU, in_=wglu_d[l].rearrange("(k p) n -> p k n", p=128)), "wld", reads=["wbuf"], writes=["wbuf"])
            p.dma("sp", lambda e, l=l: e.dma_start(out=lamP, in_=lamP_d[l]), "s0", reads=["lamP"], writes=["lamP"])
            p.dma("sp", lambda e, l=l: e.dma_start(out=lamC, in_=lamC_d[l]), "s1", reads=["lamC"], writes=["lamC"])
            p.dma("sp", lambda e, l=l: e.dma_start(out=bC, in_=bC_d[l]), "s2", reads=["bC"], writes=["bC"])
            p.dma("sp", lambda e, l=l: e.dma_start(out=cP, in_=cP_d[l]), "s3", reads=["cP"], writes=["cP"])
            p.mark('wload_issued')
            SP_ = [pa[:, i, :] for i in range(12)]
            ArP, AiP, _, _ = ssm_prep(SP_, lamP[:, 0, :], lamP[:, 1, :], lamP[:, 2, :], ["pa", "lamP"])
            p.op("dve", lambda e: e.tensor_copy(out=Apow[:, 0, :, 0], in_=ArP), reads=["pa", "Apow"], writes=["Apow"])
            p.op("dve", lambda e: e.tensor_copy(out=Apow[:, 1, :, 0], in_=AiP), reads=["pa", "Apow"], writes=["Apow"])
            for r in range(1, LC):
                cmul("dve", Apow[:, 0, :, r], Apow[:, 1, :, r], Apow[:, 0, :, r - 1], Apow[:, 1, :, r - 1], ArP, AiP, pa[:, 5, :], pa[:, 6, :], ["pa", "Apow"])
            for ri, Av in ((0, ArP), (1, AiP)):
                p.op("dve", lambda e, ri=ri, Av=Av: e.tensor_copy(out=Abc[:, ri, :, :], in_=Av.unsqueeze(2).to_broadcast([128, NP, NCH])), reads=["pa", "Abc"], writes=["Abc"])
            p.mark('Apow_done')
            for q in range(4):
                for m2 in range(2):
                    c0 = q * 32 + m2 * 16
                    p.op("dve", lambda e, q=q, m2=m2, c0=c0: e.tensor_scalar(out=cW[:, q:NP:4, 0, c0:c0 + 16], in0=cP[:, 0, q:NP:4, :], scalar1=mk[:, m2:m2 + 1], scalar2=None, op0=ALU.mult),
                         reads=["cP", "mk", "cW"], writes=["cW"])
                    p.op("dve", lambda e, q=q, m2=m2, c0=c0: e.tensor_scalar(out=cW[:, q:NP:4, 1, c0:c0 + 16], in0=cP[:, 1, q:NP:4, :], scalar1=mk[:, m2:m2 + 1], scalar2=-1.0, op0=ALU.mult, op1=ALU.mult),
                         reads=["cP", "mk", "cW"], writes=["cW"])
            p.mark('cW_done')
            SC_ = [pc[:, i, :, :] for i in range(12)]
            _, _, FrC, FiC = ssm_prep(SC_, lamC[:, 0, :, :], lamC[:, 1, :, :], lamC[:, 2, :, :], ["pc", "lamC"])
            cmul("dve", pc[:, 1, :, :], pc[:, 2, :, :], FrC, FiC, bC[:, 0, :, :], bC[:, 1, :, :], pc[:, 5, :, :], pc[:, 6, :, :], ["pc", "bC"])
            for q in range(4):
                for m2 in range(2):
                    for ri in range(2):
                        p.op("dve", lambda e, q=q, m2=m2, ri=ri: e.tensor_scalar(out=buW[:, q:NP:4, ri, m2 * 64:(m2 + 1) * 64], in0=pc[:, 1 + ri, :, :], scalar1=mk[:, 4 + q * 2 + m2:5 + q * 2 + m2], scalar2=None, op0=ALU.mult),
                             reads=["pc", "mk", "buW"], writes=["buW"])
            fence(["pc", "lamC", "bC"] + BUK)
            p.op("dve", lambda e: e.memset(kT, 0.0), reads=["kT"], writes=["kT"])
            p.mark('tables_done')

            def load_x(ti, from_input, bs=SA):
                t0 = ti * TT
                xT, P_ = bs.xT, bs.pre
                if from_input:
                    for blk in range(NB):
                        p.dma("sp", lambda e, blk=blk: e.dma_start(out=xtm, in_=x_d[t0 + blk * 128:t0 + (blk + 1) * 128, :]), "xtmld", reads=["xtm"], writes=["xtm"])
                        for k in range(8):
                            b = ps()
                            p.op("pe", lambda e, b=b, k=k: e.transpose(pss[b][:, 0:128], xtm[:, k * 128:(k + 1) * 128], ident[:]), reads=["xtm", "ident"], writes=[f"ps{b}"])
                            if k % 2:
                                p.op("dve", lambda e, b=b, blk=blk, k=k: e.tensor_copy(out=xT[:, k, blk * 128:(blk + 1) * 128], in_=pss[b][:, 0:128]), reads=[f"ps{b}"], writes=[f"{P_}xT{k}"])
                            else:
                                p.op("act", lambda e, b=b, blk=blk, k=k: e.activation(out=xT[:, k, blk * 128:(blk + 1) * 128], in_=pss[b][:, 0:128], func=AF.Identity), reads=[f"ps{b}"], writes=[f"{P_}xT{k}"])
                else:
                    p.dma("sp", lambda e: e.dma_start(out=xT[:], in_=xs_d[:, :, t0:t0 + TT]), "xld" + P_, reads=[f"xs{ti}"], writes=[P_ + k_ for k_ in XT])

            def store_x(ti, to_output, bs=SA):
                t0 = ti * TT
                xT, P_ = bs.xT, bs.pre
                if to_output:
                    for blk in range(NB):
                        for k in range(8):
                            b = ps()
                            p.op("pe", lambda e, b=b, blk=blk, k=k: e.transpose(pss[b][:, 0:128], xT[:, k, blk * 128:(blk + 1) * 128], ident[:]), reads=[f"{P_}xT{k}", "ident"], writes=[f"ps{b}"])
                            p.op("dve", lambda e, b=b, k=k: e.tensor_copy(out=xtm[:, k * 128:(k + 1) * 128], in_=pss[b][:, 0:128]), reads=[f"ps{b}", "xtm"], writes=["xtm"])
                        p.dma("sp", lambda e, blk=blk: e.dma_start(out=y_d[t0 + blk * 128:t0 + (blk + 1) * 128, :], in_=xtm), "yst", reads=["xtm"])
                else:
                    p.dma("sp", lambda e: e.dma_start(out=xs_d[:, :, t0:t0 + TT], in_=xT[:]), "xst" + P_, reads=[P_ + k_ for k_ in XT], writes=[f"xs{ti}"])

            def norm_in(o_gs, o_sh, bs=SA):
                xT, sq, tmpf, hb, rstd, P_ = bs.xT, bs.sq, bs.tmpf, bs.hb, bs.rstd, bs.pre
                for k in range(8):
                    p.op("act", lambda e, k=k: e.activation(out=sq[:, k, :], in_=xT[:, k, :], func=AF.Square), reads=[f"{P_}xT{k}"], writes=[f"{P_}sq{k}"])
                stats_rstd([f"{P_}sq{k}" for k in range(8)], [sq[:, k, :] for k in range(8)], 1.0 / D, rstd, P_ + "rstd")
                for k in range(8):
                    p.op("dve", lambda e, k=k: e.scalar_tensor_tensor(out=tmpf[:, k, :], in0=xT[:, k, :], scalar=coefs[:, o_gs, k:k + 1], in1=rstd, op0=ALU.mult, op1=ALU.mult),
                         reads=[f"{P_}xT{k}", "coefs", P_ + "rstd"], writes=[f"{P_}tmpf{k}"])
                    p.op("act", lambda e, k=k: e.activation(out=hb[:, k, :], in_=tmpf[:, k, :], func=AF.Identity, bias=coefs[:, o_sh, k:k + 1]), reads=[f"{P_}tmpf{k}", "coefs"], writes=[f"{P_}hb{k}"])

            def resid_out(o_coef, src_mm, bs=SA):
                xT, sq, tmpf, hb, rstd, P_ = bs.xT, bs.sq, bs.tmpf, bs.hb, bs.rstd, bs.pre
                for m in range(8):
                    b = ps()
                    src_mm(m, b)
                    p.op("dve", lambda e, b=b, m=m: e.tensor_copy(out=tmpf[:, m, :], in_=pss[b][:, 0:TT]), reads=[f"ps{b}"], writes=[f"{P_}tmpf{m}"])
                    p.op("act", lambda e, m=m: e.activation(out=sq[:, m, :], in_=tmpf[:, m, :], func=AF.Square), reads=[f"{P_}tmpf{m}"], writes=[f"{P_}sq{m}"])
                stats_rstd([f"{P_}sq{k}" for k in range(8)], [sq[:, k, :] for k in range(8)], 1.0 / D, rstd, P_ + "rstd")
                for m in range(8):
                    p.op("dve", lambda e, m=m: e.tensor_tensor(out=tmpf[:, m, :], in0=tmpf[:, m, :], in1=rstd, op=ALU.mult), reads=[f"{P_}tmpf{m}", P_ + "rstd"], writes=[f"{P_}tmpf{m}"])
                    p.op("dve", lambda e, m=m: e.scalar_tensor_tensor(out=xT[:, m, :], in0=tmpf[:, m, :], scalar=coefs[:, o_coef, m:m + 1], in1=xT[:, m, :], op0=ALU.mult, op1=ALU.add),
                         reads=[f"{P_}tmpf{m}", "coefs", f"{P_}xT{m}"], writes=[f"{P_}xT{m}"])

            for ti in range(NT):
                load_x(ti, first)
                p.mark('x_loaded')
                norm_in(0, 1)
                p.mark('norm_done')
                for m in range(6):
                    b = ps()
                    for k in range(8):
                        p.op("pe", lambda e, b=b, k=k, m=m: e.matmul(pss[b][:, 0:TT], lhsT=W_IN[:, k, m * 128:(m + 1) * 128], rhs=hb[:, k, :], start=(k == 0), stop=(k == 7)), reads=["wbuf", f"hb{k}"], writes=[f"ps{b}"])
                    if m < 4:
                        p.op("act", lambda e, b=b, m=m: e.activation(out=qT[:, m, :], in_=pss[b][:, 0:TT], func=AF.Identity, scale=0.125), reads=[f"ps{b}"], writes=[f"qT{m}"])
                    else:
                        p.op("dve", lambda e, b=b, m=m: e.tensor_copy(out=kT[0:64, m - 4, 0, 128:128 + TT], in_=pss[b][0:64, 0:TT]), reads=[f"ps{b}"], writes=["kT"])
                        p.op("dve", lambda e, b=b, m=m: e.tensor_copy(out=kT[64:128, m - 4, 1, 128:128 + TT], in_=pss[b][64:128, 0:TT]), reads=[f"ps{b}"], writes=["kT"])
                for s in range(4):
                    b = ps()
                    for k in range(8):
                        p.op("pe", lambda e, b=b, k=k, s=s: e.matmul(pss[b][:, 0:TT], lhsT=W_IN[:, k, 896 + s * 128:896 + (s + 1) * 128], rhs=hb[:, k, :], start=(k == 0), stop=(k == 7)), reads=["wbuf", f"hb{k}"], writes=[f"ps{b}"])
                    if s % 2:
                        p.op("act", lambda e, b=b, s=s: e.activation(out=uT[:, s, :], in_=pss[b][:, 0:TT], func=AF.Identity), reads=[f"ps{b}"], writes=[f"uT{s}"])
                    else:
                        p.op("dve", lambda e, b=b, s=s: e.tensor_copy(out=uT[:, s, :], in_=pss[b][:, 0:TT]), reads=[f"ps{b}"], writes=[f"uT{s}"])
                for blk in range(NB):
                    b = ps()
                    for k in range(8):
                        p.op("pe", lambda e, b=b, k=k, blk=blk: e.matmul(pss[b][:, 0:128], lhsT=hb[:, k, blk * 128:(blk + 1) * 128], rhs=W_IN[:, k, 768:896], start=(k == 0), stop=(k == 7)), reads=["wbuf", f"hb{k}"], writes=[f"ps{b}"])
                    for g2 in range(2):
                        p.op("dve", lambda e, b=b, blk=blk, g2=g2: e.tensor_copy(out=vpad[:, blk + 1, g2, 64:128], in_=pss[b][:, g2 * 64:(g2 + 1) * 64]), reads=[f"ps{b}"], writes=[f"vp{blk + 1}"])
                for pr in range(NP):
                    for ri in range(2):
                        b = ps()
                        p.op("pe", lambda e, b=b, pr=pr, ri=ri: e.matmul(pss[b][:, 0:TT], lhsT=buW[:, pr, ri, :], rhs=uT[:, pr // 4, :], start=True, stop=True), reads=["buW", f"uT{pr // 4}"], writes=[f"ps{b}"])
                        if ri:
                            p.op("act", lambda e, b=b, pr=pr, ri=ri: e.activation(out=bu[:, ri, pr, :], in_=pss[b][:, 0:TT], func=AF.Identity), reads=[f"ps{b}"], writes=[bukey(pr, ri)])
                        else:
                            p.op("dve", lambda e, b=b, pr=pr, ri=ri: e.tensor_copy(out=bu[:, ri, pr, :], in_=pss[b][:, 0:TT]), reads=[f"ps{b}"], writes=[bukey(pr, ri)])
                p.mark('bu_done')
                _i0 = len(p.ops)
                p.mark('inproj_done')
                for blk in range(NB):
                    gb = ti * NB + blk
                    kts = (1,) if gb == 0 else (0, 1)
                    for g2 in range(2):
                        for kt in kts:
                            b = ps()
                            for hh in range(4):
                                h = 4 * g2 + hh
                                qt, half = h // 2, h % 2
                                p.op("pe", lambda e, b=b, hh=hh, qt=qt, half=half, kt=kt, blk=blk, g2=g2: e.matmul(
                                    pss[b][:, hh * 128:(hh + 1) * 128], lhsT=kT[:, g2, half, (blk + kt) * 128:(blk + kt + 1) * 128],
                                    rhs=qT[:, qt, blk * 128:(blk + 1) * 128], start=True, stop=True), reads=["kT", f"qT{qt}"], writes=[f"ps{b}"])
                            p.op("dve", lambda e, b=b, g2=g2, kt=kt: e.tensor_tensor(out=sc_f.rearrange("p (h q) -> p h q", h=4), in0=pss[b][:, :].rearrange("p (h q) -> p h q", h=4),
                                 in1=maskb[:, 4 * g2:4 * g2 + 4, kt, :], op=ALU.add), reads=[f"ps{b}", "maskb"], writes=["sc_f"])
                            p.op("act", lambda e, kt=kt: e.activation(out=pexp[:, kt, :], in_=sc_f, func=AF.Exp), reads=["sc_f"], writes=[f"pexp{kt}"])
                        for j in range(2):
                            qt = 2 * g2 + j
                            bo, bd = ps(), ps()
                            n = len(kts) * 2
                            i = 0
                            for kt in kts:
                                for half in range(2):
                                    hh = 2 * j + half
                                    lo = 64 if half == 0 else 0
                                    p.op("pe", lambda e, bo=bo, kt=kt, hh=hh, lo=lo, i=i, n=n, blk=blk, g2=g2: e.matmul(pss[bo][:, 0:128], lhsT=vpad[:, blk + kt, g2, lo:lo + 128], rhs=pexp[:, kt, hh * 128:(hh + 1) * 128], start=(i == 0), stop=(i == n - 1)),
                                         reads=[f"vp{blk + kt}", f"pexp{kt}"], writes=[f"ps{bo}"])
                                    p.op("pe", lambda e, bd=bd, kt=kt, hh=hh, lo=lo, i=i, n=n: e.matmul(pss[bd][:, 0:128], lhsT=ones_pad[:, lo:lo + 128], rhs=pexp[:, kt, hh * 128:(hh + 1) * 128], start=(i == 0), stop=(i == n - 1)),
                                         reads=["ones_pad", f"pexp{kt}"], writes=[f"ps{bd}"])
                                    i += 1
                            p.op("dve", lambda e, bd=bd, qt=qt, l=l: e.tensor_scalar(out=rden, in0=pss[bd][:, 0:128], scalar1=esink[:, l, qt:qt + 1], scalar2=None, op0=ALU.add), reads=[f"ps{bd}", "esink"], writes=["rden"])
                            p.op("dve", lambda e: e.reciprocal(out=rden, in_=rden), reads=["rden"], writes=["rden"])
                            p.op("dve", lambda e, bo=bo, qt=qt, blk=blk: e.tensor_tensor(out=attn[:, qt, blk * 128:(blk + 1) * 128], in0=pss[bo][:, 0:128], in1=rden, op=ALU.mult), reads=[f"ps{bo}", "rden"], writes=[f"attn{qt}"])
                p.mark('attn_done')
                p.op("dve", lambda e: e.tensor_copy(out=kT[:, :, :, 0:128], in_=kT[:, :, :, TT:TT + 128]), reads=["kT"], writes=["kT"])
                p.op("dve", lambda e: e.tensor_copy(out=vpad[:, 0, :, 64:128], in_=vpad[:, NB, :, 64:128]), reads=[f"vp{NB}"], writes=["vp0"])
                for k in range(4):
                    p.op("act", lambda e, k=k: e.activation(out=sq[:, k, :], in_=attn[:, k, :], func=AF.Square), reads=[f"attn{k}"], writes=[f"sq{k}"])
                stats_rstd([f"sq{k}" for k in range(4)], [sq[:, k, :] for k in range(4)], 1.0 / 512)
                for k in range(4):
                    p.op("dve", lambda e, k=k, l=l: e.scalar_tensor_tensor(out=heads[:, k, :], in0=attn[:, k, :], scalar=vec[:, l, 32 + k:33 + k], in1=rstd[:], op0=ALU.mult, op1=ALU.mult), reads=[f"attn{k}", "vec", "rstd"], writes=[f"heads{k}"])
                p.mark('heads_done')
                _s1 = p.ops[_i0:]; del p.ops[_i0:]
                for (eng, pa_, pb_) in SCAN_SPLIT:
                    nb_ = pb_ - pa_
                    kb = f"bus{pa_}"
                    m1 = t1.rearrange("p a (i j) -> p i a j", i=4)[:, 0:2, pa_:pb_, :]
                    m2 = t2.rearrange("p a (i j) -> p i a j", i=4)[:, 0:2, pa_:pb_, :]
                    n1 = t1.rearrange("p a (i j) -> p i a j", i=4)[:, 2:4, pa_:pb_, :]
                    n2 = t2.rearrange("p a (i j) -> p i a j", i=4)[:, 2:4, pa_:pb_, :]
                    km = f"scr{pa_}"
                    zv = lambda r, pa_=pa_, pb_=pb_: bu[:, :, pa_:pb_, :].rearrange("p i a (j r) -> p i a j r", r=LC)[:, :, :, :, r]
                    AR2 = Abc[:, 0, pa_:pb_, :].unsqueeze(1).to_broadcast([128, 2, nb_, NCH])
                    AI2 = Abc[:, 1, pa_:pb_, :].unsqueeze(1).to_broadcast([128, 2, nb_, NCH])
                    for r in range(1, LC):
                        p.op(eng, lambda e, r=r, zv=zv, m1=m1, AR2=AR2: e.tensor_tensor(out=m1, in0=AR2, in1=zv(r - 1), op=ALU.mult), reads=["Abc", kb + "r", kb + "i"], writes=[km + "a"])
                        p.op(eng, lambda e, r=r, zv=zv, m2=m2, AI2=AI2: e.tensor_tensor(out=m2, in0=AI2, in1=zv(r - 1), op=ALU.mult), reads=["Abc", kb + "r", kb + "i"], writes=[km + "b"])
                        p.op(eng, lambda e, r=r, zv=zv, m1=m1: e.tensor_tensor(out=zv(r), in0=zv(r), in1=m1, op=ALU.add), reads=[km + "a", kb + "r", kb + "i"], writes=[kb + "r", kb + "i"])
                        p.op(eng, lambda e, r=r, zv=zv, m2=m2: e.tensor_tensor(out=zv(r)[:, 0], in0=zv(r)[:, 0], in1=m2[:, 1], op=ALU.subtract), reads=[km + "b", kb + "r"], writes=[kb + "r"])
                        p.op(eng, lambda e, r=r, zv=zv, m2=m2: e.tensor_tensor(out=zv(r)[:, 1], in0=zv(r)[:, 1], in1=m2[:, 0], op=ALU.add), reads=[km + "b", kb + "i"], writes=[kb + "i"])
                    PR2 = Apow[:, 0, pa_:pb_, :].unsqueeze(1).to_broadcast([128, 2, nb_, LC])
                    PI2 = Apow[:, 1, pa_:pb_, :].unsqueeze(1).to_broadcast([128, 2, nb_, LC])
                    Hb = Hc[:, :, pa_:pb_].unsqueeze(3).to_broadcast([128, 2, nb_, LC])
                    kh = f"Hc{pa_}"
                    for j in range(NCH):
                        zj = bu[:, :, pa_:pb_, j * LC:(j + 1) * LC]
                        p.op(eng, lambda e, n1=n1, PR2=PR2, Hb=Hb: e.tensor_tensor(out=n1, in0=PR2, in1=Hb, op=ALU.mult), reads=["Apow", kh], writes=[km + "c"])
                        p.op(eng, lambda e, n2=n2, PI2=PI2, Hb=Hb: e.tensor_tensor(out=n2, in0=PI2, in1=Hb, op=ALU.mult), reads=["Apow", kh], writes=[km + "d"])
                        p.op(eng, lambda e, zj=zj, n1=n1: e.tensor_tensor(out=zj, in0=zj, in1=n1, op=ALU.add), reads=[km + "c", kb + "r", kb + "i"], writes=[kb + "r", kb + "i"])
                        p.op(eng, lambda e, zj=zj, n2=n2: e.tensor_tensor(out=zj[:, 0], in0=zj[:, 0], in1=n2[:, 1], op=ALU.subtract), reads=[km + "d", kb + "r"], writes=[kb + "r"])
                        p.op(eng, lambda e, zj=zj, n2=n2: e.tensor_tensor(out=zj[:, 1], in0=zj[:, 1], in1=n2[:, 0], op=ALU.add), reads=[km + "d", kb + "i"], writes=[kb + "i"])
                        p.op(eng, lambda e, j=j, l=l, pa_=pa_, pb_=pb_: e.tensor_copy(out=Hc[:, :, pa_:pb_], in_=bu[:, :, pa_:pb_, j * LC + LC - 1]), reads=[kb + "r", kb + "i"], writes=[kh])
                _s2 = p.ops[_i0:]; del p.ops[_i0:]
                p.ops.extend(_interleave(_s1, _s2))
                p.mark('scan_done')
                for hf in range(2):
                    for ri in range(2):
                        p.op("act", lambda e, hf=hf, ri=ri: e.activation(out=hbf[:, ri], in_=bu[:, ri, hf * 8:(hf + 1) * 8, :], func=AF.Identity), reads=BUK, writes=["hbf"])
                    for s in (2 * hf, 2 * hf + 1):
                        b = ps()
                        i = 0
                        for q in range(4):
                            pr = 4 * s + q
                            for ri in range(2):
                                p.op("pe", lambda e, b=b, pr=pr, ri=ri, i=i, hf=hf: e.matmul(pss[b][:, 0:TT], lhsT=cW[:, pr, ri, :], rhs=hbf[:, ri, pr - 8 * hf, :], start=(i == 0), stop=(i == 7)), reads=["cW", "hbf"], writes=[f"ps{b}"])
                                i += 1
                        p.op("dve", lambda e, b=b, s=s, l=l: e.scalar_tensor_tensor(out=yss[:, s, :], in0=uT[:, s, :], scalar=vec[:, l, 44 + s:45 + s], in1=pss[b][:, 0:TT], op0=ALU.mult, op1=ALU.add), reads=[f"ps{b}", f"uT{s}", "vec"], writes=["yss"])
                p.mark('y_done')
                p.op("pool", lambda e: e.tensor_tensor(out=ys2, in0=yss, in1=yss, op=ALU.mult), reads=["yss", "ys2"], writes=["ys2"])
                p.op("pool", lambda e: e.tensor_scalar(out=ys2, in0=ys2, scalar1=0.044715, scalar2=1.0, op0=ALU.mult, op1=ALU.add), reads=["ys2"], writes=["ys2"])
                p.op("pool", lambda e: e.tensor_tensor(out=ys2, in0=ys2, in1=yss, op=ALU.mult), reads=["yss", "ys2"], writes=["ys2"])
                p.op("act", lambda e: e.activation(out=ys2, in_=ys2, func=AF.Sigmoid, scale=1.5957691216057308), reads=["ys2"], writes=["ys2"])
                p.op("dve", lambda e: e.tensor_tensor(out=yss, in0=yss, in1=ys2, op=ALU.mult), reads=["yss", "ys2"], writes=["yss"])
                p.op("act", lambda e: e.activation(out=zb, in_=yss, func=AF.Identity), reads=["yss", "zb"], writes=["zb"])
                for mo in range(4):
                    b = ps()
                    for k in range(4):
                        p.op("pe", lambda e, b=b, k=k, mo=mo: e.matmul(pss[b][:, 0:TT], lhsT=W_GLU[:, k, mo * 128:(mo + 1) * 128], rhs=zb[:, k, :], start=(k == 0), stop=(k == 3)), reads=["wbuf", "zb"], writes=[f"ps{b}"])
                    p.op("act", lambda e, b=b, mo=mo, l=l: e.activation(out=ys2[:, mo, :], in_=pss[b][:, 0:TT], func=AF.Sigmoid, bias=vec[:, l, 48 + mo:49 + mo]), reads=[f"ps{b}", "vec"], writes=["ys2"])
                p.op("dve", lambda e: e.tensor_tensor(out=ys2, in0=yss, in1=ys2, op=ALU.mult), reads=["yss", "ys2"], writes=["ys2"])
                p.op("act", lambda e: e.activation(out=zb, in_=ys2, func=AF.Square), reads=["ys2", "zb"], writes=["zb"])
                stats_rstd(["zb"] * 4, [zb[:, k, :] for k in range(4)], 1.0 / 512)
                for k in range(4):
                    p.op("dve", lambda e, k=k, l=l: e.scalar_tensor_tensor(out=sheads[:, k, :], in0=ys2[:, k, :], scalar=vec[:, l, 40 + k:41 + k], in1=rstd[:], op0=ALU.mult, op1=ALU.mult), reads=["ys2", "vec", "rstd"], writes=[f"sheads{k}"])

                p.mark('ssm_done')
                def mm_out(m, b):
                    for k in range(8):
                        src, key = (heads, f"heads{k}") if k < 4 else (sheads, f"sheads{k - 4}")
                        p.op("pe", lambda e, k=k, src=src: e.matmul(pss[b][:, 0:TT], lhsT=W_OUT[:, k, m * 128:(m + 1) * 128], rhs=src[:, k % 4, :], start=(k == 0), stop=(k == 7)), reads=["wbuf", key], writes=[f"ps{b}"])
                resid_out(2, mm_out)
                store_x(ti, False)

            p.mark('mixer_done')
            fence(TAILK)
            p.dma("pool", lambda e, l=l: e.dma_start(out=W_M1, in_=wm1_d[l].rearrange("(k p) n -> p k n", p=128)), "wld", reads=["wbuf"], writes=["wbuf"])
            p.dma("pool", lambda e, l=l: e.dma_start(out=W_M2, in_=wm2_d[l].rearrange("(k p) n -> p k n", p=128)), "wld", reads=["wbuf"], writes=["wbuf"])
            sets = [SA, SB]
            load_x(0, False, sets[0])
            norm_in(3, 4, sets[0])
            for ti in range(NT):
                bs = sets[ti % 2]
                for f in range(32):
                    b = ps()
                    for k in range(8):
                        p.op("pe", lambda e, b=b, k=k, f=f, bs=bs: e.matmul(pss[b][:, 0:TT], lhsT=W_M1[:, k, f * 128:(f + 1) * 128], rhs=bs.hb[:, k, :], start=(k == 0), stop=(k == 7)), reads=["wbuf", f"{bs.pre}hb{k}"], writes=[f"ps{b}"])
                    p.op("act", lambda e, b=b, f=f: e.activation(out=hid[:, f, :], in_=pss[b][:, 0:TT], func=AF.Relu), reads=[f"ps{b}"], writes=[f"hid{f}"])
                    p.op("pool" if f % 2 else "dve", lambda e, f=f: e.tensor_tensor(out=hid[:, f, :], in0=hid[:, f, :], in1=hid[:, f, :], op=ALU.mult), reads=[f"hid{f}"], writes=[f"hid{f}"])
                if ti + 1 < NT:
                    load_x(ti + 1, False, sets[(ti + 1) % 2])
                    norm_in(3, 4, sets[(ti + 1) % 2])

                def mm_mlp(m, b):
                    for f in range(32):
                        p.op("pe", lambda e, f=f: e.matmul(pss[b][:, 0:TT], lhsT=W_M2[:, f, m * 128:(m + 1) * 128], rhs=hid[:, f, :], start=(f == 0), stop=(f == 31)), reads=["wbuf", f"hid{f}"], writes=[f"ps{b}"])
                resid_out(5, mm_mlp, bs)
                store_x(ti, last, bs)
        print("n_ops", len(p.ops), p.marks, flush=True)
        import os
        if os.environ.get("KTRUNC"):
            p.ops = p.ops[:int(os.environ["KTRUNC"])]
        if os.environ.get("KDBG"):
            dbg_list = {"modcol": modcol[:], "coefs": coefs[:], "xT": xT[:], "hb": hb[:], "qT": qT, "kT": kT, "uT": uT, "vpad": vpad, "attn": attn,
                        "heads": heads, "bu": bu, "yss": yss, "ys2": ys2, "sheads": sheads, "Apow": Apow, "buW": buW, "cW": cW, "Hc": Hc, "rstd": rstd[:], "tmpf": tmpf[:], "pexp": pexp, "hid": hid}
            for nm in os.environ["KDBG"].split(","):
                ap_ = dbg_list[nm]
                shp = list(ap_.shape)
                dd = nc.dram_tensor("dbg_" + nm, shp, F32, kind="ExternalOutput").ap()
                p.dma("pool", lambda e, dd=dd, ap_=ap_: e.dma_start(out=dd, in_=ap_), "dbg_" + nm, reads=TAILK + XT + HIDK + ["modcol", "coefs", "buW", "cW", "rstd"] + [f"hb{k}" for k in range(8)] + [f"tmpf{k}" for k in range(8)] + [f"vp{i}" for i in range(NB + 1)])
        fin = ["yst"] if not os.environ.get("KTRUNC") else []
        if os.environ.get("KDBG"):
            fin += ["dbg_" + nm for nm in os.environ["KDBG"].split(",")]
        p.emit(final_dma_tags=fin)
    return nc


def _host_prep(inputs, b, T, DEPTH):
    f = lambda a: np.ascontiguousarray(a, dtype=np.float32)
    L = DEPTH
    col = lambda v, n: v.reshape(n, 128).T
    m = {}
    m["x"] = f(inputs["x"][b, :T])
    m["ccol"] = f(col(inputs["c"][b], 8))
    m["w_ada"] = f(inputs["w_ada"][:L])
    m["b_ada"] = f(inputs["b_ada"][:L].reshape(L, 1, 6 * D))
    vecs = np.zeros((L, 128, NV), np.float32)
    for l in range(L):
        vecs[l, :, 0:8] = col(inputs["pre_mix_g"][l], 8)
        vecs[l, :, 8:16] = col(inputs["post_mix_g"][l], 8)
        vecs[l, :, 16:24] = col(inputs["pre_mlp_g"][l], 8)
        vecs[l, :, 24:32] = col(inputs["post_mlp_g"][l], 8)
        vecs[l, :, 32:36] = col(inputs["attn_out_g"][l], 4)
        vecs[l, :, 36:40] = np.repeat(inputs["attn_sinks"][l].reshape(4, 2), 64, axis=1).T
        vecs[l, :, 40:44] = col(inputs["ssm_out_g"][l], 4)
        vecs[l, :, 44:48] = col(inputs["d_skip"][l], 4)
        vecs[l, :, 48:52] = col(inputs["b_glu"][l], 4)
    m["vecs"] = vecs
    w = inputs["w_in"][:L]
    m["w_in"] = f(np.concatenate([w[:, :, 0:512], w[:, :, 512:576], w[:, :, 512:576], w[:, :, 576:640], w[:, :, 576:640], w[:, :, 640:768], w[:, :, 768:1280]], axis=2))
    for k in ("w_glu", "w_out", "w_mlp_in", "w_mlp_out"):
        m[k] = f(inputs[k][:L])
    lam = np.stack([inputs["lam_re"][:L], inputs["lam_im"][:L], np.broadcast_to(inputs["log_dt"][:L][:, :, None], (L, G, P))], axis=1)
    lam5 = lam.reshape(L, 3, NP, 2, P)
    m["lamP"] = f(lam5.transpose(0, 3, 4, 1, 2).reshape(L, 128, 3, NP))
    lam6 = lam.reshape(L, 3, 4, 4, 2, P)
    lamC = np.broadcast_to(lam6[:, :, :, :, :, None, :], (L, 3, 4, 4, 2, C, P))
    m["lamC"] = f(lamC.transpose(0, 3, 4, 5, 1, 2, 6).reshape(L, 128, 3, 4, P))
    bb = np.stack([inputs["b_re"][:L], inputs["b_im"][:L]], axis=1).reshape(L, 2, 4, 4, 2, P, C)
    m["bC"] = f(bb.transpose(0, 3, 4, 6, 1, 2, 5).reshape(L, 128, 2, 4, P))
    cc = np.stack([inputs["c_re"][:L], inputs["c_im"][:L]], axis=1).reshape(L, 2, NP, 2, C, P)
    m["cP"] = f(cc.transpose(0, 3, 5, 1, 2, 4).reshape(L, 128, 2, NP, C))
    slopes = 2.0 ** (-8.0 * np.arange(1, NH + 1) / NH)
    s_ = np.arange(128)[:, None]; q_ = np.arange(128)[None, :]
    mb = np.full((128, NH, 2, 128), -30000.0, np.float32)
    for h in range(NH):
        d0 = 128 + q_ - s_
        d1 = q_ - s_
        mb[:, h, 0, :] = np.where((d0 >= 0) & (d0 < 128), -slopes[h] * d0, -30000.0)
        mb[:, h, 1, :] = np.where((d1 >= 0) & (d1 < 128), -slopes[h] * d1, -30000.0)
    m["maskb"] = mb
    m["ident"] = np.eye(128, dtype=np.float32)
    mk = np.zeros((128, 12), np.float32); mk[:64, 0] = 1; mk[64:, 1] = 1; mk[:, 3] = 1e-6
    rows = np.arange(128)
    for q in range(4):
        for mm in range(2):
            mk[:, 4 + q * 2 + mm] = ((rows // 32 == q) & ((rows % 32) // 16 == mm)).astype(np.float32)
    m["masks"] = mk
    return m


def kernel(**inputs):
    inputs = {k: np.asarray(v) for k, v in inputs.items()}
    B, T, _ = inputs["x"].shape
    DEPTH = inputs["w_in"].shape[0]
    nc = build(T, DEPTH)
    in_maps = [_host_prep(inputs, b, T, DEPTH) for b in range(B)]
    res = run_bass_kernel_spmd(nc, in_maps, core_ids=list(range(B)))
    return np.stack([r["y"] for r in res.results], axis=0).astype(np.float32)
```
